# Optimizing a Trainium2 kernel written in Bass

```python
import jax, jax.numpy as jnp
from jax import lax
import numpy as np

D_MODEL = 2048
BATCH = 2
SEQ = 4096
DEPTH = 4

GRID_W = 64
CTX_LEN = 256

N_HEADS = 4
D_KEY = D_MODEL // 2
D_VAL = D_MODEL
HEAD_K = D_KEY // N_HEADS
HEAD_V = D_VAL // N_HEADS
GATE_RANK = 16
GATE_NORM = 16.0
CHUNK = 64

POOL_WINDOWS = (2, 4, 8, 16)
D_POOL = D_MODEL // 2
POOL_GROUP = D_POOL // len(POOL_WINDOWS)

D_FF = ((8 * D_MODEL + 3 * 256 - 1) // (3 * 256)) * 256

N_BRANCH = 2
N_MOD = 6
DEEPNORM_ALPHA = (2.0 * DEPTH) ** 0.25
DEEPNORM_BETA = (8.0 * DEPTH) ** -0.25
LN_EPS = 1e-5
RMS_EPS = 1e-6

PROJ_WIDTHS = (D_KEY, D_KEY, D_VAL, D_VAL, 2 * GATE_RANK, D_POOL, N_BRANCH * D_MODEL)
D_PROJ = sum(PROJ_WIDTHS)
SPLIT_POINTS = tuple(int(s) for s in np.cumsum(PROJ_WIDTHS)[:-1])

kernel_name = "hybrid_gla_pool_diffusion_trunk"


def _layer_norm(x, gain, bias):
    xf = x.astype(jnp.float32)
    mu = jnp.mean(xf, axis=-1, keepdims=True)
    var = jnp.mean(jnp.square(xf - mu), axis=-1, keepdims=True)
    y = (xf - mu) * lax.rsqrt(var + LN_EPS)
    return (y * gain + bias).astype(x.dtype)


def _modulation(cond, w_ada, b_ada):
    return jnp.split(jax.nn.silu(cond) @ w_ada + b_ada, N_MOD, axis=-1)


def _heads(t, head_dim):
    b, l, _ = t.shape
    return t.reshape(b, l, -1, head_dim).transpose(0, 2, 1, 3)


def _split_proj(proj, w_decay_up, b_decay_up):
    q, k, v, g, a_lr, p, bg = jnp.split(proj, SPLIT_POINTS, axis=-1)
    q = _heads(q * HEAD_K ** -0.5, HEAD_K)
    k = _heads(k, HEAD_K)
    v = _heads(v, HEAD_V)
    log_decay = []
    for d in range(2):
        z = a_lr[..., d * GATE_RANK:(d + 1) * GATE_RANK] @ w_decay_up[d] + b_decay_up[d]
        log_decay.append(_heads(jax.nn.log_sigmoid(z.astype(jnp.float32)) / GATE_NORM, HEAD_K))
    return q, k, v, log_decay[0], log_decay[1], g, p, bg


def _gla_chunked(q, k, v, log_a, s0):
    b_, h_, l_, _ = q.shape
    n = l_ // CHUNK

    def chunks(t):
        return t.reshape(b_, h_, n, CHUNK, t.shape[-1]).astype(jnp.float32)

    q, k, v, log_a = chunks(q), chunks(k), chunks(v), chunks(log_a)
    cum = jnp.cumsum(log_a, axis=3)
    ref = cum[:, :, :, CHUNK // 2 - 1:CHUNK // 2]
    q_in = q * jnp.exp(cum - ref)
    k_in = k * jnp.exp(ref - cum)
    scores = jnp.einsum('bhnid,bhnjd->bhnij', q_in, k_in)
    mask = jnp.tril(jnp.ones((CHUNK, CHUNK), dtype=bool))
    scores = jnp.where(mask, scores, 0.0)
    o_intra = jnp.einsum('bhnij,bhnjv->bhniv', scores, v)
    last = cum[:, :, :, -1:]
    q_inter = q * jnp.exp(cum)
    k_state = k * jnp.exp(last - cum)
    decay_chunk = jnp.exp(last[:, :, :, 0, :])
    xs = (jnp.moveaxis(q_inter, 2, 0), jnp.moveaxis(k_state, 2, 0),
          jnp.moveaxis(v, 2, 0), jnp.moveaxis(decay_chunk, 2, 0))

    def step(state, inp):
        qc, kc, vc, dc = inp
        o = jnp.einsum('bhid,bhdv->bhiv', qc, state)
        state = dc[..., None] * state + jnp.einsum('bhjd,bhjv->bhdv', kc, vc)
        return state, o

    s_final, o_inter = lax.scan(step, s0.astype(jnp.float32), xs)
    o = o_intra + jnp.moveaxis(o_inter, 0, 2)
    return o.reshape(b_, h_, l_, HEAD_V), s_final


def _gla_two_way(q, k, v, la_fwd, la_bwd, s0_fwd, s0_bwd):
    flip = lambda t: jnp.flip(t, axis=2)
    o_f, s_f = _gla_chunked(q, k, v, la_fwd, s0_fwd)
    o_b, s_b = _gla_chunked(flip(q), flip(k), flip(v), flip(la_bwd), s0_bwd)
    return o_f + flip(o_b), s_f, s_b


def _box_mean(t, axis, window):
    n = t.shape[axis]
    lo = window // 2
    hi = window - lo - 1
    cs = jnp.cumsum(t.astype(jnp.float32), axis=axis)
    zero = jnp.zeros_like(lax.slice_in_dim(cs, 0, 1, axis=axis))
    cs = jnp.concatenate([zero, cs], axis=axis)
    idx = jnp.arange(n)
    upper = jnp.minimum(idx + hi + 1, n)
    lower = jnp.maximum(idx - lo, 0)
    total = jnp.take(cs, upper, axis=axis) - jnp.take(cs, lower, axis=axis)
    shape = [1] * t.ndim
    shape[axis] = n
    count = (upper - lower).astype(jnp.float32).reshape(shape)
    return (total / count).astype(t.dtype)


def _pool_branch(p, rows, w_pool_group, pool_scale, w_pool_out):
    outs = []
    for i, w in enumerate(POOL_WINDOWS):
        pg = p[..., i * POOL_GROUP:(i + 1) * POOL_GROUP]
        if rows is None:
            mean = _box_mean(pg, 1, w)
        else:
            b, l, ch = pg.shape
            img = pg.reshape(b, rows, GRID_W, ch)
            mean = _box_mean(_box_mean(img, 2, w), 1, w).reshape(b, l, ch)
        outs.append((mean - pg) @ w_pool_group[i])
    return (jnp.concatenate(outs, axis=-1) * pool_scale) @ w_pool_out


def _gla_output(o, g, gain, w_gla_out):
    of = o.transpose(0, 2, 1, 3)
    of = of * lax.rsqrt(jnp.mean(of * of, axis=-1, keepdims=True) + RMS_EPS)
    b, l = of.shape[:2]
    of = of.reshape(b, l, D_VAL).astype(g.dtype) * gain
    return (of * jax.nn.silu(g)) @ w_gla_out


def _merge(o, g, p, bg, rows, gla_norm_gain, w_gla_out, w_pool_group, pool_scale, w_pool_out, w_out):
    y_gla = _gla_output(o, g, gla_norm_gain, w_gla_out)
    y_pool = _pool_branch(p, rows, w_pool_group, pool_scale, w_pool_out)
    gate_gla, gate_pool = jnp.split(jax.nn.sigmoid(bg), N_BRANCH, axis=-1)
    return (gate_gla * y_gla + gate_pool * y_pool) @ w_out


def _token_mixer(h_lat, h_ctx, rows, ctx_out, w_in, w_decay_up, b_decay_up, gla_norm_gain,
                 w_pool_group, pool_scale, w_gla_out, w_pool_out, w_out):
    q_c, k_c, v_c, laf_c, lab_c, g_c, p_c, bg_c = _split_proj(h_ctx @ w_in, w_decay_up, b_decay_up)
    q_l, k_l, v_l, laf_l, lab_l, g_l, p_l, bg_l = _split_proj(h_lat @ w_in, w_decay_up, b_decay_up)
    s0 = jnp.zeros((h_lat.shape[0], N_HEADS, HEAD_K, HEAD_V), jnp.float32)
    o_c, s_fwd, s_bwd = _gla_two_way(q_c, k_c, v_c, laf_c, lab_c, s0, s0)
    o_l, _, _ = _gla_two_way(q_l, k_l, v_l, laf_l, lab_l, s_fwd, s_bwd)
    y_lat = _merge(o_l, g_l, p_l, bg_l, rows, gla_norm_gain, w_gla_out, w_pool_group,
                   pool_scale, w_pool_out, w_out)
    y_ctx = None
    if ctx_out:
        y_ctx = _merge(o_c, g_c, p_c, bg_c, None, gla_norm_gain, w_gla_out, w_pool_group,
                       pool_scale, w_pool_out, w_out)
    return y_lat, y_ctx


def _swiglu(h, w_ffn_in, w_ffn_out):
    gate, up = jnp.split(h @ w_ffn_in, 2, axis=-1)
    return (jax.nn.silu(gate) * up) @ w_ffn_out


def setup_inputs(seed: int = 0) -> dict:
    key = jax.random.key(seed)
    ks = jax.random.split(key, 24)
    f32 = jnp.float32
    nrm = lambda k, shape, s: jax.random.normal(k, shape, f32) * s
    L = DEPTH
    return {
        "x": nrm(ks[0], (BATCH, SEQ, D_MODEL), 1.0),
        "c": nrm(ks[1], (BATCH, D_MODEL), 1.0),
        "ctx": nrm(ks[2], (BATCH, CTX_LEN, D_MODEL), 1.0),
        "c_ctx": nrm(ks[3], (D_MODEL,), 1.0),
        "w_ada": nrm(ks[4], (L, D_MODEL, N_MOD * D_MODEL), 0.5 * D_MODEL ** -0.5),
        "b_ada": nrm(ks[5], (L, N_MOD * D_MODEL), 0.02),
        "w_in": nrm(ks[6], (L, D_MODEL, D_PROJ), D_MODEL ** -0.5),
        "w_decay_up": nrm(ks[7], (L, 2, GATE_RANK, D_KEY), GATE_RANK ** -0.5),
        "b_decay_up": nrm(ks[8], (L, 2, D_KEY), 0.1),
        "gla_norm_gain": 1.0 + nrm(ks[9], (L, D_VAL), 0.1),
        "w_pool_group": nrm(ks[10], (L, len(POOL_WINDOWS), POOL_GROUP, POOL_GROUP), POOL_GROUP ** -0.5),
        "pool_scale": 1.0 + nrm(ks[11], (L, D_POOL), 0.1),
        "w_gla_out": nrm(ks[12], (L, D_VAL, D_MODEL), D_VAL ** -0.5),
        "w_pool_out": nrm(ks[13], (L, D_POOL, D_MODEL), D_POOL ** -0.5),
        "w_out": nrm(ks[14], (L, D_MODEL, D_MODEL), DEEPNORM_BETA * D_MODEL ** -0.5),
        "ln_mix_gain": 1.0 + nrm(ks[15], (L, D_MODEL), 0.1),
        "ln_mix_bias": nrm(ks[16], (L, D_MODEL), 0.02),
        "w_ffn_in": nrm(ks[17], (L, D_MODEL, 2 * D_FF), D_MODEL ** -0.5),
        "w_ffn_out": nrm(ks[18], (L, D_FF, D_MODEL), DEEPNORM_BETA * D_FF ** -0.5),
        "ln_ffn_gain": 1.0 + nrm(ks[19], (L, D_MODEL), 0.1),
        "ln_ffn_bias": nrm(ks[20], (L, D_MODEL), 0.02),
    }


def reference(x, c, ctx, c_ctx, w_ada, b_ada, w_in, w_decay_up, b_decay_up, gla_norm_gain,
              w_pool_group, pool_scale, w_gla_out, w_pool_out, w_out, ln_mix_gain, ln_mix_bias,
              w_ffn_in, w_ffn_out, ln_ffn_gain, ln_ffn_bias):
    rows = x.shape[1] // GRID_W
    for layer in range(DEPTH):
        ctx_out = layer < DEPTH - 1
        sh_m, sc_m, gt_m, sh_f, sc_f, gt_f = _modulation(c[:, None, :], w_ada[layer], b_ada[layer])
        csh_m, csc_m, cgt_m, csh_f, csc_f, cgt_f = _modulation(c_ctx, w_ada[layer], b_ada[layer])
        h_lat = x * (1.0 + sc_m) + sh_m
        h_ctx = ctx * (1.0 + csc_m) + csh_m
        mix_lat, mix_ctx = _token_mixer(h_lat, h_ctx, rows, ctx_out, w_in[layer], w_decay_up[layer],
                                        b_decay_up[layer], gla_norm_gain[layer], w_pool_group[layer],
                                        pool_scale[layer], w_gla_out[layer], w_pool_out[layer], w_out[layer])
        x = _layer_norm(DEEPNORM_ALPHA * x + gt_m * mix_lat, ln_mix_gain[layer], ln_mix_bias[layer])
        ffn_lat = _swiglu(x * (1.0 + sc_f) + sh_f, w_ffn_in[layer], w_ffn_out[layer])
        x = _layer_norm(DEEPNORM_ALPHA * x + gt_f * ffn_lat, ln_ffn_gain[layer], ln_ffn_bias[layer])
        if ctx_out:
            ctx = _layer_norm(DEEPNORM_ALPHA * ctx + cgt_m * mix_ctx, ln_mix_gain[layer], ln_mix_bias[layer])
            ffn_ctx = _swiglu(ctx * (1.0 + csc_f) + csh_f, w_ffn_in[layer], w_ffn_out[layer])
            ctx = _layer_norm(DEEPNORM_ALPHA * ctx + cgt_f * ffn_ctx, ln_ffn_gain[layer], ln_ffn_bias[layer])
    return x
```

```python
import math
from contextlib import ExitStack

import numpy as np
import concourse.bass as bass
import concourse.mybir as mybir
from concourse.bass_utils import run_bass_kernel_spmd

F32 = mybir.dt.float32
BF16 = mybir.dt.bfloat16
AF = mybir.ActivationFunctionType
ALU = mybir.AluOpType

D = 2048
SEQ = 4096
CTX = 256
T = SEQ + CTX
DEPTH = 4
DK = 1024
DV = 2048
DFF = 5632
DPOOL = 1024
WINS = (2, 4, 8, 16)
ALPHA = (2.0 * DEPTH) ** 0.25
LN_EPS = 1e-5
RMS_EPS = 1e-6
NDS = 8
NCORES = 2
import os
STOP = os.environ.get("KSTOP", "")

SUBS = [(0, 256)] + [(256 + 512 * i, 512) for i in range(8)]
LV_BADA, LV_GAIN, LV_PSC, LV_LMG, LV_LMB, LV_LFG, LV_LFB, LV_BDEC, LV_N = 0, 96, 112, 120, 136, 152, 168, 184, 200
NB_IN = 73


class Prog:
    ENG = ("pe", "act", "dve", "pool", "sp")

    def __init__(self, nc):
        self.nc = nc
        self.ops = {e: [] for e in self.ENG}
        self.sem = {e: nc.alloc_semaphore("s_" + e) for e in self.ENG}
        self.semobj = dict(self.sem)
        self.cnt = {e: 0 for e in self.ENG}
        self.known = {e: {} for e in self.ENG}
        self.res = {}
        self.dq = {}
        for q in ("sp", "pool"):
            for i in range(NDS):
                self.semobj[(q, i)] = nc.alloc_semaphore(f"d_{q}{i}")
            self.dq[q] = dict(i=0, val=[0] * NDS)

    def _deps(self, reads, writes):
        d = {}
        for r in reads:
            for k, v in self.res.get(r, ({}, {}))[0].items():
                d[k] = max(d.get(k, 0), v)
        for w in writes:
            rw = self.res.get(w, ({}, {}))
            for dd in rw:
                for k, v in dd.items():
                    d[k] = max(d.get(k, 0), v)
        return d

    def _wait(self, e, d):
        for k, v in d.items():
            if e == "pe" and k == "pe":
                continue
            if self.known[e].get(k, 0) >= v:
                continue
            self.known[e][k] = v
            self.ops[e].append(("w", k, v))

    def _mark(self, reads, writes, k, v):
        for r in reads:
            dd = self.res.setdefault(r, ({}, {}))[1]
            dd[k] = max(dd.get(k, 0), v)
        for w in writes:
            dd = self.res.setdefault(w, ({}, {}))[0]
            dd[k] = max(dd.get(k, 0), v)

    def op(self, e, meth, *args, reads=(), writes=(), inc=True, **kw):
        self._wait(e, self._deps(reads, writes))
        if inc:
            self.cnt[e] += 1
            seq = self.cnt[e]
        else:
            seq = self.cnt[e] + 1
        self.ops[e].append(("i", (meth, args, kw), inc))
        self._mark(reads, writes, e, seq)

    def dma(self, q, out, in_, reads=(), writes=(), **kw):
        st = self.dq[q]
        i = st["i"]
        st["i"] = (i + 1) % NDS
        key = (q, i)
        d = self._deps(reads, writes)
        if st["val"][i] > 0:
            d[key] = max(d.get(key, 0), st["val"][i])
        self._wait(q, d)
        st["val"][i] += 16
        self.ops[q].append(("d", out, in_, key, kw))
        self._mark(reads, writes, key, st["val"][i])

    def barrier(self):
        d = {e: self.cnt[e] for e in ("pe", "act", "dve", "pool") if self.cnt[e] > 0}
        for q, st in self.dq.items():
            for i, v in enumerate(st["val"]):
                if v > 0:
                    d[(q, i)] = v
        for e in self.ENG:
            self._wait(e, dict(d))

    def build(self):
        nc = self.nc
        with nc.Block() as block:
            decos = dict(sp=block.sync, pool=block.gpsimd, act=block.scalar, dve=block.vector, pe=block.tensor)
            for e in self.ENG:
                def body(eng, e=e):
                    for o in self.ops[e]:
                        if o[0] == "w":
                            eng.wait_ge(self.semobj[o[1]], o[2])
                        elif o[0] == "i":
                            meth, a, kw = o[1]
                            ins = getattr(eng, meth)(*a, **kw)
                            if o[2]:
                                ins.then_inc(self.sem[e], 1)
                        else:
                            eng.dma_start(out=o[1], in_=o[2], **o[4]).then_inc(self.semobj[o[3]], 16)
                decos[e](body)


class K:
    def __init__(self, nc, depth, dbg):
        self.nc = nc
        self.P = Prog(nc)
        self.depth = depth
        self.dbg = dbg
        self.alt = 0

    def din(self, name, shape, dt=F32):
        return self.nc.dram_tensor(name, list(shape), dt, kind="ExternalInput").ap()

    def dscr(self, name, shape, dt=F32):
        kind = "ExternalOutput" if name in self.dbg else "Internal"
        return self.nc.dram_tensor(name, list(shape), dt, kind=kind).ap()


def build_program(depth=DEPTH, dbg=()):
    nc = bass.Bass("TRN2", target_bir_lowering=False)
    k = K(nc, depth, dbg)
    P = k.P
    L = depth
    xT0 = k.din("xT0", [D, T])
    cT = k.din("cT", [128, 32])
    ident_d = k.din("ident", [128, 128])
    masks_d = k.din("masks", [64, 128])
    rmask_d = k.din("rmask", [128, 512])
    invc_d = k.din("invc", [128, 1536])
    wada = k.din("wada", [L * 96, 128, 2048])
    lvec = k.din("lvec", [L, 128, LV_N])
    wdu = k.din("wdu", [L, 16, 2048])
    win_fm = k.din("win_fm", [L * NB_IN, 128, 2048])
    win_v = k.din("win_v", [L * 4, 128, 16 * 512])
    wpg = k.din("wpg", [L * 8, 128, 256])
    wgo = k.din("wgo", [L * 16, 128, 2048])
    wpo = k.din("wpo", [L * 16, 128, 1024])
    wo = k.din("wo", [L * 16, 128, 2048])
    wfi = k.din("wfi", [L * 44, 128, 16 * 256])
    wfo = k.din("wfo", [L * 16, 128, 44 * 128])
    yT = nc.dram_tensor("yT", [D, SEQ], F32, kind="ExternalOutput").ap()

    xT = k.dscr("xT", [D, T])
    uT = k.dscr("uT", [D, T])
    hT = k.dscr("hT", [D, T], BF16)
    qT = k.dscr("qT", [DK, T])
    kT = k.dscr("kT", [DK, T])
    gT = k.dscr("gT", [DV, T])
    alrT = k.dscr("alrT", [128, T], BF16)
    pT = k.dscr("pT", [DPOOL, T])
    bgT = k.dscr("bgT", [2 * D, T])
    vtm = k.dscr("vtm", [T, DV], BF16)
    oT = k.dscr("oT", [DV, T])
    ogT = k.dscr("ogT", [DV, T], BF16)
    dT = k.dscr("dT", [DPOOL, T], BF16)
    eT = k.dscr("eT", [DPOOL, T], BF16)
    t1T = k.dscr("t1T", [D, T])
    mT = k.dscr("mT", [D, T], BF16)
    hidT = k.dscr("hidT", [DFF, T], BF16)

    sb = nc.alloc_sbuf_tensor
    ones_bf = sb("ones_bf", [128, 128], BF16)
    ones_f = sb("ones_f", [128, 128], F32)
    ident = sb("ident_sb", [128, 128], BF16)
    masks = sb("masks_sb", [64, 128], F32)
    rmask = sb("rmask_sb", [128, 512], F32)
    invc = sb("invc_sb", [128, 1536], F32)
    lv = sb("lv_sb", [128, LV_N], F32)
    negb = sb("negb_sb", [128, 16], F32)
    mods = sb("mods_sb", [128, 96, 2], F32)
    ops1 = sb("ops1_sb", [128, 96, 2], F32)
    scin = sb("scin_sb", [128, 16, 2], BF16)
    ctmp = sb("ctmp_sb", [128, 32], F32)
    wdu_sb = sb("wdu_sb", [16, 2048], BF16)
    epsr = sb("epsr_sb", [128, 1], F32)
    epsl = sb("epsl_sb", [128, 1], F32)
    ln16 = sb("ln16_sb", [128, 1], F32)
    psf = [nc.alloc_psum_tensor(f"psf{i}", [128, 512], F32) for i in range(7)]
    psb = nc.alloc_psum_tensor("psb", [128, 1024], BF16)
    PSF = [f"psf{i}" for i in range(7)]

    uid = [0]

    def sbt(name, shape, dt):
        uid[0] += 1
        return nc.sbuf_tensor(f"{name}_u{uid[0]}", shape, dt)

    def alt_eng():
        k.alt ^= 1
        return "act" if k.alt else "dve"

    P.op("dve", "memset", ones_bf[:], 1.0, writes=["ones_bf"])
    P.op("dve", "memset", ones_f[:], 1.0, writes=["ones_f"])
    P.op("dve", "memset", epsr[:], RMS_EPS, writes=["epsr"])
    P.op("dve", "memset", epsl[:], LN_EPS, writes=["epsl"])
    P.op("dve", "memset", ln16[:], math.log(1.0 / 16.0), writes=["ln16"])
    P.dma("pool", ident[:], ident_d, writes=["ident"])
    P.dma("sp", masks[:], masks_d, writes=["masks"])
    P.dma("sp", rmask[:], rmask_d, writes=["rmask"])
    P.dma("sp", invc[:], invc_d, writes=["invc"])
    P.dma("sp", ctmp[:], cT, writes=["ctmp"])
    P.op("act", "activation", out=scin[:].rearrange("p a b -> p (a b)"), in_=ctmp[:], func=AF.Silu,
         reads=["ctmp"], writes=["scin"])

    def copy_dram(dst, src, rows, dt_bytes=4):
        for r0 in range(0, rows, 128):
            P.dma("sp", dst[r0:r0 + 128, :], src[r0:r0 + 128, :], reads=["D:" + src.name], writes=["D:" + dst.name])

    def tgroups(maxtok):
        gs, cur, tot = [], [], 0
        for s in SUBS:
            if cur and tot + s[1] > maxtok:
                gs.append(cur)
                cur, tot = [], 0
            cur.append(s)
            tot += s[1]
        gs.append(cur)
        return gs

    def linear(name, xin, xname, KC, wblk, wrow0, NB, nw, maxtok, pre, epi, kcs=None):
        nkc = KC if kcs is None else len(kcs(0))
        with ExitStack() as es:
            xs = es.enter_context(sbt(f"{name}_x", [128, KC, maxtok], BF16))
            wbs = [es.enter_context(sbt(f"{name}_w{i}", [128, nkc, nw * 128], BF16)) for i in range(3)]
            xv = xin.rearrange("(kc p) t -> p kc t", p=128)
            bank = 0
            widx = 0
            for grp in tgroups(maxtok):
                g0 = grp[0][0]
                gw = sum(s[1] for s in grp)
                step = max(1, KC // 4)
                for c0 in range(0, KC, step):
                    P.dma("sp", xs[:, c0:c0 + step, 0:gw], xv[:, c0:c0 + step, g0:g0 + gw],
                          reads=["D:" + xname], writes=[f"{name}_x"])
                items = [(nb_, t0_, tw_) for nb_ in range(NB) for (t0_, tw_) in grp]
                PF = 3
                pre_i = 0
                it_i = 0
                if pre is not None:
                    while pre_i < min(PF, len(items)):
                        pre(*items[pre_i])
                        pre_i += 1
                for nb in range(NB):
                    wb = wbs[widx % 3]
                    wn = f"{name}_w{widx % 3}"
                    widx += 1
                    per = nkc * nw * 128
                    wsrc = wblk[wrow0 + nb]
                    cstep = 2048
                    wflat = wb[:].rearrange("p a b -> p (a b)")
                    for c0 in range(0, per, cstep):
                        c1 = min(per, c0 + cstep)
                        P.dma("pool", wflat[:, c0:c1], wsrc[:, c0:c1], writes=[wn])
                    kl = list(range(KC)) if kcs is None else kcs(nb)
                    for (t0, tw) in grp:
                        if pre is not None and pre_i < len(items):
                            pre(*items[pre_i])
                            pre_i += 1
                        it_i += 1
                        pss = []
                        for j in range(nw):
                            b = bank % 6
                            bank += 1
                            pss.append(b)
                            for i, kc in enumerate(kl):
                                last = i == len(kl) - 1
                                P.op("pe", "matmul",
                                    psf[b][:, 0:tw], wb[:, i, j * 128:(j + 1) * 128], xs[:, kc, t0 - g0:t0 - g0 + tw],
                                    start=(i == 0), stop=last,
                                    reads=[wn, f"{name}_x"], writes=[PSF[b]], inc=last)
                        epi(nb, t0, tw, pss)
        P.barrier()

    class Ring:
        def __init__(self, es, name, n, shape, dt):
            self.t = [es.enter_context(sbt(f"{name}{i}", shape, dt)) for i in range(n)]
            self.names = [f"{name}{i}" for i in range(n)]
            self.i = 0

        def next(self):
            j = self.i % len(self.t)
            self.i += 1
            return self.t[j], self.names[j]

    def col_of(t0):
        return 1 if t0 < CTX else 0

    for l in range(L):
        P.dma("sp", lv[:], lvec[l], writes=["lv"])
        P.dma("pool", wdu_sb[:], wdu[l], writes=["wdu"])
        P.op("dve", "tensor_scalar", negb[:], lv[:, LV_BDEC:LV_BDEC + 16], -1.0, None, ALU.mult,
             reads=["lv"], writes=["negb"])
        with ExitStack() as es:
            wbs = [es.enter_context(sbt(f"ada_w{i}", [128, 16, 128], BF16)) for i in range(3)]
            for nb in range(96):
                wb = wbs[nb % 3]
                wn = f"ada_w{nb % 3}"
                P.dma("pool", wb[:].rearrange("p a b -> p (a b)"), wada[l * 96 + nb], writes=[wn])
                for kc in range(16):
                    P.op("pe", "matmul",
                        psf[6][:, nb * 2:nb * 2 + 2], wb[:, kc, :], scin[:, kc, :], start=(kc == 0), stop=(kc == 15),
                        reads=[wn, "scin"], writes=[PSF[6]], inc=(kc == 15))
            P.op("dve", "tensor_tensor",
                mods[:], psf[6][:, 0:192].rearrange("p (a b) -> p a b", b=2),
                lv[:, LV_BADA:LV_BADA + 96].unsqueeze(2).to_broadcast([128, 96, 2]), ALU.add,
                reads=[PSF[6], "lv"], writes=["mods"])
            P.op("dve", "tensor_scalar", ops1[:], mods[:], 1.0, None, ALU.add, reads=["mods"], writes=["ops1"])
        P.barrier()

        xsrc, xsn = (xT0, "xT0") if l == 0 else (xT, "xT")

        def modulate(src, srcname, m_sh, m_sc):
            with ExitStack() as es:
                xr = Ring(es, "mod_x", 3, [128, 16, 512], F32)
                hr = Ring(es, "mod_h", 2, [128, 16, 512], BF16)
                sv = src.rearrange("(c p) t -> p c t", p=128)
                hv = hT.rearrange("(c p) t -> p c t", p=128)
                loaded = {}

                def mload(i):
                    t0_, tw_ = SUBS[i]
                    xb_, xn_ = xr.next()
                    P.dma("sp", xb_[:, :, 0:tw_], sv[:, :, t0_:t0_ + tw_], reads=["D:" + srcname], writes=[xn_])
                    loaded[i] = (xb_, xn_)
                mload(0)
                for si, (t0, tw) in enumerate(SUBS):
                    if si + 1 < len(SUBS):
                        mload(si + 1)
                    xb, xn = loaded.pop(si)
                    hb, hn = hr.next()
                    col = col_of(t0)
                    for c in range(16):
                        eng = alt_eng()
                        sc_ap = ops1[:, m_sc * 16 + c, col:col + 1]
                        sh_ap = mods[:, m_sh * 16 + c, col:col + 1]
                        if eng == "act":
                            P.op("act", "activation",
                                out=hb[:, c, 0:tw], in_=xb[:, c, 0:tw], func=AF.Identity, bias=sh_ap, scale=sc_ap,
                                reads=[xn, "mods", "ops1"], writes=[hn])
                        else:
                            P.op("dve", "tensor_scalar",
                                hb[:, c, 0:tw], xb[:, c, 0:tw], sc_ap, sh_ap, ALU.mult, ALU.add,
                                reads=[xn, "mods", "ops1"], writes=[hn])
                    P.dma("sp", hv[:, :, t0:t0 + tw], hb[:, :, 0:tw], reads=[hn], writes=["D:hT"])
            P.barrier()

        modulate(xsrc, xsn, 0, 1)

        segs = [(qT, 0, F32, 8), (kT, 0, F32, 8), (gT, 0, F32, 16), (alrT, 0, BF16, 1), (pT, 0, F32, 8), (bgT, 0, F32, 32)]
        segmap = []
        for (dst, _, dt_, n) in segs:
            for i in range(n):
                segmap.append((dst, i, dt_))
        with ExitStack() as es:
            r32 = Ring(es, "in_s32", 4, [128, 512], F32)
            r16 = Ring(es, "in_s16", 2, [128, 512], BF16)

            def epi_in(nb, t0, tw, pss):
                dst, i, dt_ = segmap[nb]
                st, sn = (r32 if dt_ == F32 else r16).next()
                eng = alt_eng()
                if eng == "act":
                    P.op("act", "activation", out=st[:, 0:tw], in_=psf[pss[0]][:, 0:tw], func=AF.Copy,
                         reads=[PSF[pss[0]]], writes=[sn])
                else:
                    P.op("dve", "tensor_copy", st[:, 0:tw], psf[pss[0]][:, 0:tw], reads=[PSF[pss[0]]], writes=[sn])
                P.dma("sp", dst[i * 128:(i + 1) * 128, t0:t0 + tw], st[:, 0:tw], reads=[sn], writes=["D:" + dst.name])

            linear("lin_in", hT, "hT", 16, win_fm, l * NB_IN, NB_IN, 1, 2304, None, epi_in)

        with ExitStack() as es:
            xs = es.enter_context(sbt("v_x", [128, 16, 2304], BF16))
            wbs = [es.enter_context(sbt(f"v_w{i}", [128, 16, 512], BF16)) for i in range(2)]
            vr = Ring(es, "v_s", 4, [128, 512], BF16)
            xv = hT.rearrange("(kc p) t -> p kc t", p=128)
            bank = 0
            widx = 0
            for (g0, gw) in ((0, 2304), (2304, 2048)):
                for c0 in range(0, 16, 4):
                    P.dma("sp", xs[:, c0:c0 + 4, 0:gw], xv[:, c0:c0 + 4, g0:g0 + gw], reads=["D:hT"], writes=["v_x"])
                for cb in range(4):
                    wb = wbs[widx % 2]
                    wn = f"v_w{widx % 2}"
                    widx += 1
                    wflat = wb[:].rearrange("p a b -> p (a b)")
                    for c0 in range(0, 16 * 512, 2048):
                        P.dma("pool", wflat[:, c0:c0 + 2048], win_v[l * 4 + cb][:, c0:c0 + 2048], writes=[wn])
                    for tt in range(gw // 128):
                        b = bank % 6
                        bank += 1
                        for kc in range(16):
                            P.op("pe", "matmul",
                                psf[b][:, 0:512], xs[:, kc, tt * 128:(tt + 1) * 128], wb[:, kc, :],
                                start=(kc == 0), stop=(kc == 15), reads=[wn, "v_x"], writes=[PSF[b]], inc=(kc == 15))
                        st, sn = vr.next()
                        eng = alt_eng()
                        if eng == "act":
                            P.op("act", "activation", out=st[:], in_=psf[b][:], func=AF.Copy,
                                 reads=[PSF[b]], writes=[sn])
                        else:
                            P.op("dve", "tensor_copy", st[:], psf[b][:], reads=[PSF[b]], writes=[sn])
                        r0 = g0 + tt * 128
                        P.dma("sp", vtm[r0:r0 + 128, cb * 512:(cb + 1) * 512], st[:], reads=[sn], writes=["D:vtm"])
        P.barrier()

        with ExitStack() as es:
            GB = 256
            Xs = [es.enter_context(sbt(f"g_S{i}", [128, 2, 512], F32)) for i in range(2)]
            q_r = Ring(es, "g_q", 2, [128, 2, GB], F32)
            k_r = Ring(es, "g_k", 2, [128, 2, GB], F32)
            a_r = Ring(es, "g_alr", 2, [16, GB], BF16)
            v_r = Ring(es, "g_v", 4, [64, 4, 512], BF16)
            of_r = Ring(es, "g_of", 4, [128, 4, GB], F32)
            gg_r = Ring(es, "g_g", 4, [128, 4, GB], F32)
            la = es.enter_context(sbt("g_la", [128, 2, GB], F32))
            cum = es.enter_context(sbt("g_cum", [128, 2, GB], F32))
            Aa = es.enter_context(sbt("g_A", [128, 2, GB], F32))
            Bb = es.enter_context(sbt("g_B", [128, 2, GB], F32))
            kin_r = Ring(es, "g_kin", 2, [128, 2, GB], BF16)
            ks_r = Ring(es, "g_ks", 2, [128, 2, GB], BF16)
            qin_r = Ring(es, "g_qin", 2, [128, 2, GB], BF16)
            qint_r = Ring(es, "g_qint", 4, [128, 2, GB], BF16)
            refv = es.enter_context(sbt("g_ref", [128, 2, 4], F32))
            lastv = es.enter_context(sbt("g_last", [128, 2, 4], F32))
            er = es.enter_context(sbt("g_er", [128, 2, 4], F32))
            elr_r = Ring(es, "g_elr", 2, [128, 2, 4], F32)
            el_r = Ring(es, "g_el", 3, [128, 2, 4], F32)
            ktm_r = Ring(es, "g_ktm", 2, [64, 4, 256], BF16)
            sT_r = Ring(es, "g_sT", 3, [64, 4, 64], BF16)
            sball = [es.enter_context(sbt(f"g_Sb{i}", [128, 4, 2, 512], BF16)) for i in range(3)]
            osb_r = Ring(es, "g_o", 2, [128, 4, GB], F32)
            sq = es.enter_context(sbt("g_sq", [128, 4, GB], BF16))
            rs = es.enter_context(sbt("g_rs", [128, GB], F32))
            og_r = Ring(es, "g_og", 2, [128, 4, GB], BF16)
            KVB = ((2, 3), (5, 6))
            OB = (4, 4)
            LN16 = math.log(1.0 / 16.0)
            lat = [(CTX + GB * i, GB) for i in range(SEQ // GB)]
            for h in range(4):
                qv = qT[h * 256:(h + 1) * 256, :].rearrange("(dc p) t -> p dc t", p=128)
                kv_ = kT[h * 256:(h + 1) * 256, :].rearrange("(dc p) t -> p dc t", p=128)
                ov = oT[h * 512:(h + 1) * 512, :].rearrange("(vc p) t -> p vc t", p=128)
                gv = gT[h * 512:(h + 1) * 512, :].rearrange("(vc p) t -> p vc t", p=128)
                ogv = ogT[h * 512:(h + 1) * 512, :].rearrange("(vc p) t -> p vc t", p=128)
                for dr in (0, 1):
                    blocks = [(0, CTX)] + (lat if dr == 0 else lat[::-1])
                    ri, li = (31, 63) if dr == 0 else (32, 0)
                    P.op("dve", "memset", Xs[0][:], 0.0, writes=["g_S0"])
                    P.op("dve", "memset", sball[0][:, 0, :, :], 0.0, writes=["g_Sb0"])
                    state = dict(si=0)
                    ctxs = {}

                    def s1(n, h=h, dr=dr, blocks=blocks, ri=ri, li=li, qv=qv, kv_=kv_, ov=ov, gv=gv):
                        t0, tw = blocks[n]
                        nch = tw // 64
                        par = n % 2
                        c = dict(t0=t0, tw=tw, nch=nch, par=par)
                        ctxs[n] = c
                        qb, qn = q_r.next()
                        kb, kn = k_r.next()
                        ab, an = a_r.next()
                        vb, vn = v_r.next()
                        c.update(vb=vb, vn=vn)
                        P.dma("sp", qb[:, :, 0:tw], qv[:, :, t0:t0 + tw], reads=["D:qT"], writes=[qn])
                        yield
                        P.dma("sp", kb[:, :, 0:tw], kv_[:, :, t0:t0 + tw], reads=["D:kT"], writes=[kn])
                        yield
                        P.dma("sp", ab[:, 0:tw], alrT[dr * 16:(dr + 1) * 16, t0:t0 + tw], reads=["D:alrT"], writes=[an])
                        yield
                        P.dma("sp", vb[:, 0:nch, :],
                              vtm[t0:t0 + tw, h * 512:(h + 1) * 512].rearrange("(c p) n -> p c n", p=64),
                              reads=["D:vtm"], writes=[vn])
                        yield
                        if dr == 1:
                            ofb, ofn = of_r.next()
                            ggb, ggn = gg_r.next()
                            c.update(ofb=ofb, ofn=ofn, ggb=ggb, ggn=ggn)
                            P.dma("sp", ofb[:, :, 0:tw], ov[:, :, t0:t0 + tw], reads=["D:oT"], writes=[ofn])
                            yield
                            P.dma("sp", ggb[:, :, 0:tw], gv[:, :, t0:t0 + tw], reads=["D:gT"], writes=[ggn])
                            yield
                        for dc in range(2):
                            col = (h * 2 + dc) * 128
                            P.op("pe", "matmul", psf[0][:, dc * 256:dc * 256 + tw],
                                 wdu_sb[0:16, dr * 1024 + col:dr * 1024 + col + 128], ab[0:16, 0:tw],
                                 start=True, stop=True, reads=["wdu", an], writes=["psz"])
                            yield
                            bi = dr * 8 + h * 2 + dc
                            P.op("act", "activation", out=la[:, dc, 0:tw], in_=psf[0][:, dc * 256:dc * 256 + tw], func=AF.Exp,
                                 bias=negb[:, bi:bi + 1], scale=-1.0, reads=["psz", "negb"], writes=["g_la"])
                            yield
                        P.op("act", "activation", out=la[:, :, 0:tw], in_=la[:, :, 0:tw], func=AF.Ln, bias=1.0,
                             reads=["g_la"], writes=["g_la"])
                        yield
                        for dc in range(2):
                            P.op("dve", "tensor_tensor_scan", cum[:, dc, 0:tw], rmask[:, 0:tw], la[:, dc, 0:tw], 0.0, ALU.mult, ALU.add,
                                 reads=["g_la", "rmask"], writes=["g_cum"])
                            yield

                        def cv(tl):
                            return tl[:, :, 0:tw].rearrange("p d (c i) -> p d c i", i=64)

                        def bc(sm):
                            return sm[:, :, 0:nch].unsqueeze(3).to_broadcast([128, 2, nch, 64])
                        if dr == 1:
                            P.op("dve", "tensor_copy", lastv[:, :, 0:nch], cv(cum)[:, :, :, 63], reads=["g_cum"], writes=["g_last"])
                            yield
                            P.op("dve", "tensor_tensor", cum[:, :, 0:tw], la[:, :, 0:tw], cum[:, :, 0:tw], ALU.subtract,
                                 reads=["g_la", "g_cum"], writes=["g_cum"])
                            yield
                            P.op("dve", "tensor_tensor", cv(cum), cv(cum), bc(lastv), ALU.add, reads=["g_cum", "g_last"], writes=["g_cum"])
                            yield
                        elb, eln = el_r.next()
                        kin, kinn = kin_r.next()
                        elr, elrn = elr_r.next()
                        c.update(elb=elb, eln=eln, kin=kin, kinn=kinn, elr=elr, elrn=elrn)
                        P.op("dve", "tensor_copy", refv[:, :, 0:nch], cv(cum)[:, :, :, ri], reads=["g_cum"], writes=["g_ref"])
                        yield
                        P.op("dve", "tensor_copy", lastv[:, :, 0:nch], cv(cum)[:, :, :, li], reads=["g_cum"], writes=["g_last"])
                        yield
                        P.op("act", "activation", out=Aa[:, :, 0:tw], in_=cum[:, :, 0:tw], func=AF.Exp, bias=ln16[:, 0:1], scale=-1.0 / 16.0,
                             reads=["g_cum", "ln16"], writes=["g_A"])
                        yield
                        qinb, qinn = qin_r.next()
                        qintb, qintn = qint_r.next()
                        c.update(qintb=qintb, qintn=qintn)
                        P.op("dve", "tensor_tensor", qintb[:, :, 0:tw], qb[:, :, 0:tw], Aa[:, :, 0:tw], ALU.mult, reads=[qn, "g_A"], writes=[qintn])
                        yield
                        P.op("dve", "tensor_tensor", cv(cum), cv(cum), bc(refv), ALU.subtract, reads=["g_cum", "g_ref"], writes=["g_cum"])
                        yield
                        P.op("dve", "tensor_tensor", elr[:, :, 0:nch], lastv[:, :, 0:nch], refv[:, :, 0:nch], ALU.subtract,
                             reads=["g_last", "g_ref"], writes=[elrn])
                        yield
                        P.op("act", "activation", out=Aa[:, :, 0:tw], in_=cum[:, :, 0:tw], func=AF.Exp, bias=ln16[:, 0:1], scale=-1.0 / 16.0,
                             reads=["g_cum", "ln16", qintn], writes=["g_A"])
                        yield
                        P.op("act", "activation", out=Bb[:, :, 0:tw], in_=cum[:, :, 0:tw], func=AF.Exp, scale=1.0 / 16.0,
                             reads=["g_cum"], writes=["g_B"])
                        yield
                        P.op("act", "activation", out=elb[:, :, 0:nch], in_=lastv[:, :, 0:nch], func=AF.Exp, scale=-1.0 / 16.0,
                             reads=["g_last"], writes=[eln])
                        yield
                        P.op("act", "activation", out=elr[:, :, 0:nch], in_=elr[:, :, 0:nch], func=AF.Exp, scale=-1.0 / 16.0,
                             reads=[elrn], writes=[elrn])
                        yield
                        P.op("dve", "tensor_tensor", kin[:, :, 0:tw], kb[:, :, 0:tw], Bb[:, :, 0:tw], ALU.mult, reads=[kn, "g_B"], writes=[kinn])
                        yield
                        ksb, ksn = ks_r.next()
                        c.update(ksb=ksb, ksn=ksn)
                        P.op("dve", "tensor_tensor", cv(ksb), cv(kin), bc(elr), ALU.mult, reads=[kinn, elrn], writes=[ksn])
                        yield
                        P.op("dve", "tensor_tensor", qinb[:, :, 0:tw], qb[:, :, 0:tw], Aa[:, :, 0:tw], ALU.mult, reads=[qn, "g_A"], writes=[qinn])
                        yield
                        order = list(range(nch)) if dr == 0 else list(range(nch))[::-1]
                        c.update(order=order, qinb=qinb, qinn=qinn)

                    def s1b(n, dr=dr):
                        c = ctxs[n]
                        nch, par, vb, vn, order = c["nch"], c["par"], c["vb"], c["vn"], c["order"]
                        qinb, qinn = c["qinb"], c["qinn"]
                        kin, kinn, ksb, ksn = c["kin"], c["kinn"], c["ksb"], c["ksn"]
                        ktm, ktn = ktm_r.next()
                        sT, sTn = sT_r.next()
                        c.update(sT=sT, sTn=sTn)
                        sbi = n % 3
                        for kk, ch in enumerate(order):
                            cs = slice(ch * 64, (ch + 1) * 64)
                            for dc in range(2):
                                P.op("pe", "transpose", psb[0:64, kk * 256 + dc * 128:kk * 256 + (dc + 1) * 128], ksb[:, dc, cs], ident[:, :],
                                     reads=[ksn, "ident"], writes=["psb"], inc=(kk == nch - 1 and dc == 1))
                        yield
                        P.op("act", "activation", out=ktm[:, 0:nch, :].rearrange("p a b -> p (a b)"), in_=psb[0:64, 0:nch * 256], func=AF.Copy,
                             reads=["psb"], writes=[ktn])
                        yield
                        sreg = "pss"
                        for kk, ch in enumerate(order):
                            cs = slice(ch * 64, (ch + 1) * 64)
                            for dc in range(2):
                                P.op("pe", "matmul", psf[1][0:64, kk * 64:(kk + 1) * 64], kin[:, dc, cs], qinb[:, dc, cs],
                                     start=(dc == 0), stop=(dc == 1), reads=[kinn, qinn], writes=[sreg],
                                     inc=(kk == nch - 1 and dc == 1))
                        yield
                        P.op("dve", "tensor_tensor", sT[:, 0:nch, :],
                             psf[1][0:64, 0:nch * 64].rearrange("p (a b) -> p a b", b=64),
                             masks[:, dr * 64:(dr + 1) * 64].unsqueeze(1).to_broadcast([64, nch, 64]), ALU.mult,
                             reads=[sreg, "masks"], writes=[sTn])
                        yield
                        for kk, ch in enumerate(order):
                            i = state["si"]
                            src, dst = Xs[i], Xs[1 - i]
                            for dc in range(2):
                                b = KVB[kk % 2][dc]
                                P.op("pe", "matmul", psf[b][:, 0:512], ktm[0:64, kk, dc * 128:(dc + 1) * 128], vb[0:64, ch, :],
                                     start=True, stop=True, reads=[ktn, vn], writes=[PSF[b]])
                                yield
                                P.op("dve", "scalar_tensor_tensor", dst[:, dc, :], src[:, dc, :], c["elb"][:, dc, ch:ch + 1],
                                     psf[b][:, 0:512], ALU.mult, ALU.add,
                                     reads=[f"g_S{i}", c["eln"], PSF[b]], writes=[f"g_S{1 - i}"])
                                yield
                            if kk < nch - 1:
                                tp, tk = sbi, kk + 1
                            else:
                                tp, tk = (sbi + 1) % 3, 0
                            P.op("act", "activation", out=sball[tp][:, tk, :, :], in_=dst[:, :, :], func=AF.Copy,
                                 reads=[f"g_S{1 - i}"], writes=[f"g_Sb{tp}"])
                            state["si"] = 1 - i
                            yield

                    def s3(n, h=h, dr=dr, ov=ov, ogv=ogv):
                        c = ctxs.pop(n)
                        par, nch, t0, tw = c["par"], c["nch"], c["t0"], c["tw"]
                        sbi = n % 3
                        vb, vn, sT, sTn, qintb, qintn = c["vb"], c["vn"], c["sT"], c["sTn"], c["qintb"], c["qintn"]
                        ob, on = osb_r.next()
                        for kk, ch in enumerate(c["order"]):
                            cs = slice(ch * 64, (ch + 1) * 64)
                            r0 = 0
                            ob_ = OB[kk % 2]
                            oreg = PSF[ob_]
                            for vc in range(4):
                                P.op("pe", "matmul", psf[ob_][:, r0 + vc * 64:r0 + (vc + 1) * 64], vb[0:64, ch, vc * 128:(vc + 1) * 128],
                                     sT[0:64, kk, :], start=True, stop=False, reads=[vn, sTn], writes=[oreg], inc=False)
                                for dc in range(2):
                                    P.op("pe", "matmul", psf[ob_][:, r0 + vc * 64:r0 + (vc + 1) * 64],
                                         sball[sbi][:, kk, dc, vc * 128:(vc + 1) * 128], qintb[:, dc, cs],
                                         start=False, stop=(dc == 1), reads=[f"g_Sb{sbi}", qintn], writes=[oreg],
                                         inc=(vc == 3 and dc == 1))
                            pov = psf[ob_][:, r0:r0 + 256].rearrange("p (v i) -> p v i", i=64)
                            if dr == 0:
                                P.op("act", "activation", out=ob[:, :, cs], in_=pov, func=AF.Copy, reads=[oreg], writes=[on])
                                yield
                            else:
                                P.op("dve", "tensor_tensor", ob[:, :, cs], pov, c["ofb"][:, :, cs], ALU.add, reads=[oreg, c["ofn"]], writes=[on])
                                yield
                        if dr == 0:
                            P.dma("sp", ov[:, :, t0:t0 + tw], ob[:, :, 0:tw], reads=[on], writes=["D:oT"])
                            yield
                        else:
                            ggb, ggn = c["ggb"], c["ggn"]
                            ogb, ogn = og_r.next()
                            rreg = PSF[4]
                            P.op("act", "activation", out=sq[:, :, 0:tw], in_=ob[:, :, 0:tw], func=AF.Square, reads=[on], writes=["g_sq"])
                            yield
                            for vc in range(4):
                                P.op("pe", "matmul", psf[4][:, 0:tw], ones_bf[:, :], sq[:, vc, 0:tw],
                                     start=(vc == 0), stop=(vc == 3), reads=["ones_bf", "g_sq"], writes=[rreg], inc=(vc == 3))
                            P.op("act", "activation", out=rs[:, 0:tw], in_=psf[4][:, 0:tw], func=AF.Ln,
                                 bias=epsr[:, 0:1], scale=1.0 / 512.0, reads=[rreg, "epsr"], writes=["g_rs"])
                            yield
                            P.op("act", "activation", out=rs[:, 0:tw], in_=rs[:, 0:tw], func=AF.Exp, scale=-0.5,
                                 reads=["g_rs"], writes=["g_rs"])
                            yield
                            P.op("act", "activation", out=ggb[:, :, 0:tw], in_=ggb[:, :, 0:tw], func=AF.Silu, reads=[ggn], writes=[ggn])
                            yield
                            for vc in range(4):
                                gi = LV_GAIN + h * 4 + vc
                                P.op("dve", "scalar_tensor_tensor", ob[:, vc, 0:tw], ob[:, vc, 0:tw], lv[:, gi:gi + 1], rs[:, 0:tw],
                                     ALU.mult, ALU.mult, reads=[on, "lv", "g_rs"], writes=[on])
                                yield
                            P.op("dve", "tensor_tensor", ogb[:, :, 0:tw], ob[:, :, 0:tw], ggb[:, :, 0:tw], ALU.mult,
                                 reads=[on, ggn], writes=[ogn])
                            yield
                            P.dma("sp", ogv[:, :, t0:t0 + tw], ogb[:, :, 0:tw], reads=[ogn], writes=["D:ogT"])
                            yield

                    def rr(*gens):
                        gens = [g for g in gens if g is not None]
                        while gens:
                            for g in list(gens):
                                try:
                                    next(g)
                                except StopIteration:
                                    gens.remove(g)

                    rr(s1(0))
                    NBk = len(blocks)
                    for n in range(NBk):
                        rr(s1b(n),
                           s1(n + 1) if n + 1 < NBk else None,
                           s3(n - 1) if n >= 1 else None)
                    rr(s3(NBk - 1))
        P.barrier()

        if STOP == "gla":
            break
        with ExitStack() as es:
            p_r = Ring(es, "pl_p", 2, [128, T], F32)
            d_r = Ring(es, "pl_d", 2, [128, T], BF16)
            xp = es.enter_context(sbt("pl_xp", [128, 80, 80], F32))
            ba = es.enter_context(sbt("pl_a", [128, 6400], F32))
            bb = es.enter_context(sbt("pl_b", [128, 6400], F32))
            yp = es.enter_context(sbt("pl_yp", [128, 80, 64], F32))
            cp = es.enter_context(sbt("pl_cp", [128, 272], F32))
            P.op("dve", "memset", xp[:], 0.0, writes=["pl_xp"])
            P.op("dve", "memset", yp[:], 0.0, writes=["pl_yp"])
            P.op("dve", "memset", cp[:], 0.0, writes=["pl_cp"])
            for pc in range(8):
                wi = pc // 2
                w = WINS[wi]
                m = int(math.log2(w))
                lo = w // 2
                if pc == 0:
                    pl_loaded = {}
                    pb0, pn0 = p_r.next()
                    P.dma("sp", pb0[:], pT[0:128, :], reads=["D:pT"], writes=[pn0])
                    pl_loaded[0] = (pb0, pn0)
                pb, pn = pl_loaded.pop(pc)
                db, dn = d_r.next()
                if pc + 1 < 8:
                    pb1, pn1 = p_r.next()
                    P.dma("sp", pb1[:], pT[(pc + 1) * 128:(pc + 2) * 128, :], reads=["D:pT"], writes=[pn1])
                    pl_loaded[pc + 1] = (pb1, pn1)
                pimg = pb[:, CTX:T].rearrange("p (r c) -> p r c", c=64)
                P.op("act", "activation", out=xp[:, 8:72, 8:72], in_=pimg, func=AF.Copy,
                     reads=[pn], writes=["pl_xp"])
                a80 = ba[:].rearrange("p (r c) -> p r c", c=80)
                b80 = bb[:].rearrange("p (r c) -> p r c", c=80)
                cur, curn, width = xp[:], "pl_xp", 80
                for s in range(m):
                    sh = 2 ** s
                    nw_ = width - sh
                    nxt, nxtn = (a80, "pl_a") if s % 2 == 0 else (b80, "pl_b")
                    P.op("dve", "tensor_tensor",
                        nxt[:, 8:72, 0:nw_], cur[:, 8:72, 0:nw_], cur[:, 8:72, sh:sh + nw_], ALU.add,
                        reads=[curn], writes=[nxtn])
                    cur, curn, width = nxt, nxtn, nw_
                icol = invc[:, wi * 64:(wi + 1) * 64].unsqueeze(1).to_broadcast([128, 64, 64])
                P.op("dve", "tensor_tensor",
                    yp[:, 8:72, :], cur[:, 8:72, 8 - lo:8 - lo + 64], icol, ALU.mult, reads=[curn, "invc"], writes=["pl_yp"])
                a64 = ba[:, 0:5120].rearrange("p (r c) -> p r c", c=64)
                b64 = bb[:, 0:5120].rearrange("p (r c) -> p r c", c=64)
                cur, curn, height = yp[:], "pl_yp", 80
                for s in range(m):
                    sh = 2 ** s
                    nh = height - sh
                    nxt, nxtn = (a64, "pl_a") if s % 2 == 0 else (b64, "pl_b")
                    P.op("dve", "tensor_tensor",
                        nxt[:, 0:nh, :], cur[:, 0:nh, :], cur[:, sh:sh + nh, :], ALU.add, reads=[curn], writes=[nxtn])
                    cur, curn, height = nxt, nxtn, nh
                irow = invc[:, 256 + wi * 64:256 + (wi + 1) * 64].unsqueeze(2).to_broadcast([128, 64, 64])
                mean_n = "pl_b" if curn == "pl_a" else "pl_a"
                mean_t = b64 if curn == "pl_a" else a64
                P.op("dve", "tensor_tensor",
                    mean_t[:, 0:64, :], cur[:, 8 - lo:8 - lo + 64, :], irow, ALU.mult, reads=[curn, "invc"], writes=[mean_n])
                dimg = db[:, CTX:T].rearrange("p (r c) -> p r c", c=64)
                P.op("dve", "tensor_tensor",
                    dimg, mean_t[:, 0:64, :], pimg, ALU.subtract, reads=[mean_n, pn], writes=[dn])
                P.op("act", "activation", out=cp[:, 8:264], in_=pb[:, 0:CTX], func=AF.Copy,
                     reads=[pn], writes=["pl_cp"])
                cur, curn, width = cp[:], "pl_cp", 272
                for s in range(m):
                    sh = 2 ** s
                    nw_ = width - sh
                    nxt, nxtn = (ba, "pl_a") if s % 2 == 0 else (bb, "pl_b")
                    P.op("dve", "tensor_tensor",
                        nxt[:, 0:nw_], cur[:, 0:nw_], cur[:, sh:sh + nw_], ALU.add, reads=[curn], writes=[nxtn])
                    cur, curn, width = nxt[:, :], nxtn, nw_
                mean_n = "pl_b" if curn == "pl_a" else "pl_a"
                mean_c = bb if curn == "pl_a" else ba
                P.op("dve", "tensor_tensor",
                    mean_c[:, 0:CTX], cur[:, 8 - lo:8 - lo + CTX], invc[:, 512 + wi * 256:512 + (wi + 1) * 256], ALU.mult,
                    reads=[curn, "invc"], writes=[mean_n])
                P.op("dve", "tensor_tensor",
                    db[:, 0:CTX], mean_c[:, 0:CTX], pb[:, 0:CTX], ALU.subtract, reads=[mean_n, pn], writes=[dn])
                P.dma("sp", dT[pc * 128:(pc + 1) * 128, :], db[:], reads=[dn], writes=["D:dT"])
        P.barrier()

        with ExitStack() as es:
            r16 = Ring(es, "pg_s", 4, [128, 512], BF16)

            def epi_pg(nb, t0, tw, pss):
                st, sn = r16.next()
                P.op("act", "activation", out=st[:, 0:tw], in_=psf[pss[0]][:, 0:tw], func=AF.Copy,
                                                   scale=lv[:, LV_PSC + nb:LV_PSC + nb + 1],
                     reads=[PSF[pss[0]], "lv"], writes=[sn])
                P.dma("sp", eT[nb * 128:(nb + 1) * 128, t0:t0 + tw], st[:, 0:tw], reads=[sn], writes=["D:eT"])

            linear("lin_pg", dT, "dT", 8, wpg, l * 8, 8, 1, 4352, None, epi_pg, kcs=lambda nb: [2 * (nb // 2), 2 * (nb // 2) + 1])

        with ExitStack() as es:
            rb = Ring(es, "go_b", 6, [128, 512], F32)
            rsg = Ring(es, "go_s", 4, [128, 512], F32)
            pend = {}

            def pre_go(nb, t0, tw):
                bt, bn = rb.next()
                P.dma("sp", bt[:, 0:tw], bgT[nb * 128:(nb + 1) * 128, t0:t0 + tw], reads=["D:bgT"], writes=[bn])
                pend[(nb, t0)] = (bt, bn)

            def epi_go(nb, t0, tw, pss):
                bt, bn = pend.pop((nb, t0))
                st, sn = rsg.next()
                P.op("act", "activation", out=bt[:, 0:tw], in_=bt[:, 0:tw], func=AF.Sigmoid, reads=[bn], writes=[bn])
                P.op("dve", "tensor_tensor", st[:, 0:tw], psf[pss[0]][:, 0:tw], bt[:, 0:tw], ALU.mult,
                     reads=[PSF[pss[0]], bn], writes=[sn])
                P.dma("sp", t1T[nb * 128:(nb + 1) * 128, t0:t0 + tw], st[:, 0:tw], reads=[sn], writes=["D:t1T"])

            linear("lin_go", ogT, "ogT", 16, wgo, l * 16, 16, 1, 2304, pre_go, epi_go)

        with ExitStack() as es:
            rb = Ring(es, "po_b", 6, [128, 512], F32)
            rt = Ring(es, "po_t", 6, [128, 512], F32)
            rsg = Ring(es, "po_s", 4, [128, 512], BF16)
            pend = {}

            def pre_po(nb, t0, tw):
                bt, bn = rb.next()
                tt, tn = rt.next()
                P.dma("sp", bt[:, 0:tw], bgT[D + nb * 128:D + (nb + 1) * 128, t0:t0 + tw], reads=["D:bgT"], writes=[bn])
                P.dma("sp", tt[:, 0:tw], t1T[nb * 128:(nb + 1) * 128, t0:t0 + tw], reads=["D:t1T"], writes=[tn])
                pend[(nb, t0)] = (bt, bn, tt, tn)

            def epi_po(nb, t0, tw, pss):
                bt, bn, tt, tn = pend.pop((nb, t0))
                st, sn = rsg.next()
                P.op("act", "activation", out=bt[:, 0:tw], in_=bt[:, 0:tw], func=AF.Sigmoid, reads=[bn], writes=[bn])
                P.op("dve", "tensor_tensor", bt[:, 0:tw], psf[pss[0]][:, 0:tw], bt[:, 0:tw], ALU.mult,
                     reads=[PSF[pss[0]], bn], writes=[bn])
                P.op("dve", "tensor_tensor", st[:, 0:tw], bt[:, 0:tw], tt[:, 0:tw], ALU.add,
                     reads=[bn, tn], writes=[sn])
                P.dma("sp", mT[nb * 128:(nb + 1) * 128, t0:t0 + tw], st[:, 0:tw], reads=[sn], writes=["D:mT"])

            linear("lin_po", eT, "eT", 8, wpo, l * 16, 16, 1, 4352, pre_po, epi_po)

        def resid(es, name, m_gt, xsrc_, xsn_):
            rx = Ring(es, name + "_x", 6, [128, 512], F32)
            rt = Ring(es, name + "_t", 4, [128, 512], F32)
            pend = {}

            def pre(nb, t0, tw):
                xt, xn = rx.next()
                P.dma("sp", xt[:, 0:tw], xsrc_[nb * 128:(nb + 1) * 128, t0:t0 + tw], reads=["D:" + xsn_], writes=[xn])
                pend[(nb, t0)] = (xt, xn)

            def epi(nb, t0, tw, pss):
                xt, xn = pend.pop((nb, t0))
                tt, tn = rt.next()
                col = col_of(t0)
                P.op("act", "activation", out=tt[:, 0:tw], in_=psf[pss[0]][:, 0:tw], func=AF.Copy,
                                                   scale=mods[:, m_gt * 16 + nb, col:col + 1],
                     reads=[PSF[pss[0]], "mods"], writes=[tn])
                P.op("dve", "scalar_tensor_tensor", xt[:, 0:tw], xt[:, 0:tw], float(ALPHA), tt[:, 0:tw], ALU.mult, ALU.add,
                     reads=[xn, tn], writes=[xn])
                P.dma("sp", uT[nb * 128:(nb + 1) * 128, t0:t0 + tw], xt[:, 0:tw], reads=[xn], writes=["D:uT"])
            return pre, epi

        def layer_norm(g_off, b_off):
            with ExitStack() as es:
                ur = Ring(es, "ln_u", 2, [128, 16, 512], F32)
                sqr = Ring(es, "ln_sq", 2, [128, 16, 512], BF16)
                mr = Ring(es, "ln_mean", 2, [128, 512], F32)
                qr = Ring(es, "ln_msq", 2, [128, 512], F32)
                rr = Ring(es, "ln_rstd", 2, [128, 512], F32)
                uv = uT.rearrange("(c p) t -> p c t", p=128)
                xv = xT.rearrange("(c p) t -> p c t", p=128)
                loaded = {}

                def lload(i):
                    t0_, tw_ = SUBS[i]
                    ub_, un_ = ur.next()
                    P.dma("sp", ub_[:, :, 0:tw_], uv[:, :, t0_:t0_ + tw_], reads=["D:uT"], writes=[un_])
                    loaded[i] = (ub_, un_)
                lload(0)
                for si, (t0, tw) in enumerate(SUBS):
                    ub, un = loaded.pop(si)
                    sqt, sqn = sqr.next()
                    mean, mn = mr.next()
                    msq, qn_ = qr.next()
                    rstd, rn = rr.next()
                    pa, pb = (0, 1) if si % 2 == 0 else (2, 3)
                    P.op("act", "activation", out=sqt[:, :, 0:tw], in_=ub[:, :, 0:tw], func=AF.Square, reads=[un], writes=[sqn])
                    for c in range(16):
                        P.op("pe", "matmul", psf[pa][:, 0:tw], ones_f[:, :], ub[:, c, 0:tw], start=(c == 0), stop=(c == 15),
                             reads=["ones_f", un], writes=[PSF[pa]], inc=(c == 15))
                    for c in range(16):
                        P.op("pe", "matmul", psf[pb][:, 0:tw], ones_bf[:, :], sqt[:, c, 0:tw], start=(c == 0), stop=(c == 15),
                             reads=["ones_bf", sqn], writes=[PSF[pb]], inc=(c == 15))
                    P.op("act", "activation", out=mean[:, 0:tw], in_=psf[pa][:, 0:tw], func=AF.Copy, scale=1.0 / D,
                         reads=[PSF[pa]], writes=[mn])
                    P.op("dve", "tensor_tensor", msq[:, 0:tw], mean[:, 0:tw], mean[:, 0:tw], ALU.mult, reads=[mn], writes=[qn_])
                    P.op("dve", "scalar_tensor_tensor", rstd[:, 0:tw], psf[pb][:, 0:tw], 1.0 / D, msq[:, 0:tw], ALU.mult, ALU.subtract,
                         reads=[PSF[pb], qn_], writes=[rn])
                    P.op("act", "activation", out=rstd[:, 0:tw], in_=rstd[:, 0:tw], func=AF.Ln, bias=epsl[:, 0:1],
                         reads=[rn, "epsl"], writes=[rn])
                    P.op("act", "activation", out=rstd[:, 0:tw], in_=rstd[:, 0:tw], func=AF.Exp, scale=-0.5, reads=[rn], writes=[rn])
                    if si + 1 < len(SUBS):
                        lload(si + 1)
                    mb = mean[:, 0:tw].unsqueeze(1).to_broadcast([128, 16, tw])
                    rb_ = rstd[:, 0:tw].unsqueeze(1).to_broadcast([128, 16, tw])
                    P.op("dve", "tensor_tensor", ub[:, :, 0:tw], ub[:, :, 0:tw], mb, ALU.subtract, reads=[un, mn], writes=[un])
                    P.op("dve", "tensor_tensor", ub[:, :, 0:tw], ub[:, :, 0:tw], rb_, ALU.mult, reads=[un, rn], writes=[un])
                    for c in range(16):
                        P.op("act", "activation", out=ub[:, c, 0:tw], in_=ub[:, c, 0:tw], func=AF.Identity,
                             bias=lv[:, b_off + c:b_off + c + 1], scale=lv[:, g_off + c:g_off + c + 1],
                             reads=[un, "lv"], writes=[un])
                    P.dma("sp", xv[:, :, t0:t0 + tw], ub[:, :, 0:tw], reads=[un], writes=["D:xT"])
            P.barrier()

        with ExitStack() as es:
            pre, epi = resid(es, "ro", 2, xsrc, xsn)
            linear("lin_o", mT, "mT", 16, wo, l * 16, 16, 1, 2304, pre, epi)
        layer_norm(LV_LMG, LV_LMB)

        modulate(xT, "xT", 3, 4)
        with ExitStack() as es:
            rs_ = Ring(es, "fi_s", 4, [128, 512], F32)
            rh = Ring(es, "fi_h", 4, [128, 512], BF16)

            def epi_fi(nb, t0, tw, pss):
                st, sn = rs_.next()
                ht, hn = rh.next()
                P.op("act", "activation", out=st[:, 0:tw], in_=psf[pss[0]][:, 0:tw], func=AF.Silu,
                     reads=[PSF[pss[0]]], writes=[sn])
                P.op("dve", "tensor_tensor", ht[:, 0:tw], st[:, 0:tw], psf[pss[1]][:, 0:tw], ALU.mult,
                     reads=[sn, PSF[pss[1]]], writes=[hn])
                P.dma("sp", hidT[nb * 128:(nb + 1) * 128, t0:t0 + tw], ht[:, 0:tw], reads=[hn], writes=["D:hidT"])

            linear("lin_fi", hT, "hT", 16, wfi, l * 44, 44, 2, 2304, None, epi_fi)
        with ExitStack() as es:
            pre, epi = resid(es, "rf", 5, xT, "xT")
            linear("lin_fo", hidT, "hidT", 44, wfo, l * 16, 16, 1, 1024, pre, epi)
        layer_norm(LV_LFG, LV_LFB)

    for r0 in range(0, D, 128):
        P.dma("sp", yT[r0:r0 + 128, :], xT[r0:r0 + 128, CTX:T], reads=["D:xT"], writes=["D:yT"])
    P.barrier()
    P.build()
    return nc


def _blocks(W):
    Kd, N = W.shape
    return np.ascontiguousarray(W.reshape(Kd // 128, 128, N // 128, 128).transpose(2, 1, 0, 3)).reshape(
        N // 128, 128, (Kd // 128) * 128)


def _consts():
    ident = np.eye(128, dtype=np.float32)
    j = np.arange(64)[:, None]
    i = np.arange(64)[None, :]
    masks = np.concatenate([(i >= j), (i <= j)], axis=1).astype(np.float32)
    rmask = np.ones((128, 512), np.float32)
    rmask[:, ::64] = 0.0
    def inv(n, w):
        lo = w // 2
        hi = w - lo - 1
        idx = np.arange(n)
        return (1.0 / (np.minimum(idx + hi + 1, n) - np.maximum(idx - lo, 0))).astype(np.float32)
    row = np.concatenate([inv(64, w) for w in WINS] + [inv(64, w) for w in WINS] + [inv(CTX, w) for w in WINS])
    invc = np.ascontiguousarray(np.broadcast_to(row[None, :], (128, 1536))).astype(np.float32)
    return dict(ident=ident, masks=masks, rmask=rmask, invc=invc)


def _prep_weights(inp, L):
    f = lambda a: np.asarray(a, dtype=np.float32)
    out = {}
    wada, lvec, wdu, win_fm, win_v, wpg, wgo, wpo, wo, wfi, wfo = ([] for _ in range(11))
    for l in range(L):
        wada.append(_blocks(f(inp["w_ada"][l])))
        lv = np.zeros((128, LV_N), np.float32)
        lv[:, LV_BADA:LV_BADA + 96] = f(inp["b_ada"][l]).reshape(96, 128).T
        lv[:, LV_GAIN:LV_GAIN + 16] = f(inp["gla_norm_gain"][l]).reshape(16, 128).T
        lv[:, LV_PSC:LV_PSC + 8] = f(inp["pool_scale"][l]).reshape(8, 128).T
        lv[:, LV_LMG:LV_LMG + 16] = f(inp["ln_mix_gain"][l]).reshape(16, 128).T
        lv[:, LV_LMB:LV_LMB + 16] = f(inp["ln_mix_bias"][l]).reshape(16, 128).T
        lv[:, LV_LFG:LV_LFG + 16] = f(inp["ln_ffn_gain"][l]).reshape(16, 128).T
        lv[:, LV_LFB:LV_LFB + 16] = f(inp["ln_ffn_bias"][l]).reshape(16, 128).T
        lv[:, LV_BDEC:LV_BDEC + 16] = f(inp["b_decay_up"][l]).reshape(16, 128).T
        lvec.append(lv)
        wdu.append(f(inp["w_decay_up"][l]).transpose(1, 0, 2).reshape(16, 2048))
        W = f(inp["w_in"][l])
        alr = np.zeros((D, 128), np.float32)
        alr[:, :32] = W[:, 6144:6176]
        Wsel = np.concatenate([W[:, 0:1024], W[:, 1024:2048], W[:, 4096:6144], alr, W[:, 6176:7200], W[:, 7200:11296]], axis=1)
        win_fm.append(_blocks(Wsel))
        win_v.append(np.ascontiguousarray(W[:, 2048:4096].reshape(16, 128, 4, 512).transpose(2, 1, 0, 3)).reshape(4, 128, 16 * 512))
        G = f(inp["w_pool_group"][l])
        wpg.append(np.ascontiguousarray(G.reshape(4, 2, 128, 2, 128).transpose(0, 3, 2, 1, 4)).reshape(8, 128, 256))
        wgo.append(_blocks(f(inp["w_gla_out"][l])))
        wpo.append(_blocks(f(inp["w_pool_out"][l])))
        wo.append(_blocks(f(inp["w_out"][l])))
        Wf = f(inp["w_ffn_in"][l])
        Gt = Wf[:, :DFF].reshape(16, 128, 44, 128).transpose(2, 1, 0, 3)
        Up = Wf[:, DFF:].reshape(16, 128, 44, 128).transpose(2, 1, 0, 3)
        wfi.append(np.ascontiguousarray(np.concatenate([Gt, Up], axis=3)).reshape(44, 128, 16 * 256))
        wfo.append(_blocks(f(inp["w_ffn_out"][l])))
    cat = lambda xs: np.ascontiguousarray(np.concatenate(xs, axis=0))
    out.update(wada=cat(wada), lvec=np.stack(lvec), wdu=np.stack(wdu), win_fm=cat(win_fm), win_v=cat(win_v), wpg=cat(wpg),
               wgo=cat(wgo), wpo=cat(wpo), wo=cat(wo), wfi=cat(wfi), wfo=cat(wfo))
    return out


def run_device(inp, depth=DEPTH, dbg=()):
    import time
    t0 = time.time()
    nc = build_program(depth, dbg)
    t1 = time.time()
    shared = _consts()
    shared.update(_prep_weights(inp, depth))
    print(f"[kernel] build {t1 - t0:.1f}s prep {time.time() - t1:.1f}s", flush=True)
    x = np.asarray(inp["x"], np.float32)
    c = np.asarray(inp["c"], np.float32)
    ctx = np.asarray(inp["ctx"], np.float32)
    cc = np.asarray(inp["c_ctx"], np.float32)
    percore = []
    for b in range(2):
        xT0 = np.ascontiguousarray(np.concatenate([ctx[b].T, x[b].T], axis=1))
        cT = np.zeros((128, 16, 2), np.float32)
        cT[:, :, 0] = c[b].reshape(16, 128).T
        cT[:, :, 1] = cc.reshape(16, 128).T
        percore.append(dict(xT0=xT0, cT=cT.reshape(128, 32)))
    in_maps = []
    for core in range(NCORES):
        m = dict(shared)
        m.update(percore[core % 2])
        in_maps.append(m)
    t2 = time.time()
    res = run_bass_kernel_spmd(nc, in_maps, core_ids=list(range(NCORES)))
    print(f"[kernel] launch {time.time() - t2:.1f}s", flush=True)
    return res.results


def kernel(**inputs):
    r = run_device(inputs, DEPTH)
    out = np.stack([np.ascontiguousarray(r[b]["yT"].T) for b in range(2)], axis=0)
    return out.astype(np.float32)
```

```python
import math
from contextlib import ExitStack

import numpy as np
import concourse.bass as bass
import concourse.mybir as mybir
from concourse.bass_utils import run_bass_kernel_spmd

F32 = mybir.dt.float32
BF16 = mybir.dt.bfloat16
AF = mybir.ActivationFunctionType
ALU = mybir.AluOpType

D = 2048
SEQ = 4096
CTX = 256
T = SEQ + CTX
DEPTH = 4
DK = 1024
DV = 2048
DFF = 5632
DPOOL = 1024
WINS = (2, 4, 8, 16)
ALPHA = (2.0 * DEPTH) ** 0.25
LN_EPS = 1e-5
RMS_EPS = 1e-6
NDS = 8
NCORES = 2
import os
STOP = os.environ.get("KSTOP", "")

SUBS = [(0, 256)] + [(256 + 512 * i, 512) for i in range(8)]
LV_BADA, LV_GAIN, LV_PSC, LV_LMG, LV_LMB, LV_LFG, LV_LFB, LV_BDEC, LV_N = 0, 96, 112, 120, 136, 152, 168, 184, 200
NB_IN = 73


class Prog:
    ENG = ("pe", "act", "dve", "pool", "sp")

    def __init__(self, nc):
        self.nc = nc
        self.ops = {e: [] for e in self.ENG}
        self.sem = {e: nc.alloc_semaphore("s_" + e) for e in self.ENG}
        self.semobj = dict(self.sem)
        self.cnt = {e: 0 for e in self.ENG}
        self.known = {e: {} for e in self.ENG}
        self.res = {}
        self.dq = {}
        for q in ("sp", "pool"):
            for i in range(NDS):
                self.semobj[(q, i)] = nc.alloc_semaphore(f"d_{q}{i}")
            self.dq[q] = dict(i=0, val=[0] * NDS)

    def _deps(self, reads, writes):
        d = {}
        for r in reads:
            for k, v in self.res.get(r, ({}, {}))[0].items():
                d[k] = max(d.get(k, 0), v)
        for w in writes:
            rw = self.res.get(w, ({}, {}))
            for dd in rw:
                for k, v in dd.items():
                    d[k] = max(d.get(k, 0), v)
        return d

    def _wait(self, e, d):
        for k, v in d.items():
            if e == "pe" and k == "pe":
                continue
            if self.known[e].get(k, 0) >= v:
                continue
            self.known[e][k] = v
            self.ops[e].append(("w", k, v))

    def _mark(self, reads, writes, k, v):
        for r in reads:
            dd = self.res.setdefault(r, ({}, {}))[1]
            dd[k] = max(dd.get(k, 0), v)
        for w in writes:
            dd = self.res.setdefault(w, ({}, {}))[0]
            dd[k] = max(dd.get(k, 0), v)

    def op(self, e, meth, *args, reads=(), writes=(), inc=True, **kw):
        self._wait(e, self._deps(reads, writes))
        if inc:
            self.cnt[e] += 1
            seq = self.cnt[e]
        else:
            seq = self.cnt[e] + 1
        self.ops[e].append(("i", (meth, args, kw), inc))
        self._mark(reads, writes, e, seq)

    def dma(self, q, out, in_, reads=(), writes=(), **kw):
        st = self.dq[q]
        i = st["i"]
        st["i"] = (i + 1) % NDS
        key = (q, i)
        d = self._deps(reads, writes)
        if st["val"][i] > 0:
            d[key] = max(d.get(key, 0), st["val"][i])
        self._wait(q, d)
        st["val"][i] += 16
        self.ops[q].append(("d", out, in_, key, kw))
        self._mark(reads, writes, key, st["val"][i])

    def barrier(self):
        d = {e: self.cnt[e] for e in ("pe", "act", "dve", "pool") if self.cnt[e] > 0}
        for q, st in self.dq.items():
            for i, v in enumerate(st["val"]):
                if v > 0:
                    d[(q, i)] = v
        for e in self.ENG:
            self._wait(e, dict(d))

    def build(self):
        nc = self.nc
        with nc.Block() as block:
            decos = dict(sp=block.sync, pool=block.gpsimd, act=block.scalar, dve=block.vector, pe=block.tensor)
            for e in self.ENG:
                def body(eng, e=e):
                    for o in self.ops[e]:
                        if o[0] == "w":
                            eng.wait_ge(self.semobj[o[1]], o[2])
                        elif o[0] == "i":
                            meth, a, kw = o[1]
                            ins = getattr(eng, meth)(*a, **kw)
                            if o[2]:
                                ins.then_inc(self.sem[e], 1)
                        else:
                            eng.dma_start(out=o[1], in_=o[2], **o[4]).then_inc(self.semobj[o[3]], 16)
                decos[e](body)


class K:
    def __init__(self, nc, depth, dbg):
        self.nc = nc
        self.P = Prog(nc)
        self.depth = depth
        self.dbg = dbg
        self.alt = 0

    def din(self, name, shape, dt=F32):
        return self.nc.dram_tensor(name, list(shape), dt, kind="ExternalInput").ap()

    def dscr(self, name, shape, dt=F32):
        kind = "ExternalOutput" if name in self.dbg else "Internal"
        return self.nc.dram_tensor(name, list(shape), dt, kind=kind).ap()


def build_program(depth=DEPTH, dbg=()):
    nc = bass.Bass("TRN2", target_bir_lowering=False)
    k = K(nc, depth, dbg)
    P = k.P
    L = depth
    xT0 = k.din("xT0", [D, T])
    cT = k.din("cT", [128, 32])
    ident_d = k.din("ident", [128, 128])
    masks_d = k.din("masks", [64, 128])
    rmask_d = k.din("rmask", [128, 512])
    invc_d = k.din("invc", [128, 1536])
    wada = k.din("wada", [L * 96, 128, 2048])
    lvec = k.din("lvec", [L, 128, LV_N])
    wdu = k.din("wdu", [L, 16, 2048])
    win_fm = k.din("win_fm", [L * NB_IN, 128, 2048])
    win_v = k.din("win_v", [L * 4, 128, 16 * 512])
    wpg = k.din("wpg", [L * 8, 128, 256])
    wgo = k.din("wgo", [L * 16, 128, 2048])
    wpo = k.din("wpo", [L * 16, 128, 1024])
    wo = k.din("wo", [L * 16, 128, 2048])
    wfi = k.din("wfi", [L * 44, 128, 16 * 256])
    wfo = k.din("wfo", [L * 16, 128, 44 * 128])
    yT = nc.dram_tensor("yT", [D, SEQ], F32, kind="ExternalOutput").ap()

    xT = k.dscr("xT", [D, T])
    uT = k.dscr("uT", [D, T])
    hT = k.dscr("hT", [D, T], BF16)
    qT = k.dscr("qT", [DK, T])
    kT = k.dscr("kT", [DK, T])
    gT = k.dscr("gT", [DV, T])
    alrT = k.dscr("alrT", [128, T], BF16)
    pT = k.dscr("pT", [DPOOL, T])
    bgT = k.dscr("bgT", [2 * D, T])
    vtm = k.dscr("vtm", [T, DV], BF16)
    oT = k.dscr("oT", [DV, T])
    ogT = k.dscr("ogT", [DV, T], BF16)
    dT = k.dscr("dT", [DPOOL, T], BF16)
    eT = k.dscr("eT", [DPOOL, T], BF16)
    t1T = k.dscr("t1T", [D, T])
    mT = k.dscr("mT", [D, T], BF16)
    hidT = k.dscr("hidT", [DFF, T], BF16)

    sb = nc.alloc_sbuf_tensor
    ones_bf = sb("ones_bf", [128, 128], BF16)
    ones_f = sb("ones_f", [128, 128], F32)
    ident = sb("ident_sb", [128, 128], BF16)
    masks = sb("masks_sb", [64, 128], F32)
    rmask = sb("rmask_sb", [128, 512], F32)
    invc = sb("invc_sb", [128, 1536], F32)
    lv = sb("lv_sb", [128, LV_N], F32)
    negb = sb("negb_sb", [128, 16], F32)
    mods = sb("mods_sb", [128, 96, 2], F32)
    ops1 = sb("ops1_sb", [128, 96, 2], F32)
    scin = sb("scin_sb", [128, 16, 2], BF16)
    ctmp = sb("ctmp_sb", [128, 32], F32)
    wdu_sb = sb("wdu_sb", [16, 2048], BF16)
    epsr = sb("epsr_sb", [128, 1], F32)
    epsl = sb("epsl_sb", [128, 1], F32)
    ln16 = sb("ln16_sb", [128, 1], F32)
    psf = [nc.alloc_psum_tensor(f"psf{i}", [128, 512], F32) for i in range(7)]
    psb = nc.alloc_psum_tensor("psb", [128, 1024], BF16)
    PSF = [f"psf{i}" for i in range(7)]

    uid = [0]

    def sbt(name, shape, dt):
        uid[0] += 1
        return nc.sbuf_tensor(f"{name}_u{uid[0]}", shape, dt)

    def alt_eng():
        k.alt ^= 1
        return "act" if k.alt else "dve"

    P.op("dve", "memset", ones_bf[:], 1.0, writes=["ones_bf"])
    P.op("dve", "memset", ones_f[:], 1.0, writes=["ones_f"])
    P.op("dve", "memset", epsr[:], RMS_EPS, writes=["epsr"])
    P.op("dve", "memset", epsl[:], LN_EPS, writes=["epsl"])
    P.op("dve", "memset", ln16[:], math.log(1.0 / 16.0), writes=["ln16"])
    P.dma("pool", ident[:], ident_d, writes=["ident"])
    P.dma("sp", masks[:], masks_d, writes=["masks"])
    P.dma("sp", rmask[:], rmask_d, writes=["rmask"])
    P.dma("sp", invc[:], invc_d, writes=["invc"])
    P.dma("sp", ctmp[:], cT, writes=["ctmp"])
    P.op("act", "activation", out=scin[:].rearrange("p a b -> p (a b)"), in_=ctmp[:], func=AF.Silu,
         reads=["ctmp"], writes=["scin"])

    def copy_dram(dst, src, rows, dt_bytes=4):
        for r0 in range(0, rows, 128):
            P.dma("sp", dst[r0:r0 + 128, :], src[r0:r0 + 128, :], reads=["D:" + src.name], writes=["D:" + dst.name])

    def tgroups(maxtok):
        gs, cur, tot = [], [], 0
        for s in SUBS:
            if cur and tot + s[1] > maxtok:
                gs.append(cur)
                cur, tot = [], 0
            cur.append(s)
            tot += s[1]
        gs.append(cur)
        return gs

    def linear(name, xin, xname, KC, wblk, wrow0, NB, nw, maxtok, pre, epi, kcs=None):
        nkc = KC if kcs is None else len(kcs(0))
        with ExitStack() as es:
            xs = es.enter_context(sbt(f"{name}_x", [128, KC, maxtok], BF16))
            wbs = [es.enter_context(sbt(f"{name}_w{i}", [128, nkc, nw * 128], BF16)) for i in range(3)]
            xv = xin.rearrange("(kc p) t -> p kc t", p=128)
            bank = 0
            widx = 0
            for grp in tgroups(maxtok):
                g0 = grp[0][0]
                gw = sum(s[1] for s in grp)
                step = max(1, KC // 4)
                for c0 in range(0, KC, step):
                    P.dma("sp", xs[:, c0:c0 + step, 0:gw], xv[:, c0:c0 + step, g0:g0 + gw],
                          reads=["D:" + xname], writes=[f"{name}_x"])
                items = [(nb_, t0_, tw_) for nb_ in range(NB) for (t0_, tw_) in grp]
                PF = 3
                pre_i = 0
                it_i = 0
                if pre is not None:
                    while pre_i < min(PF, len(items)):
                        pre(*items[pre_i])
                        pre_i += 1
                for nb in range(NB):
                    wb = wbs[widx % 3]
                    wn = f"{name}_w{widx % 3}"
                    widx += 1
                    per = nkc * nw * 128
                    wsrc = wblk[wrow0 + nb]
                    cstep = 2048
                    wflat = wb[:].rearrange("p a b -> p (a b)")
                    for c0 in range(0, per, cstep):
                        c1 = min(per, c0 + cstep)
                        P.dma("pool", wflat[:, c0:c1], wsrc[:, c0:c1], writes=[wn])
                    kl = list(range(KC)) if kcs is None else kcs(nb)
                    for (t0, tw) in grp:
                        if pre is not None and pre_i < len(items):
                            pre(*items[pre_i])
                            pre_i += 1
                        it_i += 1
                        pss = []
                        for j in range(nw):
                            b = bank % 6
                            bank += 1
                            pss.append(b)
                            for i, kc in enumerate(kl):
                                last = i == len(kl) - 1
                                P.op("pe", "matmul",
                                    psf[b][:, 0:tw], wb[:, i, j * 128:(j + 1) * 128], xs[:, kc, t0 - g0:t0 - g0 + tw],
                                    start=(i == 0), stop=last,
                                    reads=[wn, f"{name}_x"], writes=[PSF[b]], inc=last)
                        epi(nb, t0, tw, pss)
        P.barrier()

    class Ring:
        def __init__(self, es, name, n, shape, dt):
            self.t = [es.enter_context(sbt(f"{name}{i}", shape, dt)) for i in range(n)]
            self.names = [f"{name}{i}" for i in range(n)]
            self.i = 0

        def next(self):
            j = self.i % len(self.t)
            self.i += 1
            return self.t[j], self.names[j]

    def col_of(t0):
        return 1 if t0 < CTX else 0

    for l in range(L):
        P.dma("sp", lv[:], lvec[l], writes=["lv"])
        P.dma("pool", wdu_sb[:], wdu[l], writes=["wdu"])
        P.op("dve", "tensor_scalar", negb[:], lv[:, LV_BDEC:LV_BDEC + 16], -1.0, None, ALU.mult,
             reads=["lv"], writes=["negb"])
        with ExitStack() as es:
            wbs = [es.enter_context(sbt(f"ada_w{i}", [128, 16, 128], BF16)) for i in range(3)]
            for nb in range(96):
                wb = wbs[nb % 3]
                wn = f"ada_w{nb % 3}"
                P.dma("pool", wb[:].rearrange("p a b -> p (a b)"), wada[l * 96 + nb], writes=[wn])
                for kc in range(16):
                    P.op("pe", "matmul",
                        psf[6][:, nb * 2:nb * 2 + 2], wb[:, kc, :], scin[:, kc, :], start=(kc == 0), stop=(kc == 15),
                        reads=[wn, "scin"], writes=[PSF[6]], inc=(kc == 15))
            P.op("dve", "tensor_tensor",
                mods[:], psf[6][:, 0:192].rearrange("p (a b) -> p a b", b=2),
                lv[:, LV_BADA:LV_BADA + 96].unsqueeze(2).to_broadcast([128, 96, 2]), ALU.add,
                reads=[PSF[6], "lv"], writes=["mods"])
            P.op("dve", "tensor_scalar", ops1[:], mods[:], 1.0, None, ALU.add, reads=["mods"], writes=["ops1"])
        P.barrier()

        xsrc, xsn = (xT0, "xT0") if l == 0 else (xT, "xT")

        def modulate(src, srcname, m_sh, m_sc):
            with ExitStack() as es:
                xr = Ring(es, "mod_x", 3, [128, 16, 512], F32)
                hr = Ring(es, "mod_h", 2, [128, 16, 512], BF16)
                sv = src.rearrange("(c p) t -> p c t", p=128)
                hv = hT.rearrange("(c p) t -> p c t", p=128)
                loaded = {}

                def mload(i):
                    t0_, tw_ = SUBS[i]
                    xb_, xn_ = xr.next()
                    P.dma("sp", xb_[:, :, 0:tw_], sv[:, :, t0_:t0_ + tw_], reads=["D:" + srcname], writes=[xn_])
                    loaded[i] = (xb_, xn_)
                mload(0)
                for si, (t0, tw) in enumerate(SUBS):
                    if si + 1 < len(SUBS):
                        mload(si + 1)
                    xb, xn = loaded.pop(si)
                    hb, hn = hr.next()
                    col = col_of(t0)
                    for c in range(16):
                        eng = alt_eng()
                        sc_ap = ops1[:, m_sc * 16 + c, col:col + 1]
                        sh_ap = mods[:, m_sh * 16 + c, col:col + 1]
                        if eng == "act":
                            P.op("act", "activation",
                                out=hb[:, c, 0:tw], in_=xb[:, c, 0:tw], func=AF.Identity, bias=sh_ap, scale=sc_ap,
                                reads=[xn, "mods", "ops1"], writes=[hn])
                        else:
                            P.op("dve", "tensor_scalar",
                                hb[:, c, 0:tw], xb[:, c, 0:tw], sc_ap, sh_ap, ALU.mult, ALU.add,
                                reads=[xn, "mods", "ops1"], writes=[hn])
                    P.dma("sp", hv[:, :, t0:t0 + tw], hb[:, :, 0:tw], reads=[hn], writes=["D:hT"])
            P.barrier()

        modulate(xsrc, xsn, 0, 1)

        segs = [(qT, 0, F32, 8), (kT, 0, F32, 8), (gT, 0, F32, 16), (alrT, 0, BF16, 1), (pT, 0, F32, 8), (bgT, 0, F32, 32)]
        segmap = []
        for (dst, _, dt_, n) in segs:
            for i in range(n):
                segmap.append((dst, i, dt_))
        with ExitStack() as es:
            r32 = Ring(es, "in_s32", 4, [128, 512], F32)
            r16 = Ring(es, "in_s16", 2, [128, 512], BF16)

            def epi_in(nb, t0, tw, pss):
                dst, i, dt_ = segmap[nb]
                st, sn = (r32 if dt_ == F32 else r16).next()
                eng = alt_eng()
                if eng == "act":
                    P.op("act", "activation", out=st[:, 0:tw], in_=psf[pss[0]][:, 0:tw], func=AF.Copy,
                         reads=[PSF[pss[0]]], writes=[sn])
                else:
                    P.op("dve", "tensor_copy", st[:, 0:tw], psf[pss[0]][:, 0:tw], reads=[PSF[pss[0]]], writes=[sn])
                P.dma("sp", dst[i * 128:(i + 1) * 128, t0:t0 + tw], st[:, 0:tw], reads=[sn], writes=["D:" + dst.name])

            linear("lin_in", hT, "hT", 16, win_fm, l * NB_IN, NB_IN, 1, 2304, None, epi_in)

        with ExitStack() as es:
            xs = es.enter_context(sbt("v_x", [128, 16, 2304], BF16))
            wbs = [es.enter_context(sbt(f"v_w{i}", [128, 16, 512], BF16)) for i in range(2)]
            vr = Ring(es, "v_s", 4, [128, 512], BF16)
            xv = hT.rearrange("(kc p) t -> p kc t", p=128)
            bank = 0
            widx = 0
            for (g0, gw) in ((0, 2304), (2304, 2048)):
                for c0 in range(0, 16, 4):
                    P.dma("sp", xs[:, c0:c0 + 4, 0:gw], xv[:, c0:c0 + 4, g0:g0 + gw], reads=["D:hT"], writes=["v_x"])
                for cb in range(4):
                    wb = wbs[widx % 2]
                    wn = f"v_w{widx % 2}"
                    widx += 1
                    wflat = wb[:].rearrange("p a b -> p (a b)")
                    for c0 in range(0, 16 * 512, 2048):
                        P.dma("pool", wflat[:, c0:c0 + 2048], win_v[l * 4 + cb][:, c0:c0 + 2048], writes=[wn])
                    for tt in range(gw // 128):
                        b = bank % 6
                        bank += 1
                        for kc in range(16):
                            P.op("pe", "matmul",
                                psf[b][:, 0:512], xs[:, kc, tt * 128:(tt + 1) * 128], wb[:, kc, :],
                                start=(kc == 0), stop=(kc == 15), reads=[wn, "v_x"], writes=[PSF[b]], inc=(kc == 15))
                        st, sn = vr.next()
                        eng = alt_eng()
                        if eng == "act":
                            P.op("act", "activation", out=st[:], in_=psf[b][:], func=AF.Copy,
                                 reads=[PSF[b]], writes=[sn])
                        else:
                            P.op("dve", "tensor_copy", st[:], psf[b][:], reads=[PSF[b]], writes=[sn])
                        r0 = g0 + tt * 128
                        P.dma("sp", vtm[r0:r0 + 128, cb * 512:(cb + 1) * 512], st[:], reads=[sn], writes=["D:vtm"])
        P.barrier()

        with ExitStack() as es:
            GB = 256
            Xs = [es.enter_context(sbt(f"g_S{i}", [128, 2, 512], F32)) for i in range(2)]
            q_r = Ring(es, "g_q", 2, [128, 2, GB], F32)
            k_r = Ring(es, "g_k", 2, [128, 2, GB], F32)
            a_r = Ring(es, "g_alr", 2, [16, GB], BF16)
            v_r = Ring(es, "g_v", 4, [64, 4, 512], BF16)
            of_r = Ring(es, "g_of", 4, [128, 4, GB], F32)
            gg_r = Ring(es, "g_g", 4, [128, 4, GB], F32)
            la = es.enter_context(sbt("g_la", [128, 2, GB], F32))
            cum = es.enter_context(sbt("g_cum", [128, 2, GB], F32))
            Aa = es.enter_context(sbt("g_A", [128, 2, GB], F32))
            Bb = es.enter_context(sbt("g_B", [128, 2, GB], F32))
            kin_r = Ring(es, "g_kin", 2, [128, 2, GB], BF16)
            qin_r = Ring(es, "g_qin", 2, [128, 2, GB], BF16)
            qint_r = Ring(es, "g_qint", 4, [128, 2, GB], BF16)
            refv = es.enter_context(sbt("g_ref", [128, 2, 4], F32))
            lastv = es.enter_context(sbt("g_last", [128, 2, 4], F32))
            er = es.enter_context(sbt("g_er", [128, 2, 4], F32))
            elr_r = Ring(es, "g_elr", 2, [128, 2, 4], F32)
            el_r = Ring(es, "g_el", 3, [128, 2, 4], F32)
            ktm_r = Ring(es, "g_ktm", 2, [64, 4, 256], BF16)
            sT_r = Ring(es, "g_sT", 3, [64, 4, 64], BF16)
            tkv_r = Ring(es, "g_tkv", 2, [128, 4, 2, 512], F32)
            sball = [es.enter_context(sbt(f"g_Sb{i}", [128, 4, 2, 512], BF16)) for i in range(3)]
            osb_r = Ring(es, "g_o", 2, [128, 4, GB], F32)
            sq = es.enter_context(sbt("g_sq", [128, 4, GB], BF16))
            rs = es.enter_context(sbt("g_rs", [128, GB], F32))
            og_r = Ring(es, "g_og", 2, [128, 4, GB], BF16)
            KVB = ((2, 3), (5, 6))
            OB = (4, 4)
            LN16 = math.log(1.0 / 16.0)
            lat = [(CTX + GB * i, GB) for i in range(SEQ // GB)]
            for h in range(4):
                qv = qT[h * 256:(h + 1) * 256, :].rearrange("(dc p) t -> p dc t", p=128)
                kv_ = kT[h * 256:(h + 1) * 256, :].rearrange("(dc p) t -> p dc t", p=128)
                ov = oT[h * 512:(h + 1) * 512, :].rearrange("(vc p) t -> p vc t", p=128)
                gv = gT[h * 512:(h + 1) * 512, :].rearrange("(vc p) t -> p vc t", p=128)
                ogv = ogT[h * 512:(h + 1) * 512, :].rearrange("(vc p) t -> p vc t", p=128)
                for dr in (0, 1):
                    blocks = [(0, CTX)] + (lat if dr == 0 else lat[::-1])
                    ri, li = (31, 63) if dr == 0 else (32, 0)
                    P.op("dve", "memset", Xs[0][:], 0.0, writes=["g_S0"])
                    P.op("dve", "memset", sball[0][:, 0, :, :], 0.0, writes=["g_Sb0"])
                    state = dict(si=0)
                    ctxs = {}

                    def s1(n, h=h, dr=dr, blocks=blocks, ri=ri, li=li, qv=qv, kv_=kv_, ov=ov, gv=gv):
                        t0, tw = blocks[n]
                        nch = tw // 64
                        par = n % 2
                        c = dict(t0=t0, tw=tw, nch=nch, par=par)
                        ctxs[n] = c
                        qb, qn = q_r.next()
                        kb, kn = k_r.next()
                        ab, an = a_r.next()
                        vb, vn = v_r.next()
                        c.update(vb=vb, vn=vn)
                        P.dma("sp", qb[:, :, 0:tw], qv[:, :, t0:t0 + tw], reads=["D:qT"], writes=[qn])
                        yield
                        P.dma("sp", kb[:, :, 0:tw], kv_[:, :, t0:t0 + tw], reads=["D:kT"], writes=[kn])
                        yield
                        P.dma("sp", ab[:, 0:tw], alrT[dr * 16:(dr + 1) * 16, t0:t0 + tw], reads=["D:alrT"], writes=[an])
                        yield
                        P.dma("sp", vb[:, 0:nch, :],
                              vtm[t0:t0 + tw, h * 512:(h + 1) * 512].rearrange("(c p) n -> p c n", p=64),
                              reads=["D:vtm"], writes=[vn])
                        yield
                        if dr == 1:
                            ofb, ofn = of_r.next()
                            ggb, ggn = gg_r.next()
                            c.update(ofb=ofb, ofn=ofn, ggb=ggb, ggn=ggn)
                            P.dma("sp", ofb[:, :, 0:tw], ov[:, :, t0:t0 + tw], reads=["D:oT"], writes=[ofn])
                            yield
                            P.dma("sp", ggb[:, :, 0:tw], gv[:, :, t0:t0 + tw], reads=["D:gT"], writes=[ggn])
                            yield
                        for dc in range(2):
                            col = (h * 2 + dc) * 128
                            P.op("pe", "matmul", psf[0][:, dc * 256:dc * 256 + tw],
                                 wdu_sb[0:16, dr * 1024 + col:dr * 1024 + col + 128], ab[0:16, 0:tw],
                                 start=True, stop=True, reads=["wdu", an], writes=["psz"])
                            yield
                            bi = dr * 8 + h * 2 + dc
                            P.op("act", "activation", out=la[:, dc, 0:tw], in_=psf[0][:, dc * 256:dc * 256 + tw], func=AF.Exp,
                                 bias=negb[:, bi:bi + 1], scale=-1.0, reads=["psz", "negb"], writes=["g_la"])
                            yield
                        P.op("act", "activation", out=la[:, :, 0:tw], in_=la[:, :, 0:tw], func=AF.Ln, bias=1.0,
                             reads=["g_la"], writes=["g_la"])
                        yield
                        for dc in range(2):
                            P.op("dve", "tensor_tensor_scan", cum[:, dc, 0:tw], rmask[:, 0:tw], la[:, dc, 0:tw], 0.0, ALU.mult, ALU.add,
                                 reads=["g_la", "rmask"], writes=["g_cum"])
                            yield

                        def cv(tl):
                            return tl[:, :, 0:tw].rearrange("p d (c i) -> p d c i", i=64)

                        def bc(sm):
                            return sm[:, :, 0:nch].unsqueeze(3).to_broadcast([128, 2, nch, 64])
                        if dr == 1:
                            P.op("dve", "tensor_copy", lastv[:, :, 0:nch], cv(cum)[:, :, :, 63], reads=["g_cum"], writes=["g_last"])
                            yield
                            P.op("dve", "tensor_tensor", cum[:, :, 0:tw], la[:, :, 0:tw], cum[:, :, 0:tw], ALU.subtract,
                                 reads=["g_la", "g_cum"], writes=["g_cum"])
                            yield
                            P.op("dve", "tensor_tensor", cv(cum), cv(cum), bc(lastv), ALU.add, reads=["g_cum", "g_last"], writes=["g_cum"])
                            yield
                        elb, eln = el_r.next()
                        kin, kinn = kin_r.next()
                        elr, elrn = elr_r.next()
                        c.update(elb=elb, eln=eln, kin=kin, kinn=kinn, elr=elr, elrn=elrn)
                        P.op("dve", "tensor_copy", refv[:, :, 0:nch], cv(cum)[:, :, :, ri], reads=["g_cum"], writes=["g_ref"])
                        yield
                        P.op("dve", "tensor_copy", lastv[:, :, 0:nch], cv(cum)[:, :, :, li], reads=["g_cum"], writes=["g_last"])
                        yield
                        P.op("act", "activation", out=Aa[:, :, 0:tw], in_=cum[:, :, 0:tw], func=AF.Exp, bias=ln16[:, 0:1], scale=-1.0 / 16.0,
                             reads=["g_cum", "ln16"], writes=["g_A"])
                        yield
                        qinb, qinn = qin_r.next()
                        qintb, qintn = qint_r.next()
                        c.update(qintb=qintb, qintn=qintn)
                        P.op("dve", "tensor_tensor", qintb[:, :, 0:tw], qb[:, :, 0:tw], Aa[:, :, 0:tw], ALU.mult, reads=[qn, "g_A"], writes=[qintn])
                        yield
                        P.op("dve", "tensor_tensor", cv(cum), cv(cum), bc(refv), ALU.subtract, reads=["g_cum", "g_ref"], writes=["g_cum"])
                        yield
                        P.op("dve", "tensor_tensor", elr[:, :, 0:nch], lastv[:, :, 0:nch], refv[:, :, 0:nch], ALU.subtract,
                             reads=["g_last", "g_ref"], writes=[elrn])
                        yield
                        P.op("act", "activation", out=Aa[:, :, 0:tw], in_=cum[:, :, 0:tw], func=AF.Exp, bias=ln16[:, 0:1], scale=-1.0 / 16.0,
                             reads=["g_cum", "ln16", qintn], writes=["g_A"])
                        yield
                        P.op("act", "activation", out=Bb[:, :, 0:tw], in_=cum[:, :, 0:tw], func=AF.Exp, scale=1.0 / 16.0,
                             reads=["g_cum"], writes=["g_B"])
                        yield
                        P.op("act", "activation", out=elb[:, :, 0:nch], in_=lastv[:, :, 0:nch], func=AF.Exp, scale=-1.0 / 16.0,
                             reads=["g_last"], writes=[eln])
                        yield
                        P.op("act", "activation", out=elr[:, :, 0:nch], in_=elr[:, :, 0:nch], func=AF.Exp, scale=-1.0 / 16.0,
                             reads=[elrn], writes=[elrn])
                        yield
                        P.op("dve", "tensor_tensor", kin[:, :, 0:tw], kb[:, :, 0:tw], Bb[:, :, 0:tw], ALU.mult, reads=[kn, "g_B"], writes=[kinn])
                        yield
                        P.op("dve", "tensor_tensor", qinb[:, :, 0:tw], qb[:, :, 0:tw], Aa[:, :, 0:tw], ALU.mult, reads=[qn, "g_A"], writes=[qinn])
                        yield
                        order = list(range(nch)) if dr == 0 else list(range(nch))[::-1]
                        c.update(order=order, qinb=qinb, qinn=qinn)

                    def s1b(n, dr=dr):
                        c = ctxs[n]
                        nch, par, vb, vn, order = c["nch"], c["par"], c["vb"], c["vn"], c["order"]
                        qinb, qinn = c["qinb"], c["qinn"]
                        kin, kinn, elr, elrn = c["kin"], c["kinn"], c["elr"], c["elrn"]
                        ktm, ktn = ktm_r.next()
                        sT, sTn = sT_r.next()
                        tkv, tkn = tkv_r.next()
                        c.update(sT=sT, sTn=sTn, tkv=tkv, tkn=tkn)
                        for kk, ch in enumerate(order):
                            cs = slice(ch * 64, (ch + 1) * 64)
                            for dc in range(2):
                                P.op("pe", "transpose", psb[0:64, kk * 256 + dc * 128:kk * 256 + (dc + 1) * 128], kin[:, dc, cs], ident[:, :],
                                     reads=[kinn, "ident"], writes=["psb"], inc=(kk == nch - 1 and dc == 1))
                        P.op("act", "activation", out=ktm[:, 0:nch, :].rearrange("p a b -> p (a b)"), in_=psb[0:64, 0:nch * 256], func=AF.Copy,
                             reads=["psb"], writes=[ktn])
                        yield
                        sreg = "pss"
                        for kk, ch in enumerate(order):
                            cs = slice(ch * 64, (ch + 1) * 64)
                            for dc in range(2):
                                P.op("pe", "matmul", psf[1][0:64, kk * 64:(kk + 1) * 64], kin[:, dc, cs], qinb[:, dc, cs],
                                     start=(dc == 0), stop=(dc == 1), reads=[kinn, qinn], writes=[sreg],
                                     inc=(kk == nch - 1 and dc == 1))
                        P.op("dve", "tensor_tensor", sT[:, 0:nch, :],
                             psf[1][0:64, 0:nch * 64].rearrange("p (a b) -> p a b", b=64),
                             masks[:, dr * 64:(dr + 1) * 64].unsqueeze(1).to_broadcast([64, nch, 64]), ALU.mult,
                             reads=[sreg, "masks"], writes=[sTn])
                        yield
                        for kk, ch in enumerate(order):
                            for dc in range(2):
                                b = KVB[kk % 2][dc]
                                P.op("pe", "matmul", psf[b][:, 0:512], ktm[0:64, kk, dc * 128:(dc + 1) * 128], vb[0:64, ch, :],
                                     start=True, stop=True, reads=[ktn, vn], writes=[PSF[b]])
                                yield
                                if dc == 0:
                                    P.op("act", "activation", out=tkv[:, kk, dc, :], in_=psf[b][:, 0:512], func=AF.Copy,
                                         scale=elr[:, dc, ch:ch + 1], reads=[PSF[b], elrn], writes=[tkn])
                                    yield
                                else:
                                    P.op("dve", "tensor_scalar", tkv[:, kk, dc, :], psf[b][:, 0:512], elr[:, dc, ch:ch + 1], None, ALU.mult,
                                         reads=[PSF[b], elrn], writes=[tkn])
                                    yield

                    def s2(n):
                        c = ctxs[n]
                        nch = c["nch"]
                        sbi = n % 3
                        for kk, ch in enumerate(c["order"]):
                            i = state["si"]
                            src, dst = Xs[i], Xs[1 - i]
                            for dc in range(2):
                                P.op("dve", "scalar_tensor_tensor", dst[:, dc, :], src[:, dc, :], c["elb"][:, dc, ch:ch + 1],
                                     c["tkv"][:, kk, dc, :], ALU.mult, ALU.add,
                                     reads=[f"g_S{i}", c["eln"], c["tkn"]], writes=[f"g_S{1 - i}"])
                            if kk < nch - 1:
                                tp, tk = sbi, kk + 1
                            else:
                                tp, tk = (sbi + 1) % 3, 0
                            P.op("act", "activation", out=sball[tp][:, tk, :, :], in_=dst[:, :, :], func=AF.Copy,
                                 reads=[f"g_S{1 - i}"], writes=[f"g_Sb{tp}"])
                            state["si"] = 1 - i
                            yield

                    def s3(n, h=h, dr=dr, ov=ov, ogv=ogv):
                        c = ctxs.pop(n)
                        par, nch, t0, tw = c["par"], c["nch"], c["t0"], c["tw"]
                        sbi = n % 3
                        vb, vn, sT, sTn, qintb, qintn = c["vb"], c["vn"], c["sT"], c["sTn"], c["qintb"], c["qintn"]
                        ob, on = osb_r.next()
                        for kk, ch in enumerate(c["order"]):
                            cs = slice(ch * 64, (ch + 1) * 64)
                            r0 = 0
                            ob_ = OB[kk % 2]
                            oreg = PSF[ob_]
                            for vc in range(4):
                                P.op("pe", "matmul", psf[ob_][:, r0 + vc * 64:r0 + (vc + 1) * 64], vb[0:64, ch, vc * 128:(vc + 1) * 128],
                                     sT[0:64, kk, :], start=True, stop=False, reads=[vn, sTn], writes=[oreg], inc=False)
                                for dc in range(2):
                                    P.op("pe", "matmul", psf[ob_][:, r0 + vc * 64:r0 + (vc + 1) * 64],
                                         sball[sbi][:, kk, dc, vc * 128:(vc + 1) * 128], qintb[:, dc, cs],
                                         start=False, stop=(dc == 1), reads=[f"g_Sb{sbi}", qintn], writes=[oreg],
                                         inc=(vc == 3 and dc == 1))
                            pov = psf[ob_][:, r0:r0 + 256].rearrange("p (v i) -> p v i", i=64)
                            if dr == 0:
                                P.op("act", "activation", out=ob[:, :, cs], in_=pov, func=AF.Copy, reads=[oreg], writes=[on])
                                yield
                            else:
                                P.op("dve", "tensor_tensor", ob[:, :, cs], pov, c["ofb"][:, :, cs], ALU.add, reads=[oreg, c["ofn"]], writes=[on])
                                yield
                        if dr == 0:
                            P.dma("sp", ov[:, :, t0:t0 + tw], ob[:, :, 0:tw], reads=[on], writes=["D:oT"])
                            yield
                        else:
                            ggb, ggn = c["ggb"], c["ggn"]
                            ogb, ogn = og_r.next()
                            rreg = PSF[4]
                            P.op("act", "activation", out=sq[:, :, 0:tw], in_=ob[:, :, 0:tw], func=AF.Square, reads=[on], writes=["g_sq"])
                            yield
                            for vc in range(4):
                                P.op("pe", "matmul", psf[4][:, 0:tw], ones_bf[:, :], sq[:, vc, 0:tw],
                                     start=(vc == 0), stop=(vc == 3), reads=["ones_bf", "g_sq"], writes=[rreg], inc=(vc == 3))
                            P.op("act", "activation", out=rs[:, 0:tw], in_=psf[4][:, 0:tw], func=AF.Ln,
                                 bias=epsr[:, 0:1], scale=1.0 / 512.0, reads=[rreg, "epsr"], writes=["g_rs"])
                            yield
                            P.op("act", "activation", out=rs[:, 0:tw], in_=rs[:, 0:tw], func=AF.Exp, scale=-0.5,
                                 reads=["g_rs"], writes=["g_rs"])
                            yield
                            P.op("act", "activation", out=ggb[:, :, 0:tw], in_=ggb[:, :, 0:tw], func=AF.Silu, reads=[ggn], writes=[ggn])
                            yield
                            for vc in range(4):
                                gi = LV_GAIN + h * 4 + vc
                                P.op("dve", "scalar_tensor_tensor", ob[:, vc, 0:tw], ob[:, vc, 0:tw], lv[:, gi:gi + 1], rs[:, 0:tw],
                                     ALU.mult, ALU.mult, reads=[on, "lv", "g_rs"], writes=[on])
                                yield
                            P.op("dve", "tensor_tensor", ogb[:, :, 0:tw], ob[:, :, 0:tw], ggb[:, :, 0:tw], ALU.mult,
                                 reads=[on, ggn], writes=[ogn])
                            yield
                            P.dma("sp", ogv[:, :, t0:t0 + tw], ogb[:, :, 0:tw], reads=[ogn], writes=["D:ogT"])
                            yield

                    def rr(*gens):
                        gens = [g for g in gens if g is not None]
                        while gens:
                            for g in list(gens):
                                try:
                                    next(g)
                                except StopIteration:
                                    gens.remove(g)

                    rr(s1(0))
                    NBk = len(blocks)
                    for n in range(NBk):
                        rr(s1b(n),
                           s1(n + 1) if n + 1 < NBk else None,
                           s3(n - 2) if n >= 2 else None,
                           s2(n - 1) if n >= 1 else None)
                    rr(s2(NBk - 1), s3(NBk - 2))
                    rr(s3(NBk - 1))
        P.barrier()

        if STOP == "gla":
            break
        with ExitStack() as es:
            p_r = Ring(es, "pl_p", 2, [128, T], F32)
            d_r = Ring(es, "pl_d", 2, [128, T], BF16)
            xp = es.enter_context(sbt("pl_xp", [128, 80, 80], F32))
            ba = es.enter_context(sbt("pl_a", [128, 6400], F32))
            bb = es.enter_context(sbt("pl_b", [128, 6400], F32))
            yp = es.enter_context(sbt("pl_yp", [128, 80, 64], F32))
            cp = es.enter_context(sbt("pl_cp", [128, 272], F32))
            P.op("dve", "memset", xp[:], 0.0, writes=["pl_xp"])
            P.op("dve", "memset", yp[:], 0.0, writes=["pl_yp"])
            P.op("dve", "memset", cp[:], 0.0, writes=["pl_cp"])
            for pc in range(8):
                wi = pc // 2
                w = WINS[wi]
                m = int(math.log2(w))
                lo = w // 2
                if pc == 0:
                    pl_loaded = {}
                    pb0, pn0 = p_r.next()
                    P.dma("sp", pb0[:], pT[0:128, :], reads=["D:pT"], writes=[pn0])
                    pl_loaded[0] = (pb0, pn0)
                pb, pn = pl_loaded.pop(pc)
                db, dn = d_r.next()
                if pc + 1 < 8:
                    pb1, pn1 = p_r.next()
                    P.dma("sp", pb1[:], pT[(pc + 1) * 128:(pc + 2) * 128, :], reads=["D:pT"], writes=[pn1])
                    pl_loaded[pc + 1] = (pb1, pn1)
                pimg = pb[:, CTX:T].rearrange("p (r c) -> p r c", c=64)
                P.op("act", "activation", out=xp[:, 8:72, 8:72], in_=pimg, func=AF.Copy,
                     reads=[pn], writes=["pl_xp"])
                a80 = ba[:].rearrange("p (r c) -> p r c", c=80)
                b80 = bb[:].rearrange("p (r c) -> p r c", c=80)
                cur, curn, width = xp[:], "pl_xp", 80
                for s in range(m):
                    sh = 2 ** s
                    nw_ = width - sh
                    nxt, nxtn = (a80, "pl_a") if s % 2 == 0 else (b80, "pl_b")
                    P.op("dve", "tensor_tensor",
                        nxt[:, 8:72, 0:nw_], cur[:, 8:72, 0:nw_], cur[:, 8:72, sh:sh + nw_], ALU.add,
                        reads=[curn], writes=[nxtn])
                    cur, curn, width = nxt, nxtn, nw_
                icol = invc[:, wi * 64:(wi + 1) * 64].unsqueeze(1).to_broadcast([128, 64, 64])
                P.op("dve", "tensor_tensor",
                    yp[:, 8:72, :], cur[:, 8:72, 8 - lo:8 - lo + 64], icol, ALU.mult, reads=[curn, "invc"], writes=["pl_yp"])
                a64 = ba[:, 0:5120].rearrange("p (r c) -> p r c", c=64)
                b64 = bb[:, 0:5120].rearrange("p (r c) -> p r c", c=64)
                cur, curn, height = yp[:], "pl_yp", 80
                for s in range(m):
                    sh = 2 ** s
                    nh = height - sh
                    nxt, nxtn = (a64, "pl_a") if s % 2 == 0 else (b64, "pl_b")
                    P.op("dve", "tensor_tensor",
                        nxt[:, 0:nh, :], cur[:, 0:nh, :], cur[:, sh:sh + nh, :], ALU.add, reads=[curn], writes=[nxtn])
                    cur, curn, height = nxt, nxtn, nh
                irow = invc[:, 256 + wi * 64:256 + (wi + 1) * 64].unsqueeze(2).to_broadcast([128, 64, 64])
                mean_n = "pl_b" if curn == "pl_a" else "pl_a"
                mean_t = b64 if curn == "pl_a" else a64
                P.op("dve", "tensor_tensor",
                    mean_t[:, 0:64, :], cur[:, 8 - lo:8 - lo + 64, :], irow, ALU.mult, reads=[curn, "invc"], writes=[mean_n])
                dimg = db[:, CTX:T].rearrange("p (r c) -> p r c", c=64)
                P.op("dve", "tensor_tensor",
                    dimg, mean_t[:, 0:64, :], pimg, ALU.subtract, reads=[mean_n, pn], writes=[dn])
                P.op("act", "activation", out=cp[:, 8:264], in_=pb[:, 0:CTX], func=AF.Copy,
                     reads=[pn], writes=["pl_cp"])
                cur, curn, width = cp[:], "pl_cp", 272
                for s in range(m):
                    sh = 2 ** s
                    nw_ = width - sh
                    nxt, nxtn = (ba, "pl_a") if s % 2 == 0 else (bb, "pl_b")
                    P.op("dve", "tensor_tensor",
                        nxt[:, 0:nw_], cur[:, 0:nw_], cur[:, sh:sh + nw_], ALU.add, reads=[curn], writes=[nxtn])
                    cur, curn, width = nxt[:, :], nxtn, nw_
                mean_n = "pl_b" if curn == "pl_a" else "pl_a"
                mean_c = bb if curn == "pl_a" else ba
                P.op("dve", "tensor_tensor",
                    mean_c[:, 0:CTX], cur[:, 8 - lo:8 - lo + CTX], invc[:, 512 + wi * 256:512 + (wi + 1) * 256], ALU.mult,
                    reads=[curn, "invc"], writes=[mean_n])
                P.op("dve", "tensor_tensor",
                    db[:, 0:CTX], mean_c[:, 0:CTX], pb[:, 0:CTX], ALU.subtract, reads=[mean_n, pn], writes=[dn])
                P.dma("sp", dT[pc * 128:(pc + 1) * 128, :], db[:], reads=[dn], writes=["D:dT"])
        P.barrier()

        with ExitStack() as es:
            r16 = Ring(es, "pg_s", 4, [128, 512], BF16)

            def epi_pg(nb, t0, tw, pss):
                st, sn = r16.next()
                P.op("act", "activation", out=st[:, 0:tw], in_=psf[pss[0]][:, 0:tw], func=AF.Copy,
                                                   scale=lv[:, LV_PSC + nb:LV_PSC + nb + 1],
                     reads=[PSF[pss[0]], "lv"], writes=[sn])
                P.dma("sp", eT[nb * 128:(nb + 1) * 128, t0:t0 + tw], st[:, 0:tw], reads=[sn], writes=["D:eT"])

            linear("lin_pg", dT, "dT", 8, wpg, l * 8, 8, 1, 4352, None, epi_pg, kcs=lambda nb: [2 * (nb // 2), 2 * (nb // 2) + 1])

        with ExitStack() as es:
            rb = Ring(es, "go_b", 6, [128, 512], F32)
            rsg = Ring(es, "go_s", 4, [128, 512], F32)
            pend = {}

            def pre_go(nb, t0, tw):
                bt, bn = rb.next()
                P.dma("sp", bt[:, 0:tw], bgT[nb * 128:(nb + 1) * 128, t0:t0 + tw], reads=["D:bgT"], writes=[bn])
                pend[(nb, t0)] = (bt, bn)

            def epi_go(nb, t0, tw, pss):
                bt, bn = pend.pop((nb, t0))
                st, sn = rsg.next()
                P.op("act", "activation", out=bt[:, 0:tw], in_=bt[:, 0:tw], func=AF.Sigmoid, reads=[bn], writes=[bn])
                P.op("dve", "tensor_tensor", st[:, 0:tw], psf[pss[0]][:, 0:tw], bt[:, 0:tw], ALU.mult,
                     reads=[PSF[pss[0]], bn], writes=[sn])
                P.dma("sp", t1T[nb * 128:(nb + 1) * 128, t0:t0 + tw], st[:, 0:tw], reads=[sn], writes=["D:t1T"])

            linear("lin_go", ogT, "ogT", 16, wgo, l * 16, 16, 1, 2304, pre_go, epi_go)

        with ExitStack() as es:
            rb = Ring(es, "po_b", 6, [128, 512], F32)
            rt = Ring(es, "po_t", 6, [128, 512], F32)
            rsg = Ring(es, "po_s", 4, [128, 512], BF16)
            pend = {}

            def pre_po(nb, t0, tw):
                bt, bn = rb.next()
                tt, tn = rt.next()
                P.dma("sp", bt[:, 0:tw], bgT[D + nb * 128:D + (nb + 1) * 128, t0:t0 + tw], reads=["D:bgT"], writes=[bn])
                P.dma("sp", tt[:, 0:tw], t1T[nb * 128:(nb + 1) * 128, t0:t0 + tw], reads=["D:t1T"], writes=[tn])
                pend[(nb, t0)] = (bt, bn, tt, tn)

            def epi_po(nb, t0, tw, pss):
                bt, bn, tt, tn = pend.pop((nb, t0))
                st, sn = rsg.next()
                P.op("act", "activation", out=bt[:, 0:tw], in_=bt[:, 0:tw], func=AF.Sigmoid, reads=[bn], writes=[bn])
                P.op("dve", "tensor_tensor", bt[:, 0:tw], psf[pss[0]][:, 0:tw], bt[:, 0:tw], ALU.mult,
                     reads=[PSF[pss[0]], bn], writes=[bn])
                P.op("dve", "tensor_tensor", st[:, 0:tw], bt[:, 0:tw], tt[:, 0:tw], ALU.add,
                     reads=[bn, tn], writes=[sn])
                P.dma("sp", mT[nb * 128:(nb + 1) * 128, t0:t0 + tw], st[:, 0:tw], reads=[sn], writes=["D:mT"])

            linear("lin_po", eT, "eT", 8, wpo, l * 16, 16, 1, 4352, pre_po, epi_po)

        def resid(es, name, m_gt, xsrc_, xsn_):
            rx = Ring(es, name + "_x", 6, [128, 512], F32)
            rt = Ring(es, name + "_t", 4, [128, 512], F32)
            pend = {}

            def pre(nb, t0, tw):
                xt, xn = rx.next()
                P.dma("sp", xt[:, 0:tw], xsrc_[nb * 128:(nb + 1) * 128, t0:t0 + tw], reads=["D:" + xsn_], writes=[xn])
                pend[(nb, t0)] = (xt, xn)

            def epi(nb, t0, tw, pss):
                xt, xn = pend.pop((nb, t0))
                tt, tn = rt.next()
                col = col_of(t0)
                P.op("act", "activation", out=tt[:, 0:tw], in_=psf[pss[0]][:, 0:tw], func=AF.Copy,
                                                   scale=mods[:, m_gt * 16 + nb, col:col + 1],
                     reads=[PSF[pss[0]], "mods"], writes=[tn])
                P.op("dve", "scalar_tensor_tensor", xt[:, 0:tw], xt[:, 0:tw], float(ALPHA), tt[:, 0:tw], ALU.mult, ALU.add,
                     reads=[xn, tn], writes=[xn])
                P.dma("sp", uT[nb * 128:(nb + 1) * 128, t0:t0 + tw], xt[:, 0:tw], reads=[xn], writes=["D:uT"])
            return pre, epi

        def layer_norm(g_off, b_off, mod=None):
            with ExitStack() as es:
                ur = Ring(es, "ln_u", 3, [128, 16, 512], F32)
                sqr = Ring(es, "ln_sq", 2, [128, 16, 512], BF16)
                mr = Ring(es, "ln_mean", 2, [128, 512], F32)
                qr = Ring(es, "ln_msq", 2, [128, 512], F32)
                rr_ = Ring(es, "ln_rstd", 2, [128, 512], F32)
                uv = uT.rearrange("(c p) t -> p c t", p=128)
                xv = xT.rearrange("(c p) t -> p c t", p=128)
                hv = hT.rearrange("(c p) t -> p c t", p=128)
                loaded = {}

                def lload(i):
                    if i >= len(SUBS) or i in loaded:
                        return
                    t0_, tw_ = SUBS[i]
                    ub_, un_ = ur.next()
                    P.dma("sp", ub_[:, :, 0:tw_], uv[:, :, t0_:t0_ + tw_], reads=["D:uT"], writes=[un_])
                    loaded[i] = (ub_, un_)

                def ln_gen(si):
                    t0, tw = SUBS[si]
                    ub, un = loaded.pop(si)
                    sqt, sqn = sqr.next()
                    mean, mn = mr.next()
                    msq, qn_ = qr.next()
                    rstd, rn = rr_.next()
                    pa, pb = (0, 1) if si % 2 == 0 else (2, 3)
                    P.op("act", "activation", out=sqt[:, :, 0:tw], in_=ub[:, :, 0:tw], func=AF.Square, reads=[un], writes=[sqn])
                    yield
                    for c in range(16):
                        P.op("pe", "matmul", psf[pa][:, 0:tw], ones_f[:, :], ub[:, c, 0:tw], start=(c == 0), stop=(c == 15),
                             reads=["ones_f", un], writes=[PSF[pa]], inc=(c == 15))
                    yield
                    for c in range(16):
                        P.op("pe", "matmul", psf[pb][:, 0:tw], ones_bf[:, :], sqt[:, c, 0:tw], start=(c == 0), stop=(c == 15),
                             reads=["ones_bf", sqn], writes=[PSF[pb]], inc=(c == 15))
                    yield
                    P.op("act", "activation", out=mean[:, 0:tw], in_=psf[pa][:, 0:tw], func=AF.Copy, scale=1.0 / D,
                         reads=[PSF[pa]], writes=[mn])
                    yield
                    P.op("dve", "tensor_tensor", msq[:, 0:tw], mean[:, 0:tw], mean[:, 0:tw], ALU.mult, reads=[mn], writes=[qn_])
                    P.op("dve", "scalar_tensor_tensor", rstd[:, 0:tw], psf[pb][:, 0:tw], 1.0 / D, msq[:, 0:tw], ALU.mult, ALU.subtract,
                         reads=[PSF[pb], qn_], writes=[rn])
                    yield
                    P.op("act", "activation", out=rstd[:, 0:tw], in_=rstd[:, 0:tw], func=AF.Ln, bias=epsl[:, 0:1],
                         reads=[rn, "epsl"], writes=[rn])
                    P.op("act", "activation", out=rstd[:, 0:tw], in_=rstd[:, 0:tw], func=AF.Exp, scale=-0.5, reads=[rn], writes=[rn])
                    yield
                    mb = mean[:, 0:tw].unsqueeze(1).to_broadcast([128, 16, tw])
                    rb_ = rstd[:, 0:tw].unsqueeze(1).to_broadcast([128, 16, tw])
                    P.op("dve", "tensor_tensor", ub[:, :, 0:tw], ub[:, :, 0:tw], mb, ALU.subtract, reads=[un, mn], writes=[un])
                    yield
                    P.op("dve", "tensor_tensor", ub[:, :, 0:tw], ub[:, :, 0:tw], rb_, ALU.mult, reads=[un, rn], writes=[un])
                    yield
                    for c in range(16):
                        P.op("act", "activation", out=ub[:, c, 0:tw], in_=ub[:, c, 0:tw], func=AF.Identity,
                             bias=lv[:, b_off + c:b_off + c + 1], scale=lv[:, g_off + c:g_off + c + 1],
                             reads=[un, "lv"], writes=[un])
                        if c % 4 == 3:
                            yield
                    P.dma("sp", xv[:, :, t0:t0 + tw], ub[:, :, 0:tw], reads=[un], writes=["D:xT"])
                    if mod is not None:
                        m_sh, m_sc = mod
                        col = col_of(t0)
                        for c in range(16):
                            sc_ap = ops1[:, m_sc * 16 + c, col:col + 1]
                            sh_ap = mods[:, m_sh * 16 + c, col:col + 1]
                            if c % 2 == 0:
                                P.op("dve", "tensor_scalar", sqt[:, c, 0:tw], ub[:, c, 0:tw], sc_ap, sh_ap, ALU.mult, ALU.add,
                                     reads=[un, "mods", "ops1"], writes=[sqn])
                            else:
                                P.op("act", "activation", out=sqt[:, c, 0:tw], in_=ub[:, c, 0:tw], func=AF.Identity, bias=sh_ap, scale=sc_ap,
                                     reads=[un, "mods", "ops1"], writes=[sqn])
                            if c % 4 == 3:
                                yield
                        P.dma("sp", hv[:, :, t0:t0 + tw], sqt[:, :, 0:tw], reads=[sqn], writes=["D:hT"])
                    yield
                    lload(si + 2)

                def rr2(*gens):
                    gens = list(gens)
                    while gens:
                        for g in list(gens):
                            try:
                                next(g)
                            except StopIteration:
                                gens.remove(g)

                lload(0)
                lload(1)
                for p0 in range(0, len(SUBS), 2):
                    gs = [ln_gen(p0)]
                    if p0 + 1 < len(SUBS):
                        gs.append(ln_gen(p0 + 1))
                    rr2(*gs)
            P.barrier()

        with ExitStack() as es:
            pre, epi = resid(es, "ro", 2, xsrc, xsn)
            linear("lin_o", mT, "mT", 16, wo, l * 16, 16, 1, 2304, pre, epi)
        layer_norm(LV_LMG, LV_LMB, mod=(3, 4))

        with ExitStack() as es:
            rs_ = Ring(es, "fi_s", 4, [128, 512], F32)
            rh = Ring(es, "fi_h", 4, [128, 512], BF16)

            def epi_fi(nb, t0, tw, pss):
                st, sn = rs_.next()
                ht, hn = rh.next()
                P.op("act", "activation", out=st[:, 0:tw], in_=psf[pss[0]][:, 0:tw], func=AF.Silu,
                     reads=[PSF[pss[0]]], writes=[sn])
                P.op("dve", "tensor_tensor", ht[:, 0:tw], st[:, 0:tw], psf[pss[1]][:, 0:tw], ALU.mult,
                     reads=[sn, PSF[pss[1]]], writes=[hn])
                P.dma("sp", hidT[nb * 128:(nb + 1) * 128, t0:t0 + tw], ht[:, 0:tw], reads=[hn], writes=["D:hidT"])

            linear("lin_fi", hT, "hT", 16, wfi, l * 44, 44, 2, 2304, None, epi_fi)
        with ExitStack() as es:
            pre, epi = resid(es, "rf", 5, xT, "xT")
            linear("lin_fo", hidT, "hidT", 44, wfo, l * 16, 16, 1, 1024, pre, epi)
        layer_norm(LV_LFG, LV_LFB)

    for r0 in range(0, D, 128):
        P.dma("sp", yT[r0:r0 + 128, :], xT[r0:r0 + 128, CTX:T], reads=["D:xT"], writes=["D:yT"])
    P.barrier()
    P.build()
    return nc


def _blocks(W):
    Kd, N = W.shape
    return np.ascontiguousarray(W.reshape(Kd // 128, 128, N // 128, 128).transpose(2, 1, 0, 3)).reshape(
        N // 128, 128, (Kd // 128) * 128)


def _consts():
    ident = np.eye(128, dtype=np.float32)
    j = np.arange(64)[:, None]
    i = np.arange(64)[None, :]
    masks = np.concatenate([(i >= j), (i <= j)], axis=1).astype(np.float32)
    rmask = np.ones((128, 512), np.float32)
    rmask[:, ::64] = 0.0
    def inv(n, w):
        lo = w // 2
        hi = w - lo - 1
        idx = np.arange(n)
        return (1.0 / (np.minimum(idx + hi + 1, n) - np.maximum(idx - lo, 0))).astype(np.float32)
    row = np.concatenate([inv(64, w) for w in WINS] + [inv(64, w) for w in WINS] + [inv(CTX, w) for w in WINS])
    invc = np.ascontiguousarray(np.broadcast_to(row[None, :], (128, 1536))).astype(np.float32)
    return dict(ident=ident, masks=masks, rmask=rmask, invc=invc)


def _prep_weights(inp, L):
    f = lambda a: np.asarray(a, dtype=np.float32)
    out = {}
    wada, lvec, wdu, win_fm, win_v, wpg, wgo, wpo, wo, wfi, wfo = ([] for _ in range(11))
    for l in range(L):
        wada.append(_blocks(f(inp["w_ada"][l])))
        lv = np.zeros((128, LV_N), np.float32)
        lv[:, LV_BADA:LV_BADA + 96] = f(inp["b_ada"][l]).reshape(96, 128).T
        lv[:, LV_GAIN:LV_GAIN + 16] = f(inp["gla_norm_gain"][l]).reshape(16, 128).T
        lv[:, LV_PSC:LV_PSC + 8] = f(inp["pool_scale"][l]).reshape(8, 128).T
        lv[:, LV_LMG:LV_LMG + 16] = f(inp["ln_mix_gain"][l]).reshape(16, 128).T
        lv[:, LV_LMB:LV_LMB + 16] = f(inp["ln_mix_bias"][l]).reshape(16, 128).T
        lv[:, LV_LFG:LV_LFG + 16] = f(inp["ln_ffn_gain"][l]).reshape(16, 128).T
        lv[:, LV_LFB:LV_LFB + 16] = f(inp["ln_ffn_bias"][l]).reshape(16, 128).T
        lv[:, LV_BDEC:LV_BDEC + 16] = f(inp["b_decay_up"][l]).reshape(16, 128).T
        lvec.append(lv)
        wdu.append(f(inp["w_decay_up"][l]).transpose(1, 0, 2).reshape(16, 2048))
        W = f(inp["w_in"][l])
        alr = np.zeros((D, 128), np.float32)
        alr[:, :32] = W[:, 6144:6176]
        Wsel = np.concatenate([W[:, 0:1024], W[:, 1024:2048], W[:, 4096:6144], alr, W[:, 6176:7200], W[:, 7200:11296]], axis=1)
        win_fm.append(_blocks(Wsel))
        win_v.append(np.ascontiguousarray(W[:, 2048:4096].reshape(16, 128, 4, 512).transpose(2, 1, 0, 3)).reshape(4, 128, 16 * 512))
        G = f(inp["w_pool_group"][l])
        wpg.append(np.ascontiguousarray(G.reshape(4, 2, 128, 2, 128).transpose(0, 3, 2, 1, 4)).reshape(8, 128, 256))
        wgo.append(_blocks(f(inp["w_gla_out"][l])))
        wpo.append(_blocks(f(inp["w_pool_out"][l])))
        wo.append(_blocks(f(inp["w_out"][l])))
        Wf = f(inp["w_ffn_in"][l])
        Gt = Wf[:, :DFF].reshape(16, 128, 44, 128).transpose(2, 1, 0, 3)
        Up = Wf[:, DFF:].reshape(16, 128, 44, 128).transpose(2, 1, 0, 3)
        wfi.append(np.ascontiguousarray(np.concatenate([Gt, Up], axis=3)).reshape(44, 128, 16 * 256))
        wfo.append(_blocks(f(inp["w_ffn_out"][l])))
    cat = lambda xs: np.ascontiguousarray(np.concatenate(xs, axis=0))
    out.update(wada=cat(wada), lvec=np.stack(lvec), wdu=np.stack(wdu), win_fm=cat(win_fm), win_v=cat(win_v), wpg=cat(wpg),
               wgo=cat(wgo), wpo=cat(wpo), wo=cat(wo), wfi=cat(wfi), wfo=cat(wfo))
    return out


def run_device(inp, depth=DEPTH, dbg=()):
    import time
    t0 = time.time()
    nc = build_program(depth, dbg)
    t1 = time.time()
    shared = _consts()
    shared.update(_prep_weights(inp, depth))
    print(f"[kernel] build {t1 - t0:.1f}s prep {time.time() - t1:.1f}s", flush=True)
    x = np.asarray(inp["x"], np.float32)
    c = np.asarray(inp["c"], np.float32)
    ctx = np.asarray(inp["ctx"], np.float32)
    cc = np.asarray(inp["c_ctx"], np.float32)
    percore = []
    for b in range(2):
        xT0 = np.ascontiguousarray(np.concatenate([ctx[b].T, x[b].T], axis=1))
        cT = np.zeros((128, 16, 2), np.float32)
        cT[:, :, 0] = c[b].reshape(16, 128).T
        cT[:, :, 1] = cc.reshape(16, 128).T
        percore.append(dict(xT0=xT0, cT=cT.reshape(128, 32)))
    in_maps = []
    for core in range(NCORES):
        m = dict(shared)
        m.update(percore[core % 2])
        in_maps.append(m)
    t2 = time.time()
    res = run_bass_kernel_spmd(nc, in_maps, core_ids=list(range(NCORES)))
    print(f"[kernel] launch {time.time() - t2:.1f}s", flush=True)
    return res.results


def kernel(**inputs):
    r = run_device(inputs, DEPTH)
    out = np.stack([np.ascontiguousarray(r[b]["yT"].T) for b in range(2)], axis=0)
    return out.astype(np.float32)
```

```python
import math
from contextlib import ExitStack

import numpy as np
import concourse.bass as bass
import concourse.mybir as mybir
from concourse.bass_utils import run_bass_kernel_spmd

F32 = mybir.dt.float32
BF16 = mybir.dt.bfloat16
AF = mybir.ActivationFunctionType
ALU = mybir.AluOpType

D = 2048
SEQ = 4096
CTX = 256
T = SEQ + CTX
DEPTH = 4
DK = 1024
DV = 2048
DFF = 5632
DPOOL = 1024
WINS = (2, 4, 8, 16)
ALPHA = (2.0 * DEPTH) ** 0.25
LN_EPS = 1e-5
RMS_EPS = 1e-6
NDS = 8
NCORES = 2
import os
STOP = os.environ.get("KSTOP", "")

SUBS = [(0, 256)] + [(256 + 512 * i, 512) for i in range(8)]
LV_BADA, LV_GAIN, LV_PSC, LV_LMG, LV_LMB, LV_LFG, LV_LFB, LV_BDEC, LV_N = 0, 96, 112, 120, 136, 152, 168, 184, 200
NB_IN = 73


class Prog:
    ENG = ("pe", "act", "dve", "pool", "sp")

    def __init__(self, nc):
        self.nc = nc
        self.ops = {e: [] for e in self.ENG}
        self.sem = {e: nc.alloc_semaphore("s_" + e) for e in self.ENG}
        self.semobj = dict(self.sem)
        self.cnt = {e: 0 for e in self.ENG}
        self.known = {e: {} for e in self.ENG}
        self.res = {}
        self.dq = {}
        for q in ("sp", "pool"):
            for i in range(NDS):
                self.semobj[(q, i)] = nc.alloc_semaphore(f"d_{q}{i}")
            self.dq[q] = dict(i=0, val=[0] * NDS)

    def _deps(self, reads, writes):
        d = {}
        for r in reads:
            for k, v in self.res.get(r, ({}, {}))[0].items():
                d[k] = max(d.get(k, 0), v)
        for w in writes:
            rw = self.res.get(w, ({}, {}))
            for dd in rw:
                for k, v in dd.items():
                    d[k] = max(d.get(k, 0), v)
        return d

    def _wait(self, e, d):
        for k, v in d.items():
            if e == "pe" and k == "pe":
                continue
            if self.known[e].get(k, 0) >= v:
                continue
            self.known[e][k] = v
            self.ops[e].append(("w", k, v))

    def _mark(self, reads, writes, k, v):
        for r in reads:
            dd = self.res.setdefault(r, ({}, {}))[1]
            dd[k] = max(dd.get(k, 0), v)
        for w in writes:
            dd = self.res.setdefault(w, ({}, {}))[0]
            dd[k] = max(dd.get(k, 0), v)

    def op(self, e, meth, *args, reads=(), writes=(), inc=True, **kw):
        self._wait(e, self._deps(reads, writes))
        if inc:
            self.cnt[e] += 1
            seq = self.cnt[e]
        else:
            seq = self.cnt[e] + 1
        self.ops[e].append(("i", (meth, args, kw), inc))
        self._mark(reads, writes, e, seq)

    def dma(self, q, out, in_, reads=(), writes=(), **kw):
        st = self.dq[q]
        i = st["i"]
        st["i"] = (i + 1) % NDS
        key = (q, i)
        d = self._deps(reads, writes)
        if st["val"][i] > 0:
            d[key] = max(d.get(key, 0), st["val"][i])
        self._wait(q, d)
        st["val"][i] += 16
        self.ops[q].append(("d", out, in_, key, kw))
        self._mark(reads, writes, key, st["val"][i])

    def barrier(self):
        d = {e: self.cnt[e] for e in ("pe", "act", "dve", "pool") if self.cnt[e] > 0}
        for q, st in self.dq.items():
            for i, v in enumerate(st["val"]):
                if v > 0:
                    d[(q, i)] = v
        for e in self.ENG:
            self._wait(e, dict(d))

    def build(self):
        nc = self.nc
        with nc.Block() as block:
            decos = dict(sp=block.sync, pool=block.gpsimd, act=block.scalar, dve=block.vector, pe=block.tensor)
            for e in self.ENG:
                def body(eng, e=e):
                    for o in self.ops[e]:
                        if o[0] == "w":
                            eng.wait_ge(self.semobj[o[1]], o[2])
                        elif o[0] == "i":
                            meth, a, kw = o[1]
                            ins = getattr(eng, meth)(*a, **kw)
                            if o[2]:
                                ins.then_inc(self.sem[e], 1)
                        else:
                            eng.dma_start(out=o[1], in_=o[2], **o[4]).then_inc(self.semobj[o[3]], 16)
                decos[e](body)


class K:
    def __init__(self, nc, depth, dbg):
        self.nc = nc
        self.P = Prog(nc)
        self.depth = depth
        self.dbg = dbg
        self.alt = 0

    def din(self, name, shape, dt=F32):
        return self.nc.dram_tensor(name, list(shape), dt, kind="ExternalInput").ap()

    def dscr(self, name, shape, dt=F32):
        kind = "ExternalOutput" if name in self.dbg else "Internal"
        return self.nc.dram_tensor(name, list(shape), dt, kind=kind).ap()


def build_program(depth=DEPTH, dbg=()):
    nc = bass.Bass("TRN2", target_bir_lowering=False)
    k = K(nc, depth, dbg)
    P = k.P
    L = depth
    xT0 = k.din("xT0", [D, T])
    cT = k.din("cT", [128, 32])
    ident_d = k.din("ident", [128, 128])
    masks_d = k.din("masks", [64, 128])
    rmask_d = k.din("rmask", [128, 512])
    invc_d = k.din("invc", [128, 1536])
    wada = k.din("wada", [L * 96, 128, 2048])
    lvec = k.din("lvec", [L, 128, LV_N])
    wdu = k.din("wdu", [L, 16, 2048])
    win_fm = k.din("win_fm", [L * NB_IN, 128, 2048])
    win_v = k.din("win_v", [L * 4, 128, 16 * 512])
    wpg = k.din("wpg", [L * 8, 128, 256])
    wgo = k.din("wgo", [L * 16, 128, 2048])
    wpo = k.din("wpo", [L * 16, 128, 1024])
    wo = k.din("wo", [L * 16, 128, 2048])
    wfi = k.din("wfi", [L * 44, 128, 16 * 256])
    wfo = k.din("wfo", [L * 16, 128, 44 * 128])
    yT = nc.dram_tensor("yT", [D, SEQ], F32, kind="ExternalOutput").ap()

    xT = k.dscr("xT", [D, T])
    uT = k.dscr("uT", [D, T])
    hT = k.dscr("hT", [D, T], BF16)
    qT = k.dscr("qT", [DK, T])
    kT = k.dscr("kT", [DK, T])
    gT = k.dscr("gT", [DV, T])
    alrT = k.dscr("alrT", [128, T], BF16)
    pT = k.dscr("pT", [DPOOL, T])
    bgT = k.dscr("bgT", [2 * D, T])
    vtm = k.dscr("vtm", [T, DV], BF16)
    oT = k.dscr("oT", [DV, T])
    ogT = k.dscr("ogT", [DV, T], BF16)
    dT = k.dscr("dT", [DPOOL, T], BF16)
    eT = k.dscr("eT", [DPOOL, T], BF16)
    t1T = k.dscr("t1T", [D, T])
    mT = k.dscr("mT", [D, T], BF16)
    hidT = k.dscr("hidT", [DFF, T], BF16)

    sb = nc.alloc_sbuf_tensor
    ones_bf = sb("ones_bf", [128, 128], BF16)
    ones_f = sb("ones_f", [128, 128], F32)
    ident = sb("ident_sb", [128, 128], BF16)
    masks = sb("masks_sb", [64, 128], F32)
    rmask = sb("rmask_sb", [128, 512], F32)
    invc = sb("invc_sb", [128, 1536], F32)
    lv = sb("lv_sb", [128, LV_N], F32)
    negb = sb("negb_sb", [128, 16], F32)
    mods = sb("mods_sb", [128, 96, 2], F32)
    ops1 = sb("ops1_sb", [128, 96, 2], F32)
    scin = sb("scin_sb", [128, 16, 2], BF16)
    ctmp = sb("ctmp_sb", [128, 32], F32)
    wdu_sb = sb("wdu_sb", [16, 2048], BF16)
    epsr = sb("epsr_sb", [128, 1], F32)
    epsl = sb("epsl_sb", [128, 1], F32)
    ln16 = sb("ln16_sb", [128, 1], F32)
    psf = [nc.alloc_psum_tensor(f"psf{i}", [128, 512], F32) for i in range(7)]
    psb = nc.alloc_psum_tensor("psb", [128, 1024], BF16)
    PSF = [f"psf{i}" for i in range(7)]

    uid = [0]

    def sbt(name, shape, dt):
        uid[0] += 1
        return nc.sbuf_tensor(f"{name}_u{uid[0]}", shape, dt)

    def alt_eng():
        k.alt ^= 1
        return "act" if k.alt else "dve"

    P.op("dve", "memset", ones_bf[:], 1.0, writes=["ones_bf"])
    P.op("dve", "memset", ones_f[:], 1.0, writes=["ones_f"])
    P.op("dve", "memset", epsr[:], RMS_EPS, writes=["epsr"])
    P.op("dve", "memset", epsl[:], LN_EPS, writes=["epsl"])
    P.op("dve", "memset", ln16[:], math.log(1.0 / 16.0), writes=["ln16"])
    P.dma("pool", ident[:], ident_d, writes=["ident"])
    P.dma("sp", masks[:], masks_d, writes=["masks"])
    P.dma("sp", rmask[:], rmask_d, writes=["rmask"])
    P.dma("sp", invc[:], invc_d, writes=["invc"])
    P.dma("sp", ctmp[:], cT, writes=["ctmp"])
    P.op("act", "activation", out=scin[:].rearrange("p a b -> p (a b)"), in_=ctmp[:], func=AF.Silu,
         reads=["ctmp"], writes=["scin"])

    def copy_dram(dst, src, rows, dt_bytes=4):
        for r0 in range(0, rows, 128):
            P.dma("sp", dst[r0:r0 + 128, :], src[r0:r0 + 128, :], reads=["D:" + src.name], writes=["D:" + dst.name])

    def tgroups(maxtok):
        gs, cur, tot = [], [], 0
        for s in SUBS:
            if cur and tot + s[1] > maxtok:
                gs.append(cur)
                cur, tot = [], 0
            cur.append(s)
            tot += s[1]
        gs.append(cur)
        return gs

    def linear(name, xin, xname, KC, wblk, wrow0, NB, nw, maxtok, pre, epi, kcs=None):
        nkc = KC if kcs is None else len(kcs(0))
        with ExitStack() as es:
            xs = es.enter_context(sbt(f"{name}_x", [128, KC, maxtok], BF16))
            wbs = [es.enter_context(sbt(f"{name}_w{i}", [128, nkc, nw * 128], BF16)) for i in range(3)]
            xv = xin.rearrange("(kc p) t -> p kc t", p=128)
            bank = 0
            widx = 0
            for grp in tgroups(maxtok):
                g0 = grp[0][0]
                gw = sum(s[1] for s in grp)
                step = max(1, KC // 4)
                for c0 in range(0, KC, step):
                    P.dma("sp", xs[:, c0:c0 + step, 0:gw], xv[:, c0:c0 + step, g0:g0 + gw],
                          reads=["D:" + xname], writes=[f"{name}_x"])
                items = [(nb_, t0_, tw_) for nb_ in range(NB) for (t0_, tw_) in grp]
                PF = 3
                pre_i = 0
                it_i = 0
                if pre is not None:
                    while pre_i < min(PF, len(items)):
                        pre(*items[pre_i])
                        pre_i += 1
                for nb in range(NB):
                    wb = wbs[widx % 3]
                    wn = f"{name}_w{widx % 3}"
                    widx += 1
                    per = nkc * nw * 128
                    wsrc = wblk[wrow0 + nb]
                    cstep = 2048
                    wflat = wb[:].rearrange("p a b -> p (a b)")
                    for c0 in range(0, per, cstep):
                        c1 = min(per, c0 + cstep)
                        P.dma("pool", wflat[:, c0:c1], wsrc[:, c0:c1], writes=[wn])
                    kl = list(range(KC)) if kcs is None else kcs(nb)
                    for (t0, tw) in grp:
                        if pre is not None and pre_i < len(items):
                            pre(*items[pre_i])
                            pre_i += 1
                        it_i += 1
                        pss = []
                        for j in range(nw):
                            b = bank % 6
                            bank += 1
                            pss.append(b)
                            for i, kc in enumerate(kl):
                                last = i == len(kl) - 1
                                P.op("pe", "matmul",
                                    psf[b][:, 0:tw], wb[:, i, j * 128:(j + 1) * 128], xs[:, kc, t0 - g0:t0 - g0 + tw],
                                    start=(i == 0), stop=last,
                                    reads=[wn, f"{name}_x"], writes=[PSF[b]], inc=last)
                        epi(nb, t0, tw, pss)
        P.barrier()

    class Ring:
        def __init__(self, es, name, n, shape, dt):
            self.t = [es.enter_context(sbt(f"{name}{i}", shape, dt)) for i in range(n)]
            self.names = [f"{name}{i}" for i in range(n)]
            self.i = 0

        def next(self):
            j = self.i % len(self.t)
            self.i += 1
            return self.t[j], self.names[j]

    def col_of(t0):
        return 1 if t0 < CTX else 0

    for l in range(L):
        P.dma("sp", lv[:], lvec[l], writes=["lv"])
        P.dma("pool", wdu_sb[:], wdu[l], writes=["wdu"])
        P.op("dve", "tensor_scalar", negb[:], lv[:, LV_BDEC:LV_BDEC + 16], -1.0, None, ALU.mult,
             reads=["lv"], writes=["negb"])
        with ExitStack() as es:
            wbs = [es.enter_context(sbt(f"ada_w{i}", [128, 16, 128], BF16)) for i in range(3)]
            for nb in range(96):
                wb = wbs[nb % 3]
                wn = f"ada_w{nb % 3}"
                P.dma("pool", wb[:].rearrange("p a b -> p (a b)"), wada[l * 96 + nb], writes=[wn])
                for kc in range(16):
                    P.op("pe", "matmul",
                        psf[6][:, nb * 2:nb * 2 + 2], wb[:, kc, :], scin[:, kc, :], start=(kc == 0), stop=(kc == 15),
                        reads=[wn, "scin"], writes=[PSF[6]], inc=(kc == 15))
            P.op("dve", "tensor_tensor",
                mods[:], psf[6][:, 0:192].rearrange("p (a b) -> p a b", b=2),
                lv[:, LV_BADA:LV_BADA + 96].unsqueeze(2).to_broadcast([128, 96, 2]), ALU.add,
                reads=[PSF[6], "lv"], writes=["mods"])
            P.op("dve", "tensor_scalar", ops1[:], mods[:], 1.0, None, ALU.add, reads=["mods"], writes=["ops1"])
        P.barrier()

        xsrc, xsn = (xT0, "xT0") if l == 0 else (xT, "xT")

        def modulate(src, srcname, m_sh, m_sc):
            with ExitStack() as es:
                xr = Ring(es, "mod_x", 3, [128, 16, 512], F32)
                hr = Ring(es, "mod_h", 2, [128, 16, 512], BF16)
                sv = src.rearrange("(c p) t -> p c t", p=128)
                hv = hT.rearrange("(c p) t -> p c t", p=128)
                loaded = {}

                def mload(i):
                    t0_, tw_ = SUBS[i]
                    xb_, xn_ = xr.next()
                    P.dma("sp", xb_[:, :, 0:tw_], sv[:, :, t0_:t0_ + tw_], reads=["D:" + srcname], writes=[xn_])
                    loaded[i] = (xb_, xn_)
                mload(0)
                for si, (t0, tw) in enumerate(SUBS):
                    if si + 1 < len(SUBS):
                        mload(si + 1)
                    xb, xn = loaded.pop(si)
                    hb, hn = hr.next()
                    col = col_of(t0)
                    for c in range(16):
                        eng = alt_eng()
                        sc_ap = ops1[:, m_sc * 16 + c, col:col + 1]
                        sh_ap = mods[:, m_sh * 16 + c, col:col + 1]
                        if eng == "act":
                            P.op("act", "activation",
                                out=hb[:, c, 0:tw], in_=xb[:, c, 0:tw], func=AF.Identity, bias=sh_ap, scale=sc_ap,
                                reads=[xn, "mods", "ops1"], writes=[hn])
                        else:
                            P.op("dve", "tensor_scalar",
                                hb[:, c, 0:tw], xb[:, c, 0:tw], sc_ap, sh_ap, ALU.mult, ALU.add,
                                reads=[xn, "mods", "ops1"], writes=[hn])
                    P.dma("sp", hv[:, :, t0:t0 + tw], hb[:, :, 0:tw], reads=[hn], writes=["D:hT"])
            P.barrier()

        modulate(xsrc, xsn, 0, 1)

        segs = [(qT, 0, F32, 8), (kT, 0, F32, 8), (gT, 0, F32, 16), (alrT, 0, BF16, 1), (pT, 0, F32, 8), (bgT, 0, F32, 32)]
        segmap = []
        for (dst, _, dt_, n) in segs:
            for i in range(n):
                segmap.append((dst, i, dt_))
        with ExitStack() as es:
            r32 = Ring(es, "in_s32", 4, [128, 512], F32)
            r16 = Ring(es, "in_s16", 2, [128, 512], BF16)

            def epi_in(nb, t0, tw, pss):
                dst, i, dt_ = segmap[nb]
                st, sn = (r32 if dt_ == F32 else r16).next()
                eng = alt_eng()
                if eng == "act":
                    P.op("act", "activation", out=st[:, 0:tw], in_=psf[pss[0]][:, 0:tw], func=AF.Copy,
                         reads=[PSF[pss[0]]], writes=[sn])
                else:
                    P.op("dve", "tensor_copy", st[:, 0:tw], psf[pss[0]][:, 0:tw], reads=[PSF[pss[0]]], writes=[sn])
                P.dma("sp", dst[i * 128:(i + 1) * 128, t0:t0 + tw], st[:, 0:tw], reads=[sn], writes=["D:" + dst.name])

            linear("lin_in", hT, "hT", 16, win_fm, l * NB_IN, NB_IN, 1, 2304, None, epi_in)

        with ExitStack() as es:
            xs = es.enter_context(sbt("v_x", [128, 16, 2304], BF16))
            wbs = [es.enter_context(sbt(f"v_w{i}", [128, 16, 512], BF16)) for i in range(2)]
            vr = Ring(es, "v_s", 4, [128, 512], BF16)
            xv = hT.rearrange("(kc p) t -> p kc t", p=128)
            bank = 0
            widx = 0
            for (g0, gw) in ((0, 2304), (2304, 2048)):
                for c0 in range(0, 16, 4):
                    P.dma("sp", xs[:, c0:c0 + 4, 0:gw], xv[:, c0:c0 + 4, g0:g0 + gw], reads=["D:hT"], writes=["v_x"])
                for cb in range(4):
                    wb = wbs[widx % 2]
                    wn = f"v_w{widx % 2}"
                    widx += 1
                    wflat = wb[:].rearrange("p a b -> p (a b)")
                    for c0 in range(0, 16 * 512, 2048):
                        P.dma("pool", wflat[:, c0:c0 + 2048], win_v[l * 4 + cb][:, c0:c0 + 2048], writes=[wn])
                    for tt in range(gw // 128):
                        b = bank % 6
                        bank += 1
                        for kc in range(16):
                            P.op("pe", "matmul",
                                psf[b][:, 0:512], xs[:, kc, tt * 128:(tt + 1) * 128], wb[:, kc, :],
                                start=(kc == 0), stop=(kc == 15), reads=[wn, "v_x"], writes=[PSF[b]], inc=(kc == 15))
                        st, sn = vr.next()
                        eng = alt_eng()
                        if eng == "act":
                            P.op("act", "activation", out=st[:], in_=psf[b][:], func=AF.Copy,
                                 reads=[PSF[b]], writes=[sn])
                        else:
                            P.op("dve", "tensor_copy", st[:], psf[b][:], reads=[PSF[b]], writes=[sn])
                        r0 = g0 + tt * 128
                        P.dma("sp", vtm[r0:r0 + 128, cb * 512:(cb + 1) * 512], st[:], reads=[sn], writes=["D:vtm"])
        P.barrier()

        with ExitStack() as es:
            GB = 256
            Xs = [es.enter_context(sbt(f"g_S{i}", [128, 2, 512], F32)) for i in range(2)]
            q_r = Ring(es, "g_q", 2, [128, 2, GB], F32)
            k_r = Ring(es, "g_k", 2, [128, 2, GB], F32)
            a_r = Ring(es, "g_alr", 2, [16, GB], BF16)
            v_r = Ring(es, "g_v", 4, [64, 4, 512], BF16)
            of_r = Ring(es, "g_of", 4, [128, 4, GB], F32)
            gg_r = Ring(es, "g_g", 4, [128, 4, GB], F32)
            la = es.enter_context(sbt("g_la", [128, 2, GB], F32))
            cum = es.enter_context(sbt("g_cum", [128, 2, GB], F32))
            Aa = es.enter_context(sbt("g_A", [128, 2, GB], F32))
            Bb = es.enter_context(sbt("g_B", [128, 2, GB], F32))
            kin_r = Ring(es, "g_kin", 2, [128, 2, GB], BF16)
            qin_r = Ring(es, "g_qin", 2, [128, 2, GB], BF16)
            qint_r = Ring(es, "g_qint", 4, [128, 2, GB], BF16)
            refv = es.enter_context(sbt("g_ref", [128, 2, 4], F32))
            lastv = es.enter_context(sbt("g_last", [128, 2, 4], F32))
            er = es.enter_context(sbt("g_er", [128, 2, 4], F32))
            elr_r = Ring(es, "g_elr", 2, [128, 2, 4], F32)
            el_r = Ring(es, "g_el", 3, [128, 2, 4], F32)
            ktm_r = Ring(es, "g_ktm", 2, [64, 4, 256], BF16)
            sT_r = Ring(es, "g_sT", 3, [64, 4, 64], BF16)
            tkv_r = Ring(es, "g_tkv", 2, [128, 4, 2, 512], F32)
            sball = [es.enter_context(sbt(f"g_Sb{i}", [128, 4, 2, 512], BF16)) for i in range(3)]
            osb_r = Ring(es, "g_o", 2, [128, 4, GB], F32)
            sq = es.enter_context(sbt("g_sq", [128, 4, GB], BF16))
            rs = es.enter_context(sbt("g_rs", [128, GB], F32))
            og_r = Ring(es, "g_og", 2, [128, 4, GB], BF16)
            KVB = ((2, 3), (5, 6))
            OB = (4, 4)
            LN16 = math.log(1.0 / 16.0)
            lat = [(CTX + GB * i, GB) for i in range(SEQ // GB)]
            for h in range(4):
                qv = qT[h * 256:(h + 1) * 256, :].rearrange("(dc p) t -> p dc t", p=128)
                kv_ = kT[h * 256:(h + 1) * 256, :].rearrange("(dc p) t -> p dc t", p=128)
                ov = oT[h * 512:(h + 1) * 512, :].rearrange("(vc p) t -> p vc t", p=128)
                gv = gT[h * 512:(h + 1) * 512, :].rearrange("(vc p) t -> p vc t", p=128)
                ogv = ogT[h * 512:(h + 1) * 512, :].rearrange("(vc p) t -> p vc t", p=128)
                for dr in (0, 1):
                    blocks = [(0, CTX)] + (lat if dr == 0 else lat[::-1])
                    ri, li = (31, 63) if dr == 0 else (32, 0)
                    P.op("dve", "memset", Xs[0][:], 0.0, writes=["g_S0"])
                    P.op("dve", "memset", sball[0][:, 0, :, :], 0.0, writes=["g_Sb0"])
                    state = dict(si=0)
                    ctxs = {}

                    def s1(n, h=h, dr=dr, blocks=blocks, ri=ri, li=li, qv=qv, kv_=kv_, ov=ov, gv=gv):
                        t0, tw = blocks[n]
                        nch = tw // 64
                        par = n % 2
                        c = dict(t0=t0, tw=tw, nch=nch, par=par)
                        ctxs[n] = c
                        qb, qn = q_r.next()
                        kb, kn = k_r.next()
                        ab, an = a_r.next()
                        vb, vn = v_r.next()
                        c.update(vb=vb, vn=vn)
                        P.dma("sp", qb[:, :, 0:tw], qv[:, :, t0:t0 + tw], reads=["D:qT"], writes=[qn])
                        yield
                        P.dma("sp", kb[:, :, 0:tw], kv_[:, :, t0:t0 + tw], reads=["D:kT"], writes=[kn])
                        yield
                        P.dma("sp", ab[:, 0:tw], alrT[dr * 16:(dr + 1) * 16, t0:t0 + tw], reads=["D:alrT"], writes=[an])
                        yield
                        P.dma("sp", vb[:, 0:nch, :],
                              vtm[t0:t0 + tw, h * 512:(h + 1) * 512].rearrange("(c p) n -> p c n", p=64),
                              reads=["D:vtm"], writes=[vn])
                        yield
                        if dr == 1:
                            ofb, ofn = of_r.next()
                            ggb, ggn = gg_r.next()
                            c.update(ofb=ofb, ofn=ofn, ggb=ggb, ggn=ggn)
                            P.dma("sp", ofb[:, :, 0:tw], ov[:, :, t0:t0 + tw], reads=["D:oT"], writes=[ofn])
                            yield
                            P.dma("sp", ggb[:, :, 0:tw], gv[:, :, t0:t0 + tw], reads=["D:gT"], writes=[ggn])
                            yield
                        for dc in range(2):
                            col = (h * 2 + dc) * 128
                            P.op("pe", "matmul", psf[0][:, dc * 256:dc * 256 + tw],
                                 wdu_sb[0:16, dr * 1024 + col:dr * 1024 + col + 128], ab[0:16, 0:tw],
                                 start=True, stop=True, reads=["wdu", an], writes=["psz"])
                            yield
                            bi = dr * 8 + h * 2 + dc
                            P.op("act", "activation", out=la[:, dc, 0:tw], in_=psf[0][:, dc * 256:dc * 256 + tw], func=AF.Exp,
                                 bias=negb[:, bi:bi + 1], scale=-1.0, reads=["psz", "negb"], writes=["g_la"])
                            yield
                        P.op("act", "activation", out=la[:, :, 0:tw], in_=la[:, :, 0:tw], func=AF.Ln, bias=1.0,
                             reads=["g_la"], writes=["g_la"])
                        yield
                        for dc in range(2):
                            P.op("dve", "tensor_tensor_scan", cum[:, dc, 0:tw], rmask[:, 0:tw], la[:, dc, 0:tw], 0.0, ALU.mult, ALU.add,
                                 reads=["g_la", "rmask"], writes=["g_cum"])
                            yield

                        def cv(tl):
                            return tl[:, :, 0:tw].rearrange("p d (c i) -> p d c i", i=64)

                        def bc(sm):
                            return sm[:, :, 0:nch].unsqueeze(3).to_broadcast([128, 2, nch, 64])
                        if dr == 1:
                            P.op("dve", "tensor_copy", lastv[:, :, 0:nch], cv(cum)[:, :, :, 63], reads=["g_cum"], writes=["g_last"])
                            yield
                            P.op("dve", "tensor_tensor", cum[:, :, 0:tw], la[:, :, 0:tw], cum[:, :, 0:tw], ALU.subtract,
                                 reads=["g_la", "g_cum"], writes=["g_cum"])
                            yield
                            P.op("dve", "tensor_tensor", cv(cum), cv(cum), bc(lastv), ALU.add, reads=["g_cum", "g_last"], writes=["g_cum"])
                            yield
                        elb, eln = el_r.next()
                        kin, kinn = kin_r.next()
                        elr, elrn = elr_r.next()
                        c.update(elb=elb, eln=eln, kin=kin, kinn=kinn, elr=elr, elrn=elrn)
                        P.op("dve", "tensor_copy", refv[:, :, 0:nch], cv(cum)[:, :, :, ri], reads=["g_cum"], writes=["g_ref"])
                        yield
                        P.op("dve", "tensor_copy", lastv[:, :, 0:nch], cv(cum)[:, :, :, li], reads=["g_cum"], writes=["g_last"])
                        yield
                        P.op("act", "activation", out=Aa[:, :, 0:tw], in_=cum[:, :, 0:tw], func=AF.Exp, bias=ln16[:, 0:1], scale=-1.0 / 16.0,
                             reads=["g_cum", "ln16"], writes=["g_A"])
                        yield
                        qinb, qinn = qin_r.next()
                        qintb, qintn = qint_r.next()
                        c.update(qintb=qintb, qintn=qintn)
                        P.op("dve", "tensor_tensor", qintb[:, :, 0:tw], qb[:, :, 0:tw], Aa[:, :, 0:tw], ALU.mult, reads=[qn, "g_A"], writes=[qintn])
                        yield
                        P.op("dve", "tensor_tensor", cv(cum), cv(cum), bc(refv), ALU.subtract, reads=["g_cum", "g_ref"], writes=["g_cum"])
                        yield
                        P.op("dve", "tensor_tensor", elr[:, :, 0:nch], lastv[:, :, 0:nch], refv[:, :, 0:nch], ALU.subtract,
                             reads=["g_last", "g_ref"], writes=[elrn])
                        yield
                        P.op("act", "activation", out=Aa[:, :, 0:tw], in_=cum[:, :, 0:tw], func=AF.Exp, bias=ln16[:, 0:1], scale=-1.0 / 16.0,
                             reads=["g_cum", "ln16", qintn], writes=["g_A"])
                        yield
                        P.op("act", "activation", out=Bb[:, :, 0:tw], in_=cum[:, :, 0:tw], func=AF.Exp, scale=1.0 / 16.0,
                             reads=["g_cum"], writes=["g_B"])
                        yield
                        P.op("act", "activation", out=elb[:, :, 0:nch], in_=lastv[:, :, 0:nch], func=AF.Exp, scale=-1.0 / 16.0,
                             reads=["g_last"], writes=[eln])
                        yield
                        P.op("act", "activation", out=elr[:, :, 0:nch], in_=elr[:, :, 0:nch], func=AF.Exp, scale=-1.0 / 16.0,
                             reads=[elrn], writes=[elrn])
                        yield
                        P.op("dve", "tensor_tensor", kin[:, :, 0:tw], kb[:, :, 0:tw], Bb[:, :, 0:tw], ALU.mult, reads=[kn, "g_B"], writes=[kinn])
                        yield
                        P.op("dve", "tensor_tensor", qinb[:, :, 0:tw], qb[:, :, 0:tw], Aa[:, :, 0:tw], ALU.mult, reads=[qn, "g_A"], writes=[qinn])
                        yield
                        order = list(range(nch)) if dr == 0 else list(range(nch))[::-1]
                        c.update(order=order, qinb=qinb, qinn=qinn)

                    def s1b(n, dr=dr):
                        c = ctxs[n]
                        nch, par, vb, vn, order = c["nch"], c["par"], c["vb"], c["vn"], c["order"]
                        qinb, qinn = c["qinb"], c["qinn"]
                        kin, kinn, elr, elrn = c["kin"], c["kinn"], c["elr"], c["elrn"]
                        ktm, ktn = ktm_r.next()
                        sT, sTn = sT_r.next()
                        tkv, tkn = tkv_r.next()
                        c.update(sT=sT, sTn=sTn, tkv=tkv, tkn=tkn)
                        for kk, ch in enumerate(order):
                            cs = slice(ch * 64, (ch + 1) * 64)
                            for dc in range(2):
                                P.op("pe", "transpose", psb[0:64, kk * 256 + dc * 128:kk * 256 + (dc + 1) * 128], kin[:, dc, cs], ident[:, :],
                                     reads=[kinn, "ident"], writes=["psb"], inc=(kk == nch - 1 and dc == 1))
                        P.op("act", "activation", out=ktm[:, 0:nch, :].rearrange("p a b -> p (a b)"), in_=psb[0:64, 0:nch * 256], func=AF.Copy,
                             reads=["psb"], writes=[ktn])
                        yield
                        sreg = "pss"
                        for kk, ch in enumerate(order):
                            cs = slice(ch * 64, (ch + 1) * 64)
                            for dc in range(2):
                                P.op("pe", "matmul", psf[1][0:64, kk * 64:(kk + 1) * 64], kin[:, dc, cs], qinb[:, dc, cs],
                                     start=(dc == 0), stop=(dc == 1), reads=[kinn, qinn], writes=[sreg],
                                     inc=(kk == nch - 1 and dc == 1))
                        P.op("dve", "tensor_tensor", sT[:, 0:nch, :],
                             psf[1][0:64, 0:nch * 64].rearrange("p (a b) -> p a b", b=64),
                             masks[:, dr * 64:(dr + 1) * 64].unsqueeze(1).to_broadcast([64, nch, 64]), ALU.mult,
                             reads=[sreg, "masks"], writes=[sTn])
                        yield
                        for kk, ch in enumerate(order):
                            for dc in range(2):
                                b = KVB[kk % 2][dc]
                                P.op("pe", "matmul", psf[b][:, 0:512], ktm[0:64, kk, dc * 128:(dc + 1) * 128], vb[0:64, ch, :],
                                     start=True, stop=True, reads=[ktn, vn], writes=[PSF[b]])
                                yield
                                if dc == 0:
                                    P.op("act", "activation", out=tkv[:, kk, dc, :], in_=psf[b][:, 0:512], func=AF.Copy,
                                         scale=elr[:, dc, ch:ch + 1], reads=[PSF[b], elrn], writes=[tkn])
                                    yield
                                else:
                                    P.op("dve", "tensor_scalar", tkv[:, kk, dc, :], psf[b][:, 0:512], elr[:, dc, ch:ch + 1], None, ALU.mult,
                                         reads=[PSF[b], elrn], writes=[tkn])
                                    yield

                    def s2(n):
                        c = ctxs[n]
                        nch = c["nch"]
                        sbi = n % 3
                        for kk, ch in enumerate(c["order"]):
                            i = state["si"]
                            src, dst = Xs[i], Xs[1 - i]
                            for dc in range(2):
                                P.op("dve", "scalar_tensor_tensor", dst[:, dc, :], src[:, dc, :], c["elb"][:, dc, ch:ch + 1],
                                     c["tkv"][:, kk, dc, :], ALU.mult, ALU.add,
                                     reads=[f"g_S{i}", c["eln"], c["tkn"]], writes=[f"g_S{1 - i}"])
                            if kk < nch - 1:
                                tp, tk = sbi, kk + 1
                            else:
                                tp, tk = (sbi + 1) % 3, 0
                            P.op("act", "activation", out=sball[tp][:, tk, :, :], in_=dst[:, :, :], func=AF.Copy,
                                 reads=[f"g_S{1 - i}"], writes=[f"g_Sb{tp}"])
                            state["si"] = 1 - i
                            yield

                    def s3(n, h=h, dr=dr, ov=ov, ogv=ogv):
                        c = ctxs.pop(n)
                        par, nch, t0, tw = c["par"], c["nch"], c["t0"], c["tw"]
                        sbi = n % 3
                        vb, vn, sT, sTn, qintb, qintn = c["vb"], c["vn"], c["sT"], c["sTn"], c["qintb"], c["qintn"]
                        ob, on = osb_r.next()
                        for kk, ch in enumerate(c["order"]):
                            cs = slice(ch * 64, (ch + 1) * 64)
                            r0 = 0
                            ob_ = OB[kk % 2]
                            oreg = PSF[ob_]
                            for vc in range(4):
                                P.op("pe", "matmul", psf[ob_][:, r0 + vc * 64:r0 + (vc + 1) * 64], vb[0:64, ch, vc * 128:(vc + 1) * 128],
                                     sT[0:64, kk, :], start=True, stop=False, reads=[vn, sTn], writes=[oreg], inc=False)
                                for dc in range(2):
                                    P.op("pe", "matmul", psf[ob_][:, r0 + vc * 64:r0 + (vc + 1) * 64],
                                         sball[sbi][:, kk, dc, vc * 128:(vc + 1) * 128], qintb[:, dc, cs],
                                         start=False, stop=(dc == 1), reads=[f"g_Sb{sbi}", qintn], writes=[oreg],
                                         inc=(vc == 3 and dc == 1))
                            pov = psf[ob_][:, r0:r0 + 256].rearrange("p (v i) -> p v i", i=64)
                            if dr == 0:
                                P.op("act", "activation", out=ob[:, :, cs], in_=pov, func=AF.Copy, reads=[oreg], writes=[on])
                                yield
                            else:
                                P.op("dve", "tensor_tensor", ob[:, :, cs], pov, c["ofb"][:, :, cs], ALU.add, reads=[oreg, c["ofn"]], writes=[on])
                                yield
                        if dr == 0:
                            P.dma("sp", ov[:, :, t0:t0 + tw], ob[:, :, 0:tw], reads=[on], writes=["D:oT"])
                            yield
                        else:
                            ggb, ggn = c["ggb"], c["ggn"]
                            ogb, ogn = og_r.next()
                            rreg = PSF[4]
                            P.op("act", "activation", out=sq[:, :, 0:tw], in_=ob[:, :, 0:tw], func=AF.Square, reads=[on], writes=["g_sq"])
                            yield
                            for vc in range(4):
                                P.op("pe", "matmul", psf[4][:, 0:tw], ones_bf[:, :], sq[:, vc, 0:tw],
                                     start=(vc == 0), stop=(vc == 3), reads=["ones_bf", "g_sq"], writes=[rreg], inc=(vc == 3))
                            P.op("act", "activation", out=rs[:, 0:tw], in_=psf[4][:, 0:tw], func=AF.Ln,
                                 bias=epsr[:, 0:1], scale=1.0 / 512.0, reads=[rreg, "epsr"], writes=["g_rs"])
                            yield
                            P.op("act", "activation", out=rs[:, 0:tw], in_=rs[:, 0:tw], func=AF.Exp, scale=-0.5,
                                 reads=["g_rs"], writes=["g_rs"])
                            yield
                            P.op("act", "activation", out=ggb[:, :, 0:tw], in_=ggb[:, :, 0:tw], func=AF.Silu, reads=[ggn], writes=[ggn])
                            yield
                            for vc in range(4):
                                gi = LV_GAIN + h * 4 + vc
                                P.op("dve", "scalar_tensor_tensor", ob[:, vc, 0:tw], ob[:, vc, 0:tw], lv[:, gi:gi + 1], rs[:, 0:tw],
                                     ALU.mult, ALU.mult, reads=[on, "lv", "g_rs"], writes=[on])
                                yield
                            P.op("dve", "tensor_tensor", ogb[:, :, 0:tw], ob[:, :, 0:tw], ggb[:, :, 0:tw], ALU.mult,
                                 reads=[on, ggn], writes=[ogn])
                            yield
                            P.dma("sp", ogv[:, :, t0:t0 + tw], ogb[:, :, 0:tw], reads=[ogn], writes=["D:ogT"])
                            yield

                    def rr(*gens):
                        gens = [g for g in gens if g is not None]
                        while gens:
                            for g in list(gens):
                                try:
                                    next(g)
                                except StopIteration:
                                    gens.remove(g)

                    rr(s1(0))
                    NBk = len(blocks)
                    for n in range(NBk):
                        rr(s1b(n),
                           s1(n + 1) if n + 1 < NBk else None,
                           s3(n - 2) if n >= 2 else None,
                           s2(n - 1) if n >= 1 else None)
                    rr(s2(NBk - 1), s3(NBk - 2))
                    rr(s3(NBk - 1))
        P.barrier()

        if STOP == "gla":
            break
        with ExitStack() as es:
            p_r = Ring(es, "pl_p", 2, [128, T], F32)
            d_r = Ring(es, "pl_d", 2, [128, T], BF16)
            xp = es.enter_context(sbt("pl_xp", [128, 80, 80], F32))
            ba = es.enter_context(sbt("pl_a", [128, 6400], F32))
            bb = es.enter_context(sbt("pl_b", [128, 6400], F32))
            yp = es.enter_context(sbt("pl_yp", [128, 80, 64], F32))
            cp = es.enter_context(sbt("pl_cp", [128, 272], F32))
            P.op("dve", "memset", xp[:], 0.0, writes=["pl_xp"])
            P.op("dve", "memset", yp[:], 0.0, writes=["pl_yp"])
            P.op("dve", "memset", cp[:], 0.0, writes=["pl_cp"])
            for pc in range(8):
                wi = pc // 2
                w = WINS[wi]
                m = int(math.log2(w))
                lo = w // 2
                if pc == 0:
                    pl_loaded = {}
                    pb0, pn0 = p_r.next()
                    P.dma("sp", pb0[:], pT[0:128, :], reads=["D:pT"], writes=[pn0])
                    pl_loaded[0] = (pb0, pn0)
                pb, pn = pl_loaded.pop(pc)
                db, dn = d_r.next()
                if pc + 1 < 8:
                    pb1, pn1 = p_r.next()
                    P.dma("sp", pb1[:], pT[(pc + 1) * 128:(pc + 2) * 128, :], reads=["D:pT"], writes=[pn1])
                    pl_loaded[pc + 1] = (pb1, pn1)
                pimg = pb[:, CTX:T].rearrange("p (r c) -> p r c", c=64)
                P.op("act", "activation", out=xp[:, 8:72, 8:72], in_=pimg, func=AF.Copy,
                     reads=[pn], writes=["pl_xp"])
                a80 = ba[:].rearrange("p (r c) -> p r c", c=80)
                b80 = bb[:].rearrange("p (r c) -> p r c", c=80)
                cur, curn, width = xp[:], "pl_xp", 80
                for s in range(m):
                    sh = 2 ** s
                    nw_ = width - sh
                    nxt, nxtn = (a80, "pl_a") if s % 2 == 0 else (b80, "pl_b")
                    P.op("dve", "tensor_tensor",
                        nxt[:, 8:72, 0:nw_], cur[:, 8:72, 0:nw_], cur[:, 8:72, sh:sh + nw_], ALU.add,
                        reads=[curn], writes=[nxtn])
                    cur, curn, width = nxt, nxtn, nw_
                icol = invc[:, wi * 64:(wi + 1) * 64].unsqueeze(1).to_broadcast([128, 64, 64])
                P.op("dve", "tensor_tensor",
                    yp[:, 8:72, :], cur[:, 8:72, 8 - lo:8 - lo + 64], icol, ALU.mult, reads=[curn, "invc"], writes=["pl_yp"])
                a64 = ba[:, 0:5120].rearrange("p (r c) -> p r c", c=64)
                b64 = bb[:, 0:5120].rearrange("p (r c) -> p r c", c=64)
                cur, curn, height = yp[:], "pl_yp", 80
                for s in range(m):
                    sh = 2 ** s
                    nh = height - sh
                    nxt, nxtn = (a64, "pl_a") if s % 2 == 0 else (b64, "pl_b")
                    P.op("dve", "tensor_tensor",
                        nxt[:, 0:nh, :], cur[:, 0:nh, :], cur[:, sh:sh + nh, :], ALU.add, reads=[curn], writes=[nxtn])
                    cur, curn, height = nxt, nxtn, nh
                irow = invc[:, 256 + wi * 64:256 + (wi + 1) * 64].unsqueeze(2).to_broadcast([128, 64, 64])
                mean_n = "pl_b" if curn == "pl_a" else "pl_a"
                mean_t = b64 if curn == "pl_a" else a64
                P.op("dve", "tensor_tensor",
                    mean_t[:, 0:64, :], cur[:, 8 - lo:8 - lo + 64, :], irow, ALU.mult, reads=[curn, "invc"], writes=[mean_n])
                dimg = db[:, CTX:T].rearrange("p (r c) -> p r c", c=64)
                P.op("dve", "tensor_tensor",
                    dimg, mean_t[:, 0:64, :], pimg, ALU.subtract, reads=[mean_n, pn], writes=[dn])
                P.op("act", "activation", out=cp[:, 8:264], in_=pb[:, 0:CTX], func=AF.Copy,
                     reads=[pn], writes=["pl_cp"])
                cur, curn, width = cp[:], "pl_cp", 272
                for s in range(m):
                    sh = 2 ** s
                    nw_ = width - sh
                    nxt, nxtn = (ba, "pl_a") if s % 2 == 0 else (bb, "pl_b")
                    P.op("dve", "tensor_tensor",
                        nxt[:, 0:nw_], cur[:, 0:nw_], cur[:, sh:sh + nw_], ALU.add, reads=[curn], writes=[nxtn])
                    cur, curn, width = nxt[:, :], nxtn, nw_
                mean_n = "pl_b" if curn == "pl_a" else "pl_a"
                mean_c = bb if curn == "pl_a" else ba
                P.op("dve", "tensor_tensor",
                    mean_c[:, 0:CTX], cur[:, 8 - lo:8 - lo + CTX], invc[:, 512 + wi * 256:512 + (wi + 1) * 256], ALU.mult,
                    reads=[curn, "invc"], writes=[mean_n])
                P.op("dve", "tensor_tensor",
                    db[:, 0:CTX], mean_c[:, 0:CTX], pb[:, 0:CTX], ALU.subtract, reads=[mean_n, pn], writes=[dn])
                P.dma("sp", dT[pc * 128:(pc + 1) * 128, :], db[:], reads=[dn], writes=["D:dT"])
        P.barrier()

        with ExitStack() as es:
            r16 = Ring(es, "pg_s", 4, [128, 512], BF16)

            def epi_pg(nb, t0, tw, pss):
                st, sn = r16.next()
                P.op("act", "activation", out=st[:, 0:tw], in_=psf[pss[0]][:, 0:tw], func=AF.Copy,
                                                   scale=lv[:, LV_PSC + nb:LV_PSC + nb + 1],
                     reads=[PSF[pss[0]], "lv"], writes=[sn])
                P.dma("sp", eT[nb * 128:(nb + 1) * 128, t0:t0 + tw], st[:, 0:tw], reads=[sn], writes=["D:eT"])

            linear("lin_pg", dT, "dT", 8, wpg, l * 8, 8, 1, 4352, None, epi_pg, kcs=lambda nb: [2 * (nb // 2), 2 * (nb // 2) + 1])

        with ExitStack() as es:
            rb = Ring(es, "go_b", 6, [128, 512], F32)
            rsg = Ring(es, "go_s", 4, [128, 512], F32)
            pend = {}

            def pre_go(nb, t0, tw):
                bt, bn = rb.next()
                P.dma("sp", bt[:, 0:tw], bgT[nb * 128:(nb + 1) * 128, t0:t0 + tw], reads=["D:bgT"], writes=[bn])
                pend[(nb, t0)] = (bt, bn)

            def epi_go(nb, t0, tw, pss):
                bt, bn = pend.pop((nb, t0))
                st, sn = rsg.next()
                P.op("act", "activation", out=bt[:, 0:tw], in_=bt[:, 0:tw], func=AF.Sigmoid, reads=[bn], writes=[bn])
                P.op("dve", "tensor_tensor", st[:, 0:tw], psf[pss[0]][:, 0:tw], bt[:, 0:tw], ALU.mult,
                     reads=[PSF[pss[0]], bn], writes=[sn])
                P.dma("sp", t1T[nb * 128:(nb + 1) * 128, t0:t0 + tw], st[:, 0:tw], reads=[sn], writes=["D:t1T"])

            linear("lin_go", ogT, "ogT", 16, wgo, l * 16, 16, 1, 2304, pre_go, epi_go)

        with ExitStack() as es:
            rb = Ring(es, "po_b", 6, [128, 512], F32)
            rt = Ring(es, "po_t", 6, [128, 512], F32)
            rsg = Ring(es, "po_s", 4, [128, 512], BF16)
            pend = {}

            def pre_po(nb, t0, tw):
                bt, bn = rb.next()
                tt, tn = rt.next()
                P.dma("sp", bt[:, 0:tw], bgT[D + nb * 128:D + (nb + 1) * 128, t0:t0 + tw], reads=["D:bgT"], writes=[bn])
                P.dma("sp", tt[:, 0:tw], t1T[nb * 128:(nb + 1) * 128, t0:t0 + tw], reads=["D:t1T"], writes=[tn])
                pend[(nb, t0)] = (bt, bn, tt, tn)

            def epi_po(nb, t0, tw, pss):
                bt, bn, tt, tn = pend.pop((nb, t0))
                st, sn = rsg.next()
                P.op("act", "activation", out=bt[:, 0:tw], in_=bt[:, 0:tw], func=AF.Sigmoid, reads=[bn], writes=[bn])
                P.op("dve", "tensor_tensor", bt[:, 0:tw], psf[pss[0]][:, 0:tw], bt[:, 0:tw], ALU.mult,
                     reads=[PSF[pss[0]], bn], writes=[bn])
                P.op("dve", "tensor_tensor", st[:, 0:tw], bt[:, 0:tw], tt[:, 0:tw], ALU.add,
                     reads=[bn, tn], writes=[sn])
                P.dma("sp", mT[nb * 128:(nb + 1) * 128, t0:t0 + tw], st[:, 0:tw], reads=[sn], writes=["D:mT"])

            linear("lin_po", eT, "eT", 8, wpo, l * 16, 16, 1, 4352, pre_po, epi_po)

        def resid(es, name, m_gt, xsrc_, xsn_):
            rx = Ring(es, name + "_x", 6, [128, 512], F32)
            rt = Ring(es, name + "_t", 4, [128, 512], F32)
            pend = {}

            def pre(nb, t0, tw):
                xt, xn = rx.next()
                P.dma("sp", xt[:, 0:tw], xsrc_[nb * 128:(nb + 1) * 128, t0:t0 + tw], reads=["D:" + xsn_], writes=[xn])
                pend[(nb, t0)] = (xt, xn)

            def epi(nb, t0, tw, pss):
                xt, xn = pend.pop((nb, t0))
                tt, tn = rt.next()
                col = col_of(t0)
                P.op("act", "activation", out=tt[:, 0:tw], in_=psf[pss[0]][:, 0:tw], func=AF.Copy,
                                                   scale=mods[:, m_gt * 16 + nb, col:col + 1],
                     reads=[PSF[pss[0]], "mods"], writes=[tn])
                P.op("dve", "scalar_tensor_tensor", xt[:, 0:tw], xt[:, 0:tw], float(ALPHA), tt[:, 0:tw], ALU.mult, ALU.add,
                     reads=[xn, tn], writes=[xn])
                P.dma("sp", uT[nb * 128:(nb + 1) * 128, t0:t0 + tw], xt[:, 0:tw], reads=[xn], writes=["D:uT"])
            return pre, epi

        def layer_norm(g_off, b_off, mod=None):
            with ExitStack() as es:
                hv = hT.rearrange("(c p) t -> p c t", p=128)
                ur = Ring(es, "ln_u", 2, [128, 16, 512], F32)
                sqr = Ring(es, "ln_sq", 2, [128, 16, 512], BF16)
                mr = Ring(es, "ln_mean", 2, [128, 512], F32)
                qr = Ring(es, "ln_msq", 2, [128, 512], F32)
                rr = Ring(es, "ln_rstd", 2, [128, 512], F32)
                uv = uT.rearrange("(c p) t -> p c t", p=128)
                xv = xT.rearrange("(c p) t -> p c t", p=128)
                loaded = {}

                def lload(i):
                    t0_, tw_ = SUBS[i]
                    ub_, un_ = ur.next()
                    P.dma("sp", ub_[:, :, 0:tw_], uv[:, :, t0_:t0_ + tw_], reads=["D:uT"], writes=[un_])
                    loaded[i] = (ub_, un_)
                lload(0)
                for si, (t0, tw) in enumerate(SUBS):
                    ub, un = loaded.pop(si)
                    sqt, sqn = sqr.next()
                    mean, mn = mr.next()
                    msq, qn_ = qr.next()
                    rstd, rn = rr.next()
                    pa, pb = (0, 1) if si % 2 == 0 else (2, 3)
                    P.op("act", "activation", out=sqt[:, :, 0:tw], in_=ub[:, :, 0:tw], func=AF.Square, reads=[un], writes=[sqn])
                    for c in range(16):
                        P.op("pe", "matmul", psf[pa][:, 0:tw], ones_f[:, :], ub[:, c, 0:tw], start=(c == 0), stop=(c == 15),
                             reads=["ones_f", un], writes=[PSF[pa]], inc=(c == 15))
                    for c in range(16):
                        P.op("pe", "matmul", psf[pb][:, 0:tw], ones_bf[:, :], sqt[:, c, 0:tw], start=(c == 0), stop=(c == 15),
                             reads=["ones_bf", sqn], writes=[PSF[pb]], inc=(c == 15))
                    P.op("act", "activation", out=mean[:, 0:tw], in_=psf[pa][:, 0:tw], func=AF.Copy, scale=1.0 / D,
                         reads=[PSF[pa]], writes=[mn])
                    P.op("dve", "tensor_tensor", msq[:, 0:tw], mean[:, 0:tw], mean[:, 0:tw], ALU.mult, reads=[mn], writes=[qn_])
                    P.op("dve", "scalar_tensor_tensor", rstd[:, 0:tw], psf[pb][:, 0:tw], 1.0 / D, msq[:, 0:tw], ALU.mult, ALU.subtract,
                         reads=[PSF[pb], qn_], writes=[rn])
                    P.op("act", "activation", out=rstd[:, 0:tw], in_=rstd[:, 0:tw], func=AF.Ln, bias=epsl[:, 0:1],
                         reads=[rn, "epsl"], writes=[rn])
                    P.op("act", "activation", out=rstd[:, 0:tw], in_=rstd[:, 0:tw], func=AF.Exp, scale=-0.5, reads=[rn], writes=[rn])
                    if si + 1 < len(SUBS):
                        lload(si + 1)
                    mb = mean[:, 0:tw].unsqueeze(1).to_broadcast([128, 16, tw])
                    rb_ = rstd[:, 0:tw].unsqueeze(1).to_broadcast([128, 16, tw])
                    P.op("dve", "tensor_tensor", ub[:, :, 0:tw], ub[:, :, 0:tw], mb, ALU.subtract, reads=[un, mn], writes=[un])
                    P.op("dve", "tensor_tensor", ub[:, :, 0:tw], ub[:, :, 0:tw], rb_, ALU.mult, reads=[un, rn], writes=[un])
                    for c in range(16):
                        P.op("act", "activation", out=ub[:, c, 0:tw], in_=ub[:, c, 0:tw], func=AF.Identity,
                             bias=lv[:, b_off + c:b_off + c + 1], scale=lv[:, g_off + c:g_off + c + 1],
                             reads=[un, "lv"], writes=[un])
                    P.dma("sp", xv[:, :, t0:t0 + tw], ub[:, :, 0:tw], reads=[un], writes=["D:xT"])
                    if mod is not None:
                        m_sh, m_sc = mod
                        col = col_of(t0)
                        for c in range(16):
                            sc_ap = ops1[:, m_sc * 16 + c, col:col + 1]
                            sh_ap = mods[:, m_sh * 16 + c, col:col + 1]
                            if c % 2 == 0:
                                P.op("dve", "tensor_scalar", sqt[:, c, 0:tw], ub[:, c, 0:tw], sc_ap, sh_ap, ALU.mult, ALU.add,
                                     reads=[un, "mods", "ops1"], writes=[sqn])
                            else:
                                P.op("act", "activation", out=sqt[:, c, 0:tw], in_=ub[:, c, 0:tw], func=AF.Identity, bias=sh_ap, scale=sc_ap,
                                     reads=[un, "mods", "ops1"], writes=[sqn])
                        P.dma("sp", hv[:, :, t0:t0 + tw], sqt[:, :, 0:tw], reads=[sqn], writes=["D:hT"])
            P.barrier()

        with ExitStack() as es:
            pre, epi = resid(es, "ro", 2, xsrc, xsn)
            linear("lin_o", mT, "mT", 16, wo, l * 16, 16, 1, 2304, pre, epi)
        layer_norm(LV_LMG, LV_LMB, mod=(3, 4))

        with ExitStack() as es:
            rs_ = Ring(es, "fi_s", 4, [128, 512], F32)
            rh = Ring(es, "fi_h", 4, [128, 512], BF16)

            def epi_fi(nb, t0, tw, pss):
                st, sn = rs_.next()
                ht, hn = rh.next()
                P.op("act", "activation", out=st[:, 0:tw], in_=psf[pss[0]][:, 0:tw], func=AF.Silu,
                     reads=[PSF[pss[0]]], writes=[sn])
                P.op("dve", "tensor_tensor", ht[:, 0:tw], st[:, 0:tw], psf[pss[1]][:, 0:tw], ALU.mult,
                     reads=[sn, PSF[pss[1]]], writes=[hn])
                P.dma("sp", hidT[nb * 128:(nb + 1) * 128, t0:t0 + tw], ht[:, 0:tw], reads=[hn], writes=["D:hidT"])

            linear("lin_fi", hT, "hT", 16, wfi, l * 44, 44, 2, 2304, None, epi_fi)
        with ExitStack() as es:
            pre, epi = resid(es, "rf", 5, xT, "xT")
            linear("lin_fo", hidT, "hidT", 44, wfo, l * 16, 16, 1, 1024, pre, epi)
        layer_norm(LV_LFG, LV_LFB)

    for r0 in range(0, D, 128):
        P.dma("sp", yT[r0:r0 + 128, :], xT[r0:r0 + 128, CTX:T], reads=["D:xT"], writes=["D:yT"])
    P.barrier()
    P.build()
    return nc


def _blocks(W):
    Kd, N = W.shape
    return np.ascontiguousarray(W.reshape(Kd // 128, 128, N // 128, 128).transpose(2, 1, 0, 3)).reshape(
        N // 128, 128, (Kd // 128) * 128)


def _consts():
    ident = np.eye(128, dtype=np.float32)
    j = np.arange(64)[:, None]
    i = np.arange(64)[None, :]
    masks = np.concatenate([(i >= j), (i <= j)], axis=1).astype(np.float32)
    rmask = np.ones((128, 512), np.float32)
    rmask[:, ::64] = 0.0
    def inv(n, w):
        lo = w // 2
        hi = w - lo - 1
        idx = np.arange(n)
        return (1.0 / (np.minimum(idx + hi + 1, n) - np.maximum(idx - lo, 0))).astype(np.float32)
    row = np.concatenate([inv(64, w) for w in WINS] + [inv(64, w) for w in WINS] + [inv(CTX, w) for w in WINS])
    invc = np.ascontiguousarray(np.broadcast_to(row[None, :], (128, 1536))).astype(np.float32)
    return dict(ident=ident, masks=masks, rmask=rmask, invc=invc)


def _prep_weights(inp, L):
    f = lambda a: np.asarray(a, dtype=np.float32)
    out = {}
    wada, lvec, wdu, win_fm, win_v, wpg, wgo, wpo, wo, wfi, wfo = ([] for _ in range(11))
    for l in range(L):
        wada.append(_blocks(f(inp["w_ada"][l])))
        lv = np.zeros((128, LV_N), np.float32)
        lv[:, LV_BADA:LV_BADA + 96] = f(inp["b_ada"][l]).reshape(96, 128).T
        lv[:, LV_GAIN:LV_GAIN + 16] = f(inp["gla_norm_gain"][l]).reshape(16, 128).T
        lv[:, LV_PSC:LV_PSC + 8] = f(inp["pool_scale"][l]).reshape(8, 128).T
        lv[:, LV_LMG:LV_LMG + 16] = f(inp["ln_mix_gain"][l]).reshape(16, 128).T
        lv[:, LV_LMB:LV_LMB + 16] = f(inp["ln_mix_bias"][l]).reshape(16, 128).T
        lv[:, LV_LFG:LV_LFG + 16] = f(inp["ln_ffn_gain"][l]).reshape(16, 128).T
        lv[:, LV_LFB:LV_LFB + 16] = f(inp["ln_ffn_bias"][l]).reshape(16, 128).T
        lv[:, LV_BDEC:LV_BDEC + 16] = f(inp["b_decay_up"][l]).reshape(16, 128).T
        lvec.append(lv)
        wdu.append(f(inp["w_decay_up"][l]).transpose(1, 0, 2).reshape(16, 2048))
        W = f(inp["w_in"][l])
        alr = np.zeros((D, 128), np.float32)
        alr[:, :32] = W[:, 6144:6176]
        Wsel = np.concatenate([W[:, 0:1024], W[:, 1024:2048], W[:, 4096:6144], alr, W[:, 6176:7200], W[:, 7200:11296]], axis=1)
        win_fm.append(_blocks(Wsel))
        win_v.append(np.ascontiguousarray(W[:, 2048:4096].reshape(16, 128, 4, 512).transpose(2, 1, 0, 3)).reshape(4, 128, 16 * 512))
        G = f(inp["w_pool_group"][l])
        wpg.append(np.ascontiguousarray(G.reshape(4, 2, 128, 2, 128).transpose(0, 3, 2, 1, 4)).reshape(8, 128, 256))
        wgo.append(_blocks(f(inp["w_gla_out"][l])))
        wpo.append(_blocks(f(inp["w_pool_out"][l])))
        wo.append(_blocks(f(inp["w_out"][l])))
        Wf = f(inp["w_ffn_in"][l])
        Gt = Wf[:, :DFF].reshape(16, 128, 44, 128).transpose(2, 1, 0, 3)
        Up = Wf[:, DFF:].reshape(16, 128, 44, 128).transpose(2, 1, 0, 3)
        wfi.append(np.ascontiguousarray(np.concatenate([Gt, Up], axis=3)).reshape(44, 128, 16 * 256))
        wfo.append(_blocks(f(inp["w_ffn_out"][l])))
    cat = lambda xs: np.ascontiguousarray(np.concatenate(xs, axis=0))
    out.update(wada=cat(wada), lvec=np.stack(lvec), wdu=np.stack(wdu), win_fm=cat(win_fm), win_v=cat(win_v), wpg=cat(wpg),
               wgo=cat(wgo), wpo=cat(wpo), wo=cat(wo), wfi=cat(wfi), wfo=cat(wfo))
    return out


def run_device(inp, depth=DEPTH, dbg=()):
    import time
    t0 = time.time()
    nc = build_program(depth, dbg)
    t1 = time.time()
    shared = _consts()
    shared.update(_prep_weights(inp, depth))
    print(f"[kernel] build {t1 - t0:.1f}s prep {time.time() - t1:.1f}s", flush=True)
    x = np.asarray(inp["x"], np.float32)
    c = np.asarray(inp["c"], np.float32)
    ctx = np.asarray(inp["ctx"], np.float32)
    cc = np.asarray(inp["c_ctx"], np.float32)
    percore = []
    for b in range(2):
        xT0 = np.ascontiguousarray(np.concatenate([ctx[b].T, x[b].T], axis=1))
        cT = np.zeros((128, 16, 2), np.float32)
        cT[:, :, 0] = c[b].reshape(16, 128).T
        cT[:, :, 1] = cc.reshape(16, 128).T
        percore.append(dict(xT0=xT0, cT=cT.reshape(128, 32)))
    in_maps = []
    for core in range(NCORES):
        m = dict(shared)
        m.update(percore[core % 2])
        in_maps.append(m)
    t2 = time.time()
    res = run_bass_kernel_spmd(nc, in_maps, core_ids=list(range(NCORES)))
    print(f"[kernel] launch {time.time() - t2:.1f}s", flush=True)
    return res.results


def kernel(**inputs):
    r = run_device(inputs, DEPTH)
    out = np.stack([np.ascontiguousarray(r[b]["yT"].T) for b in range(2)], axis=0)
    return out.astype(np.float32)
```

```python
import math
from contextlib import ExitStack

import numpy as np
import concourse.bass as bass
import concourse.mybir as mybir
from concourse.bass_utils import run_bass_kernel_spmd

F32 = mybir.dt.float32
BF16 = mybir.dt.bfloat16
AF = mybir.ActivationFunctionType
ALU = mybir.AluOpType

D = 2048
SEQ = 4096
CTX = 256
T = SEQ + CTX
DEPTH = 4
DK = 1024
DV = 2048
DFF = 5632
DPOOL = 1024
WINS = (2, 4, 8, 16)
ALPHA = (2.0 * DEPTH) ** 0.25
LN_EPS = 1e-5
RMS_EPS = 1e-6
NDS = 8
NCORES = 2
import os
STOP = os.environ.get("KSTOP", "")

SUBS = [(0, 256)] + [(256 + 512 * i, 512) for i in range(8)]
LV_BADA, LV_GAIN, LV_PSC, LV_LMG, LV_LMB, LV_LFG, LV_LFB, LV_BDEC, LV_N = 0, 96, 112, 120, 136, 152, 168, 184, 200
NB_IN = 73


class Prog:
    ENG = ("pe", "act", "dve", "pool", "sp")

    def __init__(self, nc):
        self.nc = nc
        self.ops = {e: [] for e in self.ENG}
        self.sem = {e: nc.alloc_semaphore("s_" + e) for e in self.ENG}
        self.semobj = dict(self.sem)
        self.cnt = {e: 0 for e in self.ENG}
        self.known = {e: {} for e in self.ENG}
        self.res = {}
        self.dq = {}
        for q in ("sp", "pool"):
            for i in range(NDS):
                self.semobj[(q, i)] = nc.alloc_semaphore(f"d_{q}{i}")
            self.dq[q] = dict(i=0, val=[0] * NDS)

    def _deps(self, reads, writes):
        d = {}
        for r in reads:
            for k, v in self.res.get(r, ({}, {}))[0].items():
                d[k] = max(d.get(k, 0), v)
        for w in writes:
            rw = self.res.get(w, ({}, {}))
            for dd in rw:
                for k, v in dd.items():
                    d[k] = max(d.get(k, 0), v)
        return d

    def _wait(self, e, d):
        for k, v in d.items():
            if e == "pe" and k == "pe":
                continue
            if self.known[e].get(k, 0) >= v:
                continue
            self.known[e][k] = v
            self.ops[e].append(("w", k, v))

    def _mark(self, reads, writes, k, v):
        for r in reads:
            dd = self.res.setdefault(r, ({}, {}))[1]
            dd[k] = max(dd.get(k, 0), v)
        for w in writes:
            dd = self.res.setdefault(w, ({}, {}))[0]
            dd[k] = max(dd.get(k, 0), v)

    def op(self, e, meth, *args, reads=(), writes=(), inc=True, **kw):
        self._wait(e, self._deps(reads, writes))
        if inc:
            self.cnt[e] += 1
            seq = self.cnt[e]
        else:
            seq = self.cnt[e] + 1
        self.ops[e].append(("i", (meth, args, kw), inc))
        self._mark(reads, writes, e, seq)

    def dma(self, q, out, in_, reads=(), writes=(), **kw):
        st = self.dq[q]
        i = st["i"]
        st["i"] = (i + 1) % NDS
        key = (q, i)
        d = self._deps(reads, writes)
        if st["val"][i] > 0:
            d[key] = max(d.get(key, 0), st["val"][i])
        self._wait(q, d)
        st["val"][i] += 16
        self.ops[q].append(("d", out, in_, key, kw))
        self._mark(reads, writes, key, st["val"][i])

    def barrier(self):
        d = {e: self.cnt[e] for e in ("pe", "act", "dve", "pool") if self.cnt[e] > 0}
        for q, st in self.dq.items():
            for i, v in enumerate(st["val"]):
                if v > 0:
                    d[(q, i)] = v
        for e in self.ENG:
            self._wait(e, dict(d))

    def build(self):
        nc = self.nc
        with nc.Block() as block:
            decos = dict(sp=block.sync, pool=block.gpsimd, act=block.scalar, dve=block.vector, pe=block.tensor)
            for e in self.ENG:
                def body(eng, e=e):
                    for o in self.ops[e]:
                        if o[0] == "w":
                            eng.wait_ge(self.semobj[o[1]], o[2])
                        elif o[0] == "i":
                            meth, a, kw = o[1]
                            ins = getattr(eng, meth)(*a, **kw)
                            if o[2]:
                                ins.then_inc(self.sem[e], 1)
                        else:
                            eng.dma_start(out=o[1], in_=o[2], **o[4]).then_inc(self.semobj[o[3]], 16)
                decos[e](body)


class K:
    def __init__(self, nc, depth, dbg):
        self.nc = nc
        self.P = Prog(nc)
        self.depth = depth
        self.dbg = dbg
        self.alt = 0

    def din(self, name, shape, dt=F32):
        return self.nc.dram_tensor(name, list(shape), dt, kind="ExternalInput").ap()

    def dscr(self, name, shape, dt=F32):
        kind = "ExternalOutput" if name in self.dbg else "Internal"
        return self.nc.dram_tensor(name, list(shape), dt, kind=kind).ap()


def build_program(depth=DEPTH, dbg=()):
    nc = bass.Bass("TRN2", target_bir_lowering=False)
    k = K(nc, depth, dbg)
    P = k.P
    L = depth
    xT0 = k.din("xT0", [D, T])
    cT = k.din("cT", [128, 32])
    ident_d = k.din("ident", [128, 128])
    masks_d = k.din("masks", [64, 128])
    rmask_d = k.din("rmask", [128, 512])
    invc_d = k.din("invc", [128, 1536])
    wada = k.din("wada", [L * 96, 128, 2048])
    lvec = k.din("lvec", [L, 128, LV_N])
    wdu = k.din("wdu", [L, 16, 2048])
    win_fm = k.din("win_fm", [L * NB_IN, 128, 2048])
    win_v = k.din("win_v", [L * 4, 128, 16 * 512])
    wpg = k.din("wpg", [L * 8, 128, 256])
    wgo = k.din("wgo", [L * 16, 128, 2048])
    wpo = k.din("wpo", [L * 16, 128, 1024])
    wo = k.din("wo", [L * 16, 128, 2048])
    wfi = k.din("wfi", [L * 44, 128, 16 * 256])
    wfo = k.din("wfo", [L * 16, 128, 44 * 128])
    yT = nc.dram_tensor("yT", [D, SEQ], F32, kind="ExternalOutput").ap()

    xT = k.dscr("xT", [D, T])
    uT = k.dscr("uT", [D, T])
    hT = k.dscr("hT", [D, T], BF16)
    qT = k.dscr("qT", [DK, T])
    kT = k.dscr("kT", [DK, T])
    gT = k.dscr("gT", [DV, T])
    alrT = k.dscr("alrT", [128, T], BF16)
    pT = k.dscr("pT", [DPOOL, T])
    bgT = k.dscr("bgT", [2 * D, T])
    vtm = k.dscr("vtm", [T, DV], BF16)
    oT = k.dscr("oT", [DV, T])
    ogT = k.dscr("ogT", [DV, T], BF16)
    dT = k.dscr("dT", [DPOOL, T], BF16)
    eT = k.dscr("eT", [DPOOL, T], BF16)
    t1T = k.dscr("t1T", [D, T])
    mT = k.dscr("mT", [D, T], BF16)
    hidT = k.dscr("hidT", [DFF, T], BF16)

    sb = nc.alloc_sbuf_tensor
    ones_bf = sb("ones_bf", [128, 128], BF16)
    ones_f = sb("ones_f", [128, 128], F32)
    ident = sb("ident_sb", [128, 128], BF16)
    masks = sb("masks_sb", [64, 128], F32)
    rmask = sb("rmask_sb", [128, 512], F32)
    invc = sb("invc_sb", [128, 1536], F32)
    lv = sb("lv_sb", [128, LV_N], F32)
    negb = sb("negb_sb", [128, 16], F32)
    mods = sb("mods_sb", [128, 96, 2], F32)
    ops1 = sb("ops1_sb", [128, 96, 2], F32)
    scin = sb("scin_sb", [128, 16, 2], BF16)
    ctmp = sb("ctmp_sb", [128, 32], F32)
    wdu_sb = sb("wdu_sb", [16, 2048], BF16)
    epsr = sb("epsr_sb", [128, 1], F32)
    epsl = sb("epsl_sb", [128, 1], F32)
    ln16 = sb("ln16_sb", [128, 1], F32)
    psf = [nc.alloc_psum_tensor(f"psf{i}", [128, 512], F32) for i in range(7)]
    psb = nc.alloc_psum_tensor("psb", [128, 1024], BF16)
    PSF = [f"psf{i}" for i in range(7)]

    uid = [0]

    def sbt(name, shape, dt):
        uid[0] += 1
        return nc.sbuf_tensor(f"{name}_u{uid[0]}", shape, dt)

    def alt_eng():
        k.alt ^= 1
        return "act" if k.alt else "dve"

    P.op("dve", "memset", ones_bf[:], 1.0, writes=["ones_bf"])
    P.op("dve", "memset", ones_f[:], 1.0, writes=["ones_f"])
    P.op("dve", "memset", epsr[:], RMS_EPS, writes=["epsr"])
    P.op("dve", "memset", epsl[:], LN_EPS, writes=["epsl"])
    P.op("dve", "memset", ln16[:], math.log(1.0 / 16.0), writes=["ln16"])
    P.dma("pool", ident[:], ident_d, writes=["ident"])
    P.dma("sp", masks[:], masks_d, writes=["masks"])
    P.dma("sp", rmask[:], rmask_d, writes=["rmask"])
    P.dma("sp", invc[:], invc_d, writes=["invc"])
    P.dma("sp", ctmp[:], cT, writes=["ctmp"])
    P.op("act", "activation", out=scin[:].rearrange("p a b -> p (a b)"), in_=ctmp[:], func=AF.Silu,
         reads=["ctmp"], writes=["scin"])

    def copy_dram(dst, src, rows, dt_bytes=4):
        for r0 in range(0, rows, 128):
            P.dma("sp", dst[r0:r0 + 128, :], src[r0:r0 + 128, :], reads=["D:" + src.name], writes=["D:" + dst.name])

    def tgroups(maxtok):
        gs, cur, tot = [], [], 0
        for s in SUBS:
            if cur and tot + s[1] > maxtok:
                gs.append(cur)
                cur, tot = [], 0
            cur.append(s)
            tot += s[1]
        gs.append(cur)
        return gs

    def linear(name, xin, xname, KC, wblk, wrow0, NB, nw, maxtok, pre, epi, kcs=None):
        nkc = KC if kcs is None else len(kcs(0))
        with ExitStack() as es:
            xs = es.enter_context(sbt(f"{name}_x", [128, KC, maxtok], BF16))
            wbs = [es.enter_context(sbt(f"{name}_w{i}", [128, nkc, nw * 128], BF16)) for i in range(3)]
            xv = xin.rearrange("(kc p) t -> p kc t", p=128)
            bank = 0
            widx = 0
            for grp in tgroups(maxtok):
                g0 = grp[0][0]
                gw = sum(s[1] for s in grp)
                step = max(1, KC // 4)
                for c0 in range(0, KC, step):
                    P.dma("sp", xs[:, c0:c0 + step, 0:gw], xv[:, c0:c0 + step, g0:g0 + gw],
                          reads=["D:" + xname], writes=[f"{name}_x{c0 // step}"])
                items = [(nb_, t0_, tw_) for nb_ in range(NB) for (t0_, tw_) in grp]
                PF = 4
                pre_i = 0
                it_i = 0
                if pre is not None:
                    while pre_i < min(PF, len(items)):
                        pre(*items[pre_i])
                        pre_i += 1
                for nb in range(NB):
                    wb = wbs[widx % 3]
                    wn = f"{name}_w{widx % 3}"
                    widx += 1
                    per = nkc * nw * 128
                    wsrc = wblk[wrow0 + nb]
                    cstep = 2048
                    wflat = wb[:].rearrange("p a b -> p (a b)")
                    for c0 in range(0, per, cstep):
                        c1 = min(per, c0 + cstep)
                        P.dma("pool", wflat[:, c0:c1], wsrc[:, c0:c1], writes=[wn])
                    kl = list(range(KC)) if kcs is None else kcs(nb)
                    for (t0, tw) in grp:
                        if pre is not None and pre_i < len(items):
                            pre(*items[pre_i])
                            pre_i += 1
                        it_i += 1
                        pss = []
                        for j in range(nw):
                            b = bank % 6
                            bank += 1
                            pss.append(b)
                            for i, kc in enumerate(kl):
                                last = i == len(kl) - 1
                                P.op("pe", "matmul",
                                    psf[b][:, 0:tw], wb[:, i, j * 128:(j + 1) * 128], xs[:, kc, t0 - g0:t0 - g0 + tw],
                                    start=(i == 0), stop=last,
                                    reads=[wn, f"{name}_x{kc // step}"], writes=[PSF[b]], inc=last)
                        epi(nb, t0, tw, pss)
        P.barrier()

    class Ring:
        def __init__(self, es, name, n, shape, dt):
            self.t = [es.enter_context(sbt(f"{name}{i}", shape, dt)) for i in range(n)]
            self.names = [f"{name}{i}" for i in range(n)]
            self.i = 0

        def next(self):
            j = self.i % len(self.t)
            self.i += 1
            return self.t[j], self.names[j]

    def col_of(t0):
        return 1 if t0 < CTX else 0

    for l in range(L):
        P.dma("sp", lv[:], lvec[l], writes=["lv"])
        P.dma("pool", wdu_sb[:], wdu[l], writes=["wdu"])
        P.op("dve", "tensor_scalar", negb[:], lv[:, LV_BDEC:LV_BDEC + 16], -1.0, None, ALU.mult,
             reads=["lv"], writes=["negb"])
        with ExitStack() as es:
            wbs = [es.enter_context(sbt(f"ada_w{i}", [128, 16, 128], BF16)) for i in range(3)]
            for nb in range(96):
                wb = wbs[nb % 3]
                wn = f"ada_w{nb % 3}"
                P.dma("pool", wb[:].rearrange("p a b -> p (a b)"), wada[l * 96 + nb], writes=[wn])
                for kc in range(16):
                    P.op("pe", "matmul",
                        psf[6][:, nb * 2:nb * 2 + 2], wb[:, kc, :], scin[:, kc, :], start=(kc == 0), stop=(kc == 15),
                        reads=[wn, "scin"], writes=[PSF[6]], inc=(kc == 15))
            P.op("dve", "tensor_tensor",
                mods[:], psf[6][:, 0:192].rearrange("p (a b) -> p a b", b=2),
                lv[:, LV_BADA:LV_BADA + 96].unsqueeze(2).to_broadcast([128, 96, 2]), ALU.add,
                reads=[PSF[6], "lv"], writes=["mods"])
            P.op("dve", "tensor_scalar", ops1[:], mods[:], 1.0, None, ALU.add, reads=["mods"], writes=["ops1"])
        P.barrier()

        xsrc, xsn = (xT0, "xT0") if l == 0 else (xT, "xT")

        def modulate(src, srcname, m_sh, m_sc):
            with ExitStack() as es:
                xr = Ring(es, "mod_x", 3, [128, 16, 512], F32)
                hr = Ring(es, "mod_h", 2, [128, 16, 512], BF16)
                sv = src.rearrange("(c p) t -> p c t", p=128)
                hv = hT.rearrange("(c p) t -> p c t", p=128)
                loaded = {}

                def mload(i):
                    t0_, tw_ = SUBS[i]
                    xb_, xn_ = xr.next()
                    P.dma("sp", xb_[:, :, 0:tw_], sv[:, :, t0_:t0_ + tw_], reads=["D:" + srcname], writes=[xn_])
                    loaded[i] = (xb_, xn_)
                mload(0)
                for si, (t0, tw) in enumerate(SUBS):
                    if si + 1 < len(SUBS):
                        mload(si + 1)
                    xb, xn = loaded.pop(si)
                    hb, hn = hr.next()
                    col = col_of(t0)
                    for c in range(16):
                        eng = alt_eng()
                        sc_ap = ops1[:, m_sc * 16 + c, col:col + 1]
                        sh_ap = mods[:, m_sh * 16 + c, col:col + 1]
                        if eng == "act":
                            P.op("act", "activation",
                                out=hb[:, c, 0:tw], in_=xb[:, c, 0:tw], func=AF.Identity, bias=sh_ap, scale=sc_ap,
                                reads=[xn, "mods", "ops1"], writes=[hn])
                        else:
                            P.op("dve", "tensor_scalar",
                                hb[:, c, 0:tw], xb[:, c, 0:tw], sc_ap, sh_ap, ALU.mult, ALU.add,
                                reads=[xn, "mods", "ops1"], writes=[hn])
                    P.dma("sp", hv[:, :, t0:t0 + tw], hb[:, :, 0:tw], reads=[hn], writes=["D:hT"])
            P.barrier()

        modulate(xsrc, xsn, 0, 1)

        segs = [(qT, 0, F32, 8), (kT, 0, F32, 8), (gT, 0, F32, 16), (alrT, 0, BF16, 1), (pT, 0, F32, 8), (bgT, 0, F32, 32)]
        segmap = []
        for (dst, _, dt_, n) in segs:
            for i in range(n):
                segmap.append((dst, i, dt_))
        with ExitStack() as es:
            r32 = Ring(es, "in_s32", 4, [128, 512], F32)
            r16 = Ring(es, "in_s16", 2, [128, 512], BF16)

            def epi_in(nb, t0, tw, pss):
                dst, i, dt_ = segmap[nb]
                st, sn = (r32 if dt_ == F32 else r16).next()
                eng = alt_eng()
                if eng == "act":
                    P.op("act", "activation", out=st[:, 0:tw], in_=psf[pss[0]][:, 0:tw], func=AF.Copy,
                         reads=[PSF[pss[0]]], writes=[sn])
                else:
                    P.op("dve", "tensor_copy", st[:, 0:tw], psf[pss[0]][:, 0:tw], reads=[PSF[pss[0]]], writes=[sn])
                P.dma("sp", dst[i * 128:(i + 1) * 128, t0:t0 + tw], st[:, 0:tw], reads=[sn], writes=["D:" + dst.name])

            linear("lin_in", hT, "hT", 16, win_fm, l * NB_IN, NB_IN, 1, 2304, None, epi_in)

        with ExitStack() as es:
            xs = es.enter_context(sbt("v_x", [128, 16, 2304], BF16))
            wbs = [es.enter_context(sbt(f"v_w{i}", [128, 16, 512], BF16)) for i in range(2)]
            vr = Ring(es, "v_s", 4, [128, 512], BF16)
            xv = hT.rearrange("(kc p) t -> p kc t", p=128)
            bank = 0
            widx = 0
            for (g0, gw) in ((0, 2304), (2304, 2048)):
                for c0 in range(0, 16, 4):
                    P.dma("sp", xs[:, c0:c0 + 4, 0:gw], xv[:, c0:c0 + 4, g0:g0 + gw], reads=["D:hT"], writes=["v_x"])
                for cb in range(4):
                    wb = wbs[widx % 2]
                    wn = f"v_w{widx % 2}"
                    widx += 1
                    wflat = wb[:].rearrange("p a b -> p (a b)")
                    for c0 in range(0, 16 * 512, 2048):
                        P.dma("pool", wflat[:, c0:c0 + 2048], win_v[l * 4 + cb][:, c0:c0 + 2048], writes=[wn])
                    for tt in range(gw // 128):
                        b = bank % 6
                        bank += 1
                        for kc in range(16):
                            P.op("pe", "matmul",
                                psf[b][:, 0:512], xs[:, kc, tt * 128:(tt + 1) * 128], wb[:, kc, :],
                                start=(kc == 0), stop=(kc == 15), reads=[wn, "v_x"], writes=[PSF[b]], inc=(kc == 15))
                        st, sn = vr.next()
                        eng = alt_eng()
                        if eng == "act":
                            P.op("act", "activation", out=st[:], in_=psf[b][:], func=AF.Copy,
                                 reads=[PSF[b]], writes=[sn])
                        else:
                            P.op("dve", "tensor_copy", st[:], psf[b][:], reads=[PSF[b]], writes=[sn])
                        r0 = g0 + tt * 128
                        P.dma("sp", vtm[r0:r0 + 128, cb * 512:(cb + 1) * 512], st[:], reads=[sn], writes=["D:vtm"])
        P.barrier()

        with ExitStack() as es:
            GB = 256
            Xs = [es.enter_context(sbt(f"g_S{i}", [128, 2, 512], F32)) for i in range(2)]
            q_r = Ring(es, "g_q", 2, [128, 2, GB], F32)
            k_r = Ring(es, "g_k", 2, [128, 2, GB], F32)
            a_r = Ring(es, "g_alr", 2, [16, GB], BF16)
            v_r = Ring(es, "g_v", 4, [64, 4, 512], BF16)
            of_r = Ring(es, "g_of", 4, [128, 4, GB], F32)
            gg_r = Ring(es, "g_g", 4, [128, 4, GB], F32)
            la = es.enter_context(sbt("g_la", [128, 2, GB], F32))
            cum = es.enter_context(sbt("g_cum", [128, 2, GB], F32))
            Aa = es.enter_context(sbt("g_A", [128, 2, GB], F32))
            Bb = es.enter_context(sbt("g_B", [128, 2, GB], F32))
            kin_r = Ring(es, "g_kin", 2, [128, 2, GB], BF16)
            qin_r = Ring(es, "g_qin", 2, [128, 2, GB], BF16)
            qint_r = Ring(es, "g_qint", 4, [128, 2, GB], BF16)
            refv = es.enter_context(sbt("g_ref", [128, 2, 4], F32))
            lastv = es.enter_context(sbt("g_last", [128, 2, 4], F32))
            er = es.enter_context(sbt("g_er", [128, 2, 4], F32))
            elr_r = Ring(es, "g_elr", 2, [128, 2, 4], F32)
            el_r = Ring(es, "g_el", 3, [128, 2, 4], F32)
            ktm_r = Ring(es, "g_ktm", 2, [64, 4, 256], BF16)
            sT_r = Ring(es, "g_sT", 3, [64, 4, 64], BF16)
            tkv_r = Ring(es, "g_tkv", 2, [128, 4, 2, 512], F32)
            sball = [es.enter_context(sbt(f"g_Sb{i}", [128, 4, 2, 512], BF16)) for i in range(3)]
            osb_r = Ring(es, "g_o", 2, [128, 4, GB], F32)
            sq = es.enter_context(sbt("g_sq", [128, 4, GB], BF16))
            rs = es.enter_context(sbt("g_rs", [128, GB], F32))
            og_r = Ring(es, "g_og", 2, [128, 4, GB], BF16)
            KVB = ((2, 3), (5, 6))
            OB = (4, 4)
            LN16 = math.log(1.0 / 16.0)
            lat = [(CTX + GB * i, GB) for i in range(SEQ // GB)]
            for h in range(4):
                qv = qT[h * 256:(h + 1) * 256, :].rearrange("(dc p) t -> p dc t", p=128)
                kv_ = kT[h * 256:(h + 1) * 256, :].rearrange("(dc p) t -> p dc t", p=128)
                ov = oT[h * 512:(h + 1) * 512, :].rearrange("(vc p) t -> p vc t", p=128)
                gv = gT[h * 512:(h + 1) * 512, :].rearrange("(vc p) t -> p vc t", p=128)
                ogv = ogT[h * 512:(h + 1) * 512, :].rearrange("(vc p) t -> p vc t", p=128)
                for dr in (0, 1):
                    blocks = [(0, CTX)] + (lat if dr == 0 else lat[::-1])
                    ri, li = (31, 63) if dr == 0 else (32, 0)
                    P.op("dve", "memset", Xs[0][:], 0.0, writes=["g_S0"])
                    P.op("dve", "memset", sball[0][:, 0, :, :], 0.0, writes=["g_Sb0"])
                    state = dict(si=0)
                    ctxs = {}

                    def s1(n, h=h, dr=dr, blocks=blocks, ri=ri, li=li, qv=qv, kv_=kv_, ov=ov, gv=gv):
                        t0, tw = blocks[n]
                        nch = tw // 64
                        par = n % 2
                        c = dict(t0=t0, tw=tw, nch=nch, par=par)
                        ctxs[n] = c
                        qb, qn = q_r.next()
                        kb, kn = k_r.next()
                        ab, an = a_r.next()
                        vb, vn = v_r.next()
                        c.update(vb=vb, vn=vn)
                        P.dma("sp", qb[:, :, 0:tw], qv[:, :, t0:t0 + tw], reads=["D:qT"], writes=[qn])
                        yield
                        P.dma("sp", kb[:, :, 0:tw], kv_[:, :, t0:t0 + tw], reads=["D:kT"], writes=[kn])
                        yield
                        P.dma("sp", ab[:, 0:tw], alrT[dr * 16:(dr + 1) * 16, t0:t0 + tw], reads=["D:alrT"], writes=[an])
                        yield
                        P.dma("sp", vb[:, 0:nch, :],
                              vtm[t0:t0 + tw, h * 512:(h + 1) * 512].rearrange("(c p) n -> p c n", p=64),
                              reads=["D:vtm"], writes=[vn])
                        yield
                        if dr == 1:
                            ofb, ofn = of_r.next()
                            ggb, ggn = gg_r.next()
                            c.update(ofb=ofb, ofn=ofn, ggb=ggb, ggn=ggn)
                            P.dma("sp", ofb[:, :, 0:tw], ov[:, :, t0:t0 + tw], reads=["D:oT"], writes=[ofn])
                            yield
                            P.dma("sp", ggb[:, :, 0:tw], gv[:, :, t0:t0 + tw], reads=["D:gT"], writes=[ggn])
                            yield
                        for dc in range(2):
                            col = (h * 2 + dc) * 128
                            P.op("pe", "matmul", psf[0][:, dc * 256:dc * 256 + tw],
                                 wdu_sb[0:16, dr * 1024 + col:dr * 1024 + col + 128], ab[0:16, 0:tw],
                                 start=True, stop=True, reads=["wdu", an], writes=["psz"])
                            yield
                            bi = dr * 8 + h * 2 + dc
                            P.op("act", "activation", out=la[:, dc, 0:tw], in_=psf[0][:, dc * 256:dc * 256 + tw], func=AF.Exp,
                                 bias=negb[:, bi:bi + 1], scale=-1.0, reads=["psz", "negb"], writes=["g_la"])
                            yield
                        P.op("act", "activation", out=la[:, :, 0:tw], in_=la[:, :, 0:tw], func=AF.Ln, bias=1.0,
                             reads=["g_la"], writes=["g_la"])
                        yield
                        for dc in range(2):
                            P.op("dve", "tensor_tensor_scan", cum[:, dc, 0:tw], rmask[:, 0:tw], la[:, dc, 0:tw], 0.0, ALU.mult, ALU.add,
                                 reads=["g_la", "rmask"], writes=["g_cum"])
                            yield

                        def cv(tl):
                            return tl[:, :, 0:tw].rearrange("p d (c i) -> p d c i", i=64)

                        def bc(sm):
                            return sm[:, :, 0:nch].unsqueeze(3).to_broadcast([128, 2, nch, 64])
                        if dr == 1:
                            P.op("dve", "tensor_copy", lastv[:, :, 0:nch], cv(cum)[:, :, :, 63], reads=["g_cum"], writes=["g_last"])
                            yield
                            P.op("dve", "tensor_tensor", cum[:, :, 0:tw], la[:, :, 0:tw], cum[:, :, 0:tw], ALU.subtract,
                                 reads=["g_la", "g_cum"], writes=["g_cum"])
                            yield
                            P.op("dve", "tensor_tensor", cv(cum), cv(cum), bc(lastv), ALU.add, reads=["g_cum", "g_last"], writes=["g_cum"])
                            yield
                        elb, eln = el_r.next()
                        kin, kinn = kin_r.next()
                        elr, elrn = elr_r.next()
                        c.update(elb=elb, eln=eln, kin=kin, kinn=kinn, elr=elr, elrn=elrn)
                        P.op("dve", "tensor_copy", refv[:, :, 0:nch], cv(cum)[:, :, :, ri], reads=["g_cum"], writes=["g_ref"])
                        yield
                        P.op("dve", "tensor_copy", lastv[:, :, 0:nch], cv(cum)[:, :, :, li], reads=["g_cum"], writes=["g_last"])
                        yield
                        P.op("act", "activation", out=Aa[:, :, 0:tw], in_=cum[:, :, 0:tw], func=AF.Exp, bias=ln16[:, 0:1], scale=-1.0 / 16.0,
                             reads=["g_cum", "ln16"], writes=["g_A"])
                        yield
                        qinb, qinn = qin_r.next()
                        qintb, qintn = qint_r.next()
                        c.update(qintb=qintb, qintn=qintn)
                        P.op("dve", "tensor_tensor", qintb[:, :, 0:tw], qb[:, :, 0:tw], Aa[:, :, 0:tw], ALU.mult, reads=[qn, "g_A"], writes=[qintn])
                        yield
                        P.op("dve", "tensor_tensor", cv(cum), cv(cum), bc(refv), ALU.subtract, reads=["g_cum", "g_ref"], writes=["g_cum"])
                        yield
                        P.op("dve", "tensor_tensor", elr[:, :, 0:nch], lastv[:, :, 0:nch], refv[:, :, 0:nch], ALU.subtract,
                             reads=["g_last", "g_ref"], writes=[elrn])
                        yield
                        P.op("act", "activation", out=Aa[:, :, 0:tw], in_=cum[:, :, 0:tw], func=AF.Exp, bias=ln16[:, 0:1], scale=-1.0 / 16.0,
                             reads=["g_cum", "ln16", qintn], writes=["g_A"])
                        yield
                        P.op("act", "activation", out=Bb[:, :, 0:tw], in_=cum[:, :, 0:tw], func=AF.Exp, scale=1.0 / 16.0,
                             reads=["g_cum"], writes=["g_B"])
                        yield
                        P.op("act", "activation", out=elb[:, :, 0:nch], in_=lastv[:, :, 0:nch], func=AF.Exp, scale=-1.0 / 16.0,
                             reads=["g_last"], writes=[eln])
                        yield
                        P.op("act", "activation", out=elr[:, :, 0:nch], in_=elr[:, :, 0:nch], func=AF.Exp, scale=-1.0 / 16.0,
                             reads=[elrn], writes=[elrn])
                        yield
                        P.op("dve", "tensor_tensor", kin[:, :, 0:tw], kb[:, :, 0:tw], Bb[:, :, 0:tw], ALU.mult, reads=[kn, "g_B"], writes=[kinn])
                        yield
                        P.op("dve", "tensor_tensor", qinb[:, :, 0:tw], qb[:, :, 0:tw], Aa[:, :, 0:tw], ALU.mult, reads=[qn, "g_A"], writes=[qinn])
                        yield
                        order = list(range(nch)) if dr == 0 else list(range(nch))[::-1]
                        c.update(order=order, qinb=qinb, qinn=qinn)

                    def s1b(n, dr=dr):
                        c = ctxs[n]
                        nch, par, vb, vn, order = c["nch"], c["par"], c["vb"], c["vn"], c["order"]
                        qinb, qinn = c["qinb"], c["qinn"]
                        kin, kinn, elr, elrn = c["kin"], c["kinn"], c["elr"], c["elrn"]
                        ktm, ktn = ktm_r.next()
                        sT, sTn = sT_r.next()
                        tkv, tkn = tkv_r.next()
                        c.update(sT=sT, sTn=sTn, tkv=tkv, tkn=tkn)
                        for kk, ch in enumerate(order):
                            cs = slice(ch * 64, (ch + 1) * 64)
                            for dc in range(2):
                                P.op("pe", "transpose", psb[0:64, kk * 256 + dc * 128:kk * 256 + (dc + 1) * 128], kin[:, dc, cs], ident[:, :],
                                     reads=[kinn, "ident"], writes=["psb"], inc=(kk == nch - 1 and dc == 1))
                        P.op("act", "activation", out=ktm[:, 0:nch, :].rearrange("p a b -> p (a b)"), in_=psb[0:64, 0:nch * 256], func=AF.Copy,
                             reads=["psb"], writes=[ktn])
                        yield
                        sreg = "pss"
                        for kk, ch in enumerate(order):
                            cs = slice(ch * 64, (ch + 1) * 64)
                            for dc in range(2):
                                P.op("pe", "matmul", psf[1][0:64, kk * 64:(kk + 1) * 64], kin[:, dc, cs], qinb[:, dc, cs],
                                     start=(dc == 0), stop=(dc == 1), reads=[kinn, qinn], writes=[sreg],
                                     inc=(kk == nch - 1 and dc == 1))
                        P.op("dve", "tensor_tensor", sT[:, 0:nch, :],
                             psf[1][0:64, 0:nch * 64].rearrange("p (a b) -> p a b", b=64),
                             masks[:, dr * 64:(dr + 1) * 64].unsqueeze(1).to_broadcast([64, nch, 64]), ALU.mult,
                             reads=[sreg, "masks"], writes=[sTn])
                        yield
                        for kk, ch in enumerate(order):
                            for dc in range(2):
                                b = KVB[kk % 2][dc]
                                P.op("pe", "matmul", psf[b][:, 0:512], ktm[0:64, kk, dc * 128:(dc + 1) * 128], vb[0:64, ch, :],
                                     start=True, stop=True, reads=[ktn, vn], writes=[PSF[b]])
                                yield
                                if dc == 0:
                                    P.op("act", "activation", out=tkv[:, kk, dc, :], in_=psf[b][:, 0:512], func=AF.Copy,
                                         scale=elr[:, dc, ch:ch + 1], reads=[PSF[b], elrn], writes=[tkn])
                                    yield
                                else:
                                    P.op("dve", "tensor_scalar", tkv[:, kk, dc, :], psf[b][:, 0:512], elr[:, dc, ch:ch + 1], None, ALU.mult,
                                         reads=[PSF[b], elrn], writes=[tkn])
                                    yield

                    def s2(n):
                        c = ctxs[n]
                        nch = c["nch"]
                        sbi = n % 3
                        for kk, ch in enumerate(c["order"]):
                            i = state["si"]
                            src, dst = Xs[i], Xs[1 - i]
                            for dc in range(2):
                                P.op("dve", "scalar_tensor_tensor", dst[:, dc, :], src[:, dc, :], c["elb"][:, dc, ch:ch + 1],
                                     c["tkv"][:, kk, dc, :], ALU.mult, ALU.add,
                                     reads=[f"g_S{i}", c["eln"], c["tkn"]], writes=[f"g_S{1 - i}"])
                            if kk < nch - 1:
                                tp, tk = sbi, kk + 1
                            else:
                                tp, tk = (sbi + 1) % 3, 0
                            P.op("act", "activation", out=sball[tp][:, tk, :, :], in_=dst[:, :, :], func=AF.Copy,
                                 reads=[f"g_S{1 - i}"], writes=[f"g_Sb{tp}"])
                            state["si"] = 1 - i
                            yield

                    def s3(n, h=h, dr=dr, ov=ov, ogv=ogv):
                        c = ctxs.pop(n)
                        par, nch, t0, tw = c["par"], c["nch"], c["t0"], c["tw"]
                        sbi = n % 3
                        vb, vn, sT, sTn, qintb, qintn = c["vb"], c["vn"], c["sT"], c["sTn"], c["qintb"], c["qintn"]
                        ob, on = osb_r.next()
                        for kk, ch in enumerate(c["order"]):
                            cs = slice(ch * 64, (ch + 1) * 64)
                            r0 = 0
                            ob_ = OB[kk % 2]
                            oreg = PSF[ob_]
                            for vc in range(4):
                                P.op("pe", "matmul", psf[ob_][:, r0 + vc * 64:r0 + (vc + 1) * 64], vb[0:64, ch, vc * 128:(vc + 1) * 128],
                                     sT[0:64, kk, :], start=True, stop=False, reads=[vn, sTn], writes=[oreg], inc=False)
                                for dc in range(2):
                                    P.op("pe", "matmul", psf[ob_][:, r0 + vc * 64:r0 + (vc + 1) * 64],
                                         sball[sbi][:, kk, dc, vc * 128:(vc + 1) * 128], qintb[:, dc, cs],
                                         start=False, stop=(dc == 1), reads=[f"g_Sb{sbi}", qintn], writes=[oreg],
                                         inc=(vc == 3 and dc == 1))
                            pov = psf[ob_][:, r0:r0 + 256].rearrange("p (v i) -> p v i", i=64)
                            if dr == 0:
                                P.op("act", "activation", out=ob[:, :, cs], in_=pov, func=AF.Copy, reads=[oreg], writes=[on])
                                yield
                            else:
                                P.op("dve", "tensor_tensor", ob[:, :, cs], pov, c["ofb"][:, :, cs], ALU.add, reads=[oreg, c["ofn"]], writes=[on])
                                yield
                        if dr == 0:
                            P.dma("sp", ov[:, :, t0:t0 + tw], ob[:, :, 0:tw], reads=[on], writes=["D:oT"])
                            yield
                        else:
                            ggb, ggn = c["ggb"], c["ggn"]
                            ogb, ogn = og_r.next()
                            rreg = PSF[4]
                            P.op("act", "activation", out=sq[:, :, 0:tw], in_=ob[:, :, 0:tw], func=AF.Square, reads=[on], writes=["g_sq"])
                            yield
                            for vc in range(4):
                                P.op("pe", "matmul", psf[4][:, 0:tw], ones_bf[:, :], sq[:, vc, 0:tw],
                                     start=(vc == 0), stop=(vc == 3), reads=["ones_bf", "g_sq"], writes=[rreg], inc=(vc == 3))
                            P.op("act", "activation", out=rs[:, 0:tw], in_=psf[4][:, 0:tw], func=AF.Ln,
                                 bias=epsr[:, 0:1], scale=1.0 / 512.0, reads=[rreg, "epsr"], writes=["g_rs"])
                            yield
                            P.op("act", "activation", out=rs[:, 0:tw], in_=rs[:, 0:tw], func=AF.Exp, scale=-0.5,
                                 reads=["g_rs"], writes=["g_rs"])
                            yield
                            P.op("act", "activation", out=ggb[:, :, 0:tw], in_=ggb[:, :, 0:tw], func=AF.Silu, reads=[ggn], writes=[ggn])
                            yield
                            for vc in range(4):
                                gi = LV_GAIN + h * 4 + vc
                                P.op("dve", "scalar_tensor_tensor", ob[:, vc, 0:tw], ob[:, vc, 0:tw], lv[:, gi:gi + 1], rs[:, 0:tw],
                                     ALU.mult, ALU.mult, reads=[on, "lv", "g_rs"], writes=[on])
                                yield
                            P.op("dve", "tensor_tensor", ogb[:, :, 0:tw], ob[:, :, 0:tw], ggb[:, :, 0:tw], ALU.mult,
                                 reads=[on, ggn], writes=[ogn])
                            yield
                            P.dma("sp", ogv[:, :, t0:t0 + tw], ogb[:, :, 0:tw], reads=[ogn], writes=["D:ogT"])
                            yield

                    def rr(*gens):
                        gens = [g for g in gens if g is not None]
                        while gens:
                            for g in list(gens):
                                try:
                                    next(g)
                                except StopIteration:
                                    gens.remove(g)

                    rr(s1(0))
                    NBk = len(blocks)
                    for n in range(NBk):
                        rr(s1b(n),
                           s1(n + 1) if n + 1 < NBk else None,
                           s3(n - 2) if n >= 2 else None,
                           s2(n - 1) if n >= 1 else None)
                    rr(s2(NBk - 1), s3(NBk - 2))
                    rr(s3(NBk - 1))
        P.barrier()

        if STOP == "gla":
            break
        with ExitStack() as es:
            p_r = Ring(es, "pl_p", 2, [128, T], F32)
            d_r = Ring(es, "pl_d", 2, [128, T], BF16)
            xp = es.enter_context(sbt("pl_xp", [128, 80, 80], F32))
            ba = es.enter_context(sbt("pl_a", [128, 6400], F32))
            bb = es.enter_context(sbt("pl_b", [128, 6400], F32))
            yp = es.enter_context(sbt("pl_yp", [128, 80, 64], F32))
            cp = es.enter_context(sbt("pl_cp", [128, 272], F32))
            P.op("dve", "memset", xp[:], 0.0, writes=["pl_xp"])
            P.op("dve", "memset", yp[:], 0.0, writes=["pl_yp"])
            P.op("dve", "memset", cp[:], 0.0, writes=["pl_cp"])
            for pc in range(8):
                wi = pc // 2
                w = WINS[wi]
                m = int(math.log2(w))
                lo = w // 2
                if pc == 0:
                    pl_loaded = {}
                    pb0, pn0 = p_r.next()
                    P.dma("sp", pb0[:], pT[0:128, :], reads=["D:pT"], writes=[pn0])
                    pl_loaded[0] = (pb0, pn0)
                pb, pn = pl_loaded.pop(pc)
                db, dn = d_r.next()
                if pc + 1 < 8:
                    pb1, pn1 = p_r.next()
                    P.dma("sp", pb1[:], pT[(pc + 1) * 128:(pc + 2) * 128, :], reads=["D:pT"], writes=[pn1])
                    pl_loaded[pc + 1] = (pb1, pn1)
                pimg = pb[:, CTX:T].rearrange("p (r c) -> p r c", c=64)
                P.op("act", "activation", out=xp[:, 8:72, 8:72], in_=pimg, func=AF.Copy,
                     reads=[pn], writes=["pl_xp"])
                a80 = ba[:].rearrange("p (r c) -> p r c", c=80)
                b80 = bb[:].rearrange("p (r c) -> p r c", c=80)
                cur, curn, width = xp[:], "pl_xp", 80
                for s in range(m):
                    sh = 2 ** s
                    nw_ = width - sh
                    nxt, nxtn = (a80, "pl_a") if s % 2 == 0 else (b80, "pl_b")
                    P.op("dve", "tensor_tensor",
                        nxt[:, 8:72, 0:nw_], cur[:, 8:72, 0:nw_], cur[:, 8:72, sh:sh + nw_], ALU.add,
                        reads=[curn], writes=[nxtn])
                    cur, curn, width = nxt, nxtn, nw_
                icol = invc[:, wi * 64:(wi + 1) * 64].unsqueeze(1).to_broadcast([128, 64, 64])
                P.op("dve", "tensor_tensor",
                    yp[:, 8:72, :], cur[:, 8:72, 8 - lo:8 - lo + 64], icol, ALU.mult, reads=[curn, "invc"], writes=["pl_yp"])
                a64 = ba[:, 0:5120].rearrange("p (r c) -> p r c", c=64)
                b64 = bb[:, 0:5120].rearrange("p (r c) -> p r c", c=64)
                cur, curn, height = yp[:], "pl_yp", 80
                for s in range(m):
                    sh = 2 ** s
                    nh = height - sh
                    nxt, nxtn = (a64, "pl_a") if s % 2 == 0 else (b64, "pl_b")
                    P.op("dve", "tensor_tensor",
                        nxt[:, 0:nh, :], cur[:, 0:nh, :], cur[:, sh:sh + nh, :], ALU.add, reads=[curn], writes=[nxtn])
                    cur, curn, height = nxt, nxtn, nh
                irow = invc[:, 256 + wi * 64:256 + (wi + 1) * 64].unsqueeze(2).to_broadcast([128, 64, 64])
                mean_n = "pl_b" if curn == "pl_a" else "pl_a"
                mean_t = b64 if curn == "pl_a" else a64
                P.op("dve", "tensor_tensor",
                    mean_t[:, 0:64, :], cur[:, 8 - lo:8 - lo + 64, :], irow, ALU.mult, reads=[curn, "invc"], writes=[mean_n])
                dimg = db[:, CTX:T].rearrange("p (r c) -> p r c", c=64)
                P.op("dve", "tensor_tensor",
                    dimg, mean_t[:, 0:64, :], pimg, ALU.subtract, reads=[mean_n, pn], writes=[dn])
                P.op("act", "activation", out=cp[:, 8:264], in_=pb[:, 0:CTX], func=AF.Copy,
                     reads=[pn], writes=["pl_cp"])
                cur, curn, width = cp[:], "pl_cp", 272
                for s in range(m):
                    sh = 2 ** s
                    nw_ = width - sh
                    nxt, nxtn = (ba, "pl_a") if s % 2 == 0 else (bb, "pl_b")
                    P.op("dve", "tensor_tensor",
                        nxt[:, 0:nw_], cur[:, 0:nw_], cur[:, sh:sh + nw_], ALU.add, reads=[curn], writes=[nxtn])
                    cur, curn, width = nxt[:, :], nxtn, nw_
                mean_n = "pl_b" if curn == "pl_a" else "pl_a"
                mean_c = bb if curn == "pl_a" else ba
                P.op("dve", "tensor_tensor",
                    mean_c[:, 0:CTX], cur[:, 8 - lo:8 - lo + CTX], invc[:, 512 + wi * 256:512 + (wi + 1) * 256], ALU.mult,
                    reads=[curn, "invc"], writes=[mean_n])
                P.op("dve", "tensor_tensor",
                    db[:, 0:CTX], mean_c[:, 0:CTX], pb[:, 0:CTX], ALU.subtract, reads=[mean_n, pn], writes=[dn])
                P.dma("sp", dT[pc * 128:(pc + 1) * 128, :], db[:], reads=[dn], writes=["D:dT"])
        P.barrier()

        with ExitStack() as es:
            r16 = Ring(es, "pg_s", 4, [128, 512], BF16)

            def epi_pg(nb, t0, tw, pss):
                st, sn = r16.next()
                P.op("act", "activation", out=st[:, 0:tw], in_=psf[pss[0]][:, 0:tw], func=AF.Copy,
                                                   scale=lv[:, LV_PSC + nb:LV_PSC + nb + 1],
                     reads=[PSF[pss[0]], "lv"], writes=[sn])
                P.dma("sp", eT[nb * 128:(nb + 1) * 128, t0:t0 + tw], st[:, 0:tw], reads=[sn], writes=["D:eT"])

            linear("lin_pg", dT, "dT", 8, wpg, l * 8, 8, 1, 4352, None, epi_pg, kcs=lambda nb: [2 * (nb // 2), 2 * (nb // 2) + 1])

        with ExitStack() as es:
            rb = Ring(es, "go_b", 6, [128, 512], F32)
            rsg = Ring(es, "go_s", 4, [128, 512], F32)
            pend = {}

            def pre_go(nb, t0, tw):
                bt, bn = rb.next()
                P.dma("sp", bt[:, 0:tw], bgT[nb * 128:(nb + 1) * 128, t0:t0 + tw], reads=["D:bgT"], writes=[bn])
                pend[(nb, t0)] = (bt, bn)

            def epi_go(nb, t0, tw, pss):
                bt, bn = pend.pop((nb, t0))
                st, sn = rsg.next()
                P.op("act", "activation", out=bt[:, 0:tw], in_=bt[:, 0:tw], func=AF.Sigmoid, reads=[bn], writes=[bn])
                P.op("dve", "tensor_tensor", st[:, 0:tw], psf[pss[0]][:, 0:tw], bt[:, 0:tw], ALU.mult,
                     reads=[PSF[pss[0]], bn], writes=[sn])
                P.dma("sp", t1T[nb * 128:(nb + 1) * 128, t0:t0 + tw], st[:, 0:tw], reads=[sn], writes=["D:t1T"])

            linear("lin_go", ogT, "ogT", 16, wgo, l * 16, 16, 1, 2304, pre_go, epi_go)

        with ExitStack() as es:
            rb = Ring(es, "po_b", 6, [128, 512], F32)
            rt = Ring(es, "po_t", 6, [128, 512], F32)
            rsg = Ring(es, "po_s", 4, [128, 512], BF16)
            pend = {}

            def pre_po(nb, t0, tw):
                bt, bn = rb.next()
                tt, tn = rt.next()
                P.dma("sp", bt[:, 0:tw], bgT[D + nb * 128:D + (nb + 1) * 128, t0:t0 + tw], reads=["D:bgT"], writes=[bn])
                P.dma("sp", tt[:, 0:tw], t1T[nb * 128:(nb + 1) * 128, t0:t0 + tw], reads=["D:t1T"], writes=[tn])
                pend[(nb, t0)] = (bt, bn, tt, tn)

            def epi_po(nb, t0, tw, pss):
                bt, bn, tt, tn = pend.pop((nb, t0))
                st, sn = rsg.next()
                P.op("act", "activation", out=bt[:, 0:tw], in_=bt[:, 0:tw], func=AF.Sigmoid, reads=[bn], writes=[bn])
                P.op("dve", "tensor_tensor", bt[:, 0:tw], psf[pss[0]][:, 0:tw], bt[:, 0:tw], ALU.mult,
                     reads=[PSF[pss[0]], bn], writes=[bn])
                P.op("dve", "tensor_tensor", st[:, 0:tw], bt[:, 0:tw], tt[:, 0:tw], ALU.add,
                     reads=[bn, tn], writes=[sn])
                P.dma("sp", mT[nb * 128:(nb + 1) * 128, t0:t0 + tw], st[:, 0:tw], reads=[sn], writes=["D:mT"])

            linear("lin_po", eT, "eT", 8, wpo, l * 16, 16, 1, 4352, pre_po, epi_po)

        def resid(es, name, m_gt, xsrc_, xsn_):
            rx = Ring(es, name + "_x", 6, [128, 512], F32)
            rt = Ring(es, name + "_t", 4, [128, 512], F32)
            pend = {}

            def pre(nb, t0, tw):
                xt, xn = rx.next()
                P.dma("sp", xt[:, 0:tw], xsrc_[nb * 128:(nb + 1) * 128, t0:t0 + tw], reads=["D:" + xsn_], writes=[xn])
                pend[(nb, t0)] = (xt, xn)

            def epi(nb, t0, tw, pss):
                xt, xn = pend.pop((nb, t0))
                tt, tn = rt.next()
                col = col_of(t0)
                P.op("act", "activation", out=tt[:, 0:tw], in_=psf[pss[0]][:, 0:tw], func=AF.Copy,
                                                   scale=mods[:, m_gt * 16 + nb, col:col + 1],
                     reads=[PSF[pss[0]], "mods"], writes=[tn])
                P.op("dve", "scalar_tensor_tensor", xt[:, 0:tw], xt[:, 0:tw], float(ALPHA), tt[:, 0:tw], ALU.mult, ALU.add,
                     reads=[xn, tn], writes=[xn])
                P.dma("sp", uT[nb * 128:(nb + 1) * 128, t0:t0 + tw], xt[:, 0:tw], reads=[xn], writes=["D:uT"])
            return pre, epi

        def layer_norm(g_off, b_off):
            with ExitStack() as es:
                ur = Ring(es, "ln_u", 2, [128, 16, 512], F32)
                sqr = Ring(es, "ln_sq", 2, [128, 16, 512], BF16)
                mr = Ring(es, "ln_mean", 2, [128, 512], F32)
                qr = Ring(es, "ln_msq", 2, [128, 512], F32)
                rr = Ring(es, "ln_rstd", 2, [128, 512], F32)
                uv = uT.rearrange("(c p) t -> p c t", p=128)
                xv = xT.rearrange("(c p) t -> p c t", p=128)
                loaded = {}

                def lload(i):
                    t0_, tw_ = SUBS[i]
                    ub_, un_ = ur.next()
                    P.dma("sp", ub_[:, :, 0:tw_], uv[:, :, t0_:t0_ + tw_], reads=["D:uT"], writes=[un_])
                    loaded[i] = (ub_, un_)
                lload(0)
                for si, (t0, tw) in enumerate(SUBS):
                    ub, un = loaded.pop(si)
                    sqt, sqn = sqr.next()
                    mean, mn = mr.next()
                    msq, qn_ = qr.next()
                    rstd, rn = rr.next()
                    pa, pb = (0, 1) if si % 2 == 0 else (2, 3)
                    P.op("act", "activation", out=sqt[:, :, 0:tw], in_=ub[:, :, 0:tw], func=AF.Square, reads=[un], writes=[sqn])
                    for c in range(16):
                        P.op("pe", "matmul", psf[pa][:, 0:tw], ones_f[:, :], ub[:, c, 0:tw], start=(c == 0), stop=(c == 15),
                             reads=["ones_f", un], writes=[PSF[pa]], inc=(c == 15))
                    for c in range(16):
                        P.op("pe", "matmul", psf[pb][:, 0:tw], ones_bf[:, :], sqt[:, c, 0:tw], start=(c == 0), stop=(c == 15),
                             reads=["ones_bf", sqn], writes=[PSF[pb]], inc=(c == 15))
                    P.op("act", "activation", out=mean[:, 0:tw], in_=psf[pa][:, 0:tw], func=AF.Copy, scale=1.0 / D,
                         reads=[PSF[pa]], writes=[mn])
                    P.op("dve", "tensor_tensor", msq[:, 0:tw], mean[:, 0:tw], mean[:, 0:tw], ALU.mult, reads=[mn], writes=[qn_])
                    P.op("dve", "scalar_tensor_tensor", rstd[:, 0:tw], psf[pb][:, 0:tw], 1.0 / D, msq[:, 0:tw], ALU.mult, ALU.subtract,
                         reads=[PSF[pb], qn_], writes=[rn])
                    P.op("act", "activation", out=rstd[:, 0:tw], in_=rstd[:, 0:tw], func=AF.Ln, bias=epsl[:, 0:1],
                         reads=[rn, "epsl"], writes=[rn])
                    P.op("act", "activation", out=rstd[:, 0:tw], in_=rstd[:, 0:tw], func=AF.Exp, scale=-0.5, reads=[rn], writes=[rn])
                    if si + 1 < len(SUBS):
                        lload(si + 1)
                    mb = mean[:, 0:tw].unsqueeze(1).to_broadcast([128, 16, tw])
                    rb_ = rstd[:, 0:tw].unsqueeze(1).to_broadcast([128, 16, tw])
                    P.op("dve", "tensor_tensor", ub[:, :, 0:tw], ub[:, :, 0:tw], mb, ALU.subtract, reads=[un, mn], writes=[un])
                    P.op("dve", "tensor_tensor", ub[:, :, 0:tw], ub[:, :, 0:tw], rb_, ALU.mult, reads=[un, rn], writes=[un])
                    for c in range(16):
                        P.op("act", "activation", out=ub[:, c, 0:tw], in_=ub[:, c, 0:tw], func=AF.Identity,
                             bias=lv[:, b_off + c:b_off + c + 1], scale=lv[:, g_off + c:g_off + c + 1],
                             reads=[un, "lv"], writes=[un])
                    P.dma("sp", xv[:, :, t0:t0 + tw], ub[:, :, 0:tw], reads=[un], writes=["D:xT"])
            P.barrier()

        with ExitStack() as es:
            pre, epi = resid(es, "ro", 2, xsrc, xsn)
            linear("lin_o", mT, "mT", 16, wo, l * 16, 16, 1, 2304, pre, epi)
        layer_norm(LV_LMG, LV_LMB)

        modulate(xT, "xT", 3, 4)
        with ExitStack() as es:
            rs_ = Ring(es, "fi_s", 4, [128, 512], F32)
            rh = Ring(es, "fi_h", 4, [128, 512], BF16)

            def epi_fi(nb, t0, tw, pss):
                st, sn = rs_.next()
                ht, hn = rh.next()
                P.op("act", "activation", out=st[:, 0:tw], in_=psf[pss[0]][:, 0:tw], func=AF.Silu,
                     reads=[PSF[pss[0]]], writes=[sn])
                P.op("dve", "tensor_tensor", ht[:, 0:tw], st[:, 0:tw], psf[pss[1]][:, 0:tw], ALU.mult,
                     reads=[sn, PSF[pss[1]]], writes=[hn])
                P.dma("sp", hidT[nb * 128:(nb + 1) * 128, t0:t0 + tw], ht[:, 0:tw], reads=[hn], writes=["D:hidT"])

            linear("lin_fi", hT, "hT", 16, wfi, l * 44, 44, 2, 2304, None, epi_fi)
        with ExitStack() as es:
            pre, epi = resid(es, "rf", 5, xT, "xT")
            linear("lin_fo", hidT, "hidT", 44, wfo, l * 16, 16, 1, 1024, pre, epi)
        layer_norm(LV_LFG, LV_LFB)

    for r0 in range(0, D, 128):
        P.dma("sp", yT[r0:r0 + 128, :], xT[r0:r0 + 128, CTX:T], reads=["D:xT"], writes=["D:yT"])
    P.barrier()
    P.build()
    return nc


def _blocks(W):
    Kd, N = W.shape
    return np.ascontiguousarray(W.reshape(Kd // 128, 128, N // 128, 128).transpose(2, 1, 0, 3)).reshape(
        N // 128, 128, (Kd // 128) * 128)


def _consts():
    ident = np.eye(128, dtype=np.float32)
    j = np.arange(64)[:, None]
    i = np.arange(64)[None, :]
    masks = np.concatenate([(i >= j), (i <= j)], axis=1).astype(np.float32)
    rmask = np.ones((128, 512), np.float32)
    rmask[:, ::64] = 0.0
    def inv(n, w):
        lo = w // 2
        hi = w - lo - 1
        idx = np.arange(n)
        return (1.0 / (np.minimum(idx + hi + 1, n) - np.maximum(idx - lo, 0))).astype(np.float32)
    row = np.concatenate([inv(64, w) for w in WINS] + [inv(64, w) for w in WINS] + [inv(CTX, w) for w in WINS])
    invc = np.ascontiguousarray(np.broadcast_to(row[None, :], (128, 1536))).astype(np.float32)
    return dict(ident=ident, masks=masks, rmask=rmask, invc=invc)


def _prep_weights(inp, L):
    f = lambda a: np.asarray(a, dtype=np.float32)
    out = {}
    wada, lvec, wdu, win_fm, win_v, wpg, wgo, wpo, wo, wfi, wfo = ([] for _ in range(11))
    for l in range(L):
        wada.append(_blocks(f(inp["w_ada"][l])))
        lv = np.zeros((128, LV_N), np.float32)
        lv[:, LV_BADA:LV_BADA + 96] = f(inp["b_ada"][l]).reshape(96, 128).T
        lv[:, LV_GAIN:LV_GAIN + 16] = f(inp["gla_norm_gain"][l]).reshape(16, 128).T
        lv[:, LV_PSC:LV_PSC + 8] = f(inp["pool_scale"][l]).reshape(8, 128).T
        lv[:, LV_LMG:LV_LMG + 16] = f(inp["ln_mix_gain"][l]).reshape(16, 128).T
        lv[:, LV_LMB:LV_LMB + 16] = f(inp["ln_mix_bias"][l]).reshape(16, 128).T
        lv[:, LV_LFG:LV_LFG + 16] = f(inp["ln_ffn_gain"][l]).reshape(16, 128).T
        lv[:, LV_LFB:LV_LFB + 16] = f(inp["ln_ffn_bias"][l]).reshape(16, 128).T
        lv[:, LV_BDEC:LV_BDEC + 16] = f(inp["b_decay_up"][l]).reshape(16, 128).T
        lvec.append(lv)
        wdu.append(f(inp["w_decay_up"][l]).transpose(1, 0, 2).reshape(16, 2048))
        W = f(inp["w_in"][l])
        alr = np.zeros((D, 128), np.float32)
        alr[:, :32] = W[:, 6144:6176]
        Wsel = np.concatenate([W[:, 0:1024], W[:, 1024:2048], W[:, 4096:6144], alr, W[:, 6176:7200], W[:, 7200:11296]], axis=1)
        win_fm.append(_blocks(Wsel))
        win_v.append(np.ascontiguousarray(W[:, 2048:4096].reshape(16, 128, 4, 512).transpose(2, 1, 0, 3)).reshape(4, 128, 16 * 512))
        G = f(inp["w_pool_group"][l])
        wpg.append(np.ascontiguousarray(G.reshape(4, 2, 128, 2, 128).transpose(0, 3, 2, 1, 4)).reshape(8, 128, 256))
        wgo.append(_blocks(f(inp["w_gla_out"][l])))
        wpo.append(_blocks(f(inp["w_pool_out"][l])))
        wo.append(_blocks(f(inp["w_out"][l])))
        Wf = f(inp["w_ffn_in"][l])
        Gt = Wf[:, :DFF].reshape(16, 128, 44, 128).transpose(2, 1, 0, 3)
        Up = Wf[:, DFF:].reshape(16, 128, 44, 128).transpose(2, 1, 0, 3)
        wfi.append(np.ascontiguousarray(np.concatenate([Gt, Up], axis=3)).reshape(44, 128, 16 * 256))
        wfo.append(_blocks(f(inp["w_ffn_out"][l])))
    cat = lambda xs: np.ascontiguousarray(np.concatenate(xs, axis=0))
    out.update(wada=cat(wada), lvec=np.stack(lvec), wdu=np.stack(wdu), win_fm=cat(win_fm), win_v=cat(win_v), wpg=cat(wpg),
               wgo=cat(wgo), wpo=cat(wpo), wo=cat(wo), wfi=cat(wfi), wfo=cat(wfo))
    return out


def run_device(inp, depth=DEPTH, dbg=()):
    import time
    t0 = time.time()
    nc = build_program(depth, dbg)
    t1 = time.time()
    shared = _consts()
    shared.update(_prep_weights(inp, depth))
    print(f"[kernel] build {t1 - t0:.1f}s prep {time.time() - t1:.1f}s", flush=True)
    x = np.asarray(inp["x"], np.float32)
    c = np.asarray(inp["c"], np.float32)
    ctx = np.asarray(inp["ctx"], np.float32)
    cc = np.asarray(inp["c_ctx"], np.float32)
    percore = []
    for b in range(2):
        xT0 = np.ascontiguousarray(np.concatenate([ctx[b].T, x[b].T], axis=1))
        cT = np.zeros((128, 16, 2), np.float32)
        cT[:, :, 0] = c[b].reshape(16, 128).T
        cT[:, :, 1] = cc.reshape(16, 128).T
        percore.append(dict(xT0=xT0, cT=cT.reshape(128, 32)))
    in_maps = []
    for core in range(NCORES):
        m = dict(shared)
        m.update(percore[core % 2])
        in_maps.append(m)
    t2 = time.time()
    res = run_bass_kernel_spmd(nc, in_maps, core_ids=list(range(NCORES)))
    print(f"[kernel] launch {time.time() - t2:.1f}s", flush=True)
    return res.results


def kernel(**inputs):
    r = run_device(inputs, DEPTH)
    out = np.stack([np.ascontiguousarray(r[b]["yT"].T) for b in range(2)], axis=0)
    return out.astype(np.float32)
```
